# Optimizing a Trainium2 kernel written in Bass

```python
import math
import jax, jax.numpy as jnp
from jax import lax
import numpy as np

D_MODEL = 2048
BATCH = 8
SEQ = 2048
DEPTH = 2

D_MIX = D_MODEL
D_SSM = D_MIX // 2
D_ATTN = D_MIX - D_SSM
SSM_GROUP = 16
N_SSM_GROUPS = D_SSM // SSM_GROUP
SSM_STATE = 64
ATTN_HEAD_DIM = 64
ATTN_V_DIM = 2 * ATTN_HEAD_DIM
N_ATTN_HEADS = D_ATTN // ATTN_V_DIM
QK_WIDTH = N_ATTN_HEADS * 2 * ATTN_HEAD_DIM
ROT_DIM = ATTN_HEAD_DIM // 4
ROPE_THETA = 500000.0
Q_BLOCK = 128
DT_MIN = 0.001
DT_MAX = 0.1
LN_EPS = 1e-5
RMS_EPS = 1e-5
DEEPNORM_ALPHA = (2.0 * DEPTH) ** 0.25
DEEPNORM_BETA = (8.0 * DEPTH) ** -0.25
PROJ_SIZES = (D_SSM, D_SSM, QK_WIDTH, QK_WIDTH, D_ATTN, D_ATTN)
D_IN_PROJ = sum(PROJ_SIZES)
PROJ_SPLITS = tuple(int(s) for s in np.cumsum(PROJ_SIZES)[:-1])

kernel_name = "hymba_s5_diffattn_deepnorm_encoder"


def layer_norm(x, g, b):
    xf = x.astype(jnp.float32)
    mu = jnp.mean(xf, axis=-1, keepdims=True)
    var = jnp.mean(jnp.square(xf - mu), axis=-1, keepdims=True)
    y = (xf - mu) * lax.rsqrt(var + LN_EPS) * g.astype(jnp.float32) + b.astype(jnp.float32)
    return y.astype(x.dtype)


def rms_norm(x, g):
    xf = x.astype(jnp.float32)
    y = xf * lax.rsqrt(jnp.mean(jnp.square(xf), axis=-1, keepdims=True) + RMS_EPS)
    return y * g.astype(jnp.float32)


def partial_rotary(t, cos, sin):
    half = ROT_DIM // 2
    r1 = t[..., :half]
    r2 = t[..., half:ROT_DIM]
    rest = t[..., ROT_DIM:]
    c = cos[None, :, None, None, :]
    s = sin[None, :, None, None, :]
    return jnp.concatenate([r1 * c - r2 * s, r2 * c + r1 * s, rest], axis=-1)


def _ssm_combine(e1, e2):
    a1r, a1i, b1r, b1i = e1
    a2r, a2i, b2r, b2i = e2
    return (a2r * a1r - a2i * a1i,
            a2r * a1i + a2i * a1r,
            a2r * b1r - a2i * b1i + b2r,
            a2r * b1i + a2i * b1r + b2i)


def s5_scan(u, lam_re, lam_im, log_step, b_re, b_im, c_re, c_im, reverse):
    step = jnp.exp(log_step)[:, None]
    zr = lam_re * step
    zi = lam_im * step
    mag = jnp.exp(zr)
    ab_re = mag * jnp.cos(zi)
    ab_im = mag * jnp.sin(zi)
    nr = ab_re - 1.0
    ni = ab_im
    den = lam_re * lam_re + lam_im * lam_im
    coef_re = (nr * lam_re + ni * lam_im) / den
    coef_im = (ni * lam_re - nr * lam_im) / den
    bb_re = coef_re[..., None] * b_re - coef_im[..., None] * b_im
    bb_im = coef_re[..., None] * b_im + coef_im[..., None] * b_re
    bu_re = jnp.einsum('bsgc,gpc->bsgp', u, bb_re)
    bu_im = jnp.einsum('bsgc,gpc->bsgp', u, bb_im)
    a_re = jnp.broadcast_to(ab_re, bu_re.shape)
    a_im = jnp.broadcast_to(ab_im, bu_im.shape)
    _, _, x_re, x_im = lax.associative_scan(
        _ssm_combine, (a_re, a_im, bu_re, bu_im), reverse=reverse, axis=1)
    return (jnp.einsum('bsgp,gcp->bsgc', x_re, c_re)
            - jnp.einsum('bsgp,gcp->bsgc', x_im, c_im))


def diff_attention(q, k, v, lam):
    b, s, h, _, d = q.shape
    nb = s // Q_BLOCK
    scale = d ** -0.5
    qb = q.reshape(b, nb, Q_BLOCK, h, 2, d).transpose(1, 0, 3, 4, 2, 5)
    kt = k.transpose(0, 2, 3, 1, 4)
    vt = v.transpose(0, 2, 1, 3)

    def block(q_blk):
        sc = jnp.einsum('bhmqd,bhmkd->bhmqk', q_blk, kt).astype(jnp.float32) * scale
        p = jax.nn.softmax(sc, axis=-1)
        w = p[:, :, 0] - lam * p[:, :, 1]
        return jnp.einsum('bhqk,bhkv->bhqv', w.astype(vt.dtype), vt)

    o = lax.map(block, qb)
    return o.transpose(1, 0, 3, 2, 4).reshape(b, s, h, 2 * d)


def setup_inputs(seed: int = 0) -> dict:
    key = jax.random.key(seed)
    ks = jax.random.split(key, 24)
    f32 = jnp.float32
    G, P, Cg = N_SSM_GROUPS, SSM_STATE, SSM_GROUP
    nrm = lambda k, shp: jax.random.normal(k, shp, f32)
    x = nrm(ks[0], (BATCH, SEQ, D_MODEL))
    ln_emb_g = 1.0 + 0.02 * nrm(ks[1], (D_MODEL,))
    ln_emb_b = 0.02 * nrm(ks[2], (D_MODEL,))
    w_in = nrm(ks[3], (DEPTH, D_MODEL, D_IN_PROJ)) * D_MODEL ** -0.5
    ssm_lam_re = -0.5 + 0.01 * nrm(ks[4], (DEPTH, 2, G, P))
    ssm_lam_im = (math.pi * jnp.arange(P, dtype=f32))[None, None, None, :] \
        + 0.0 * nrm(ks[5], (DEPTH, 2, G, P)) * 0.0 + jnp.zeros((DEPTH, 2, G, P), f32)
    ssm_log_step = jax.random.uniform(ks[6], (DEPTH, 2, G), f32,
                                      minval=math.log(DT_MIN), maxval=math.log(DT_MAX))
    ssm_b_re = nrm(ks[7], (DEPTH, 2, G, P, Cg)) * (2.0 * Cg) ** -0.5
    ssm_b_im = nrm(ks[8], (DEPTH, 2, G, P, Cg)) * (2.0 * Cg) ** -0.5
    ssm_c_re = nrm(ks[9], (DEPTH, 2, G, Cg, P)) * P ** -0.5
    ssm_c_im = nrm(ks[10], (DEPTH, 2, G, Cg, P)) * P ** -0.5
    ssm_d = nrm(ks[11], (DEPTH, D_SSM))
    w_glu = nrm(ks[12], (DEPTH, D_SSM, D_SSM)) * D_SSM ** -0.5
    b_glu = 0.02 * nrm(ks[13], (DEPTH, D_SSM))
    lambda_q1 = 0.1 * nrm(ks[14], (DEPTH, ATTN_HEAD_DIM))
    lambda_k1 = 0.1 * nrm(ks[15], (DEPTH, ATTN_HEAD_DIM))
    lambda_q2 = 0.1 * nrm(ks[16], (DEPTH, ATTN_HEAD_DIM))
    lambda_k2 = 0.1 * nrm(ks[17], (DEPTH, ATTN_HEAD_DIM))
    attn_norm_g = 1.0 + 0.02 * nrm(ks[18], (DEPTH, ATTN_V_DIM))
    w_out = nrm(ks[19], (DEPTH, D_MIX, D_MODEL)) * (D_MIX ** -0.5) * DEEPNORM_BETA
    ln_g = 1.0 + 0.02 * nrm(ks[20], (DEPTH, D_MODEL))
    ln_b = 0.02 * nrm(ks[21], (DEPTH, D_MODEL))
    return {"x": x, "ln_emb_g": ln_emb_g, "ln_emb_b": ln_emb_b, "w_in": w_in,
            "ssm_lam_re": ssm_lam_re, "ssm_lam_im": ssm_lam_im, "ssm_log_step": ssm_log_step,
            "ssm_b_re": ssm_b_re, "ssm_b_im": ssm_b_im, "ssm_c_re": ssm_c_re, "ssm_c_im": ssm_c_im,
            "ssm_d": ssm_d, "w_glu": w_glu, "b_glu": b_glu,
            "lambda_q1": lambda_q1, "lambda_k1": lambda_k1, "lambda_q2": lambda_q2,
            "lambda_k2": lambda_k2, "attn_norm_g": attn_norm_g, "w_out": w_out,
            "ln_g": ln_g, "ln_b": ln_b}


def reference(x, ln_emb_g, ln_emb_b, w_in, ssm_lam_re, ssm_lam_im, ssm_log_step,
              ssm_b_re, ssm_b_im, ssm_c_re, ssm_c_im, ssm_d, w_glu, b_glu,
              lambda_q1, lambda_k1, lambda_q2, lambda_k2, attn_norm_g, w_out,
              ln_g, ln_b):
    f32 = jnp.float32
    b, s, _ = x.shape
    dt = x.dtype
    pos = jnp.arange(s, dtype=f32)
    inv_freq = ROPE_THETA ** (-jnp.arange(0, ROT_DIM, 2, dtype=f32) / ROT_DIM)
    ang = pos[:, None] * inv_freq[None, :]
    cos = jnp.cos(ang).astype(dt)
    sin = jnp.sin(ang).astype(dt)

    x = layer_norm(x, ln_emb_g, ln_emb_b)
    for l in range(DEPTH):
        proj = jnp.einsum('bsd,de->bse', x, w_in[l])
        u, g_ssm, q, k, v, g_attn = jnp.split(proj, PROJ_SPLITS, axis=-1)

        uf = u.astype(f32).reshape(b, s, N_SSM_GROUPS, SSM_GROUP)
        y = uf * ssm_d[l].astype(f32).reshape(N_SSM_GROUPS, SSM_GROUP)
        for direction in range(2):
            y = y + s5_scan(uf,
                            ssm_lam_re[l, direction].astype(f32), ssm_lam_im[l, direction].astype(f32),
                            ssm_log_step[l, direction].astype(f32),
                            ssm_b_re[l, direction].astype(f32), ssm_b_im[l, direction].astype(f32),
                            ssm_c_re[l, direction].astype(f32), ssm_c_im[l, direction].astype(f32),
                            reverse=(direction == 1))
        y = jax.nn.gelu(y.reshape(b, s, D_SSM)).astype(dt)
        y = y * jax.nn.sigmoid(jnp.einsum('bsc,ce->bse', y, w_glu[l]) + b_glu[l])
        y_ssm = y * jax.nn.silu(g_ssm)

        qh = partial_rotary(q.reshape(b, s, N_ATTN_HEADS, 2, ATTN_HEAD_DIM), cos, sin)
        kh = partial_rotary(k.reshape(b, s, N_ATTN_HEADS, 2, ATTN_HEAD_DIM), cos, sin)
        vh = v.reshape(b, s, N_ATTN_HEADS, ATTN_V_DIM)
        lambda_init = 0.8 - 0.6 * math.exp(-0.3 * l)
        lam = (jnp.exp(jnp.sum(lambda_q1[l].astype(f32) * lambda_k1[l].astype(f32)))
               - jnp.exp(jnp.sum(lambda_q2[l].astype(f32) * lambda_k2[l].astype(f32)))
               + lambda_init)
        o = diff_attention(qh, kh, vh, lam)
        o = rms_norm(o, attn_norm_g[l]) * (1.0 - lambda_init)
        y_attn = o.reshape(b, s, D_ATTN).astype(dt) * jax.nn.silu(g_attn)

        mix = jnp.concatenate([y_ssm, y_attn], axis=-1)
        out = jnp.einsum('bse,ed->bsd', mix, w_out[l])
        x = layer_norm(DEEPNORM_ALPHA * x + out, ln_g[l], ln_b[l])
    return x
```

```python
import math
from contextlib import ExitStack
import numpy as np
import ml_dtypes
import concourse.bass as bass
import concourse.mybir as mybir
from concourse.bass_utils import run_bass_kernel_spmd

F32 = mybir.dt.float32
BF16 = mybir.dt.bfloat16
AF = mybir.ActivationFunctionType
ALU = mybir.AluOpType
AX = mybir.AxisListType

L = 2
T = 2048
D = 2048
NT = 16
G = 64
DIN = 6144
LN_EPS = 1e-5
RMS_EPS = 1e-5
ALPHA = (2.0 * L) ** 0.25
TWO_PI = 2.0 * math.pi
GB = 16
NB = G // GB

COVERLAP = True
CHAIN_ENG = "vector"
SAME_ENGINE_SYNC = True


class Prog:
    ENGS = ["tensor", "vector", "scalar", "gpsimd", "sync"]
    NDMA = 10

    def __init__(self, nc):
        self.nc = nc
        self.q = {e: [] for e in self.ENGS}
        self.sem = {e: nc.alloc_semaphore(name=f"sem_{e}") for e in self.ENGS}
        self.cnt = {e: 0 for e in self.ENGS}
        self.dq = ["sync", "gpsimd", "scalar"]
        self.dsem = {e: [nc.alloc_semaphore(name=f"dsem_{e}_{i}") for i in range(self.NDMA)] for e in self.dq}
        self.dval = {e: [0] * self.NDMA for e in self.dq}
        self.dnext = {e: 0 for e in self.dq}
        self.semobj = {}
        for e in self.ENGS:
            self.semobj[("c", e)] = self.sem[e]
        for e in self.dq:
            for i in range(self.NDMA):
                self.semobj[("d", e, i)] = self.dsem[e][i]
        self.known = {e: {} for e in self.ENGS}
        self.res = {}
        self.ninstr = 0

    def _wait(self, eng, key, val):
        if self.known[eng].get(key, 0) >= val:
            return
        self.known[eng][key] = val
        so = self.semobj[key]
        self.q[eng].append(lambda E, so=so, val=val: E.wait_ge(so, val))

    def _deps(self, eng, reads, writes):
        deps = {}

        def add(tok):
            if tok is None:
                return
            k, v = tok
            if deps.get(k, 0) < v:
                deps[k] = v

        for r in reads:
            st = self.res.get(r)
            if st:
                add(st["w"])
        for w in writes:
            st = self.res.get(w)
            if st:
                add(st["w"])
                for t in st["r"]:
                    add(t)
        for k, v in deps.items():
            if k == ("c", eng) and (eng == "tensor" or not SAME_ENGINE_SYNC):
                continue
            self._wait(eng, k, v)

    def _record(self, tok, reads, writes):
        for r in reads:
            st = self.res.setdefault(r, {"w": None, "r": []})
            st["r"] = [t for t in st["r"] if t[0] != tok[0]] + [tok]
        for w in writes:
            self.res[w] = {"w": tok, "r": []}

    def op(self, eng, fn, reads=(), writes=()):
        self._deps(eng, reads, writes)
        self.cnt[eng] += 1
        c = self.cnt[eng]
        so = self.sem[eng]
        self.q[eng].append(lambda E, fn=fn, so=so: fn(E).then_inc(so, 1))
        self.ninstr += 1
        tok = (("c", eng), c)
        self._record(tok, reads, writes)
        return tok

    def dma(self, eng, out, in_, reads=(), writes=()):
        self._deps(eng, reads, writes)
        i = self.dnext[eng]
        self.dnext[eng] = (i + 1) % self.NDMA
        key = ("d", eng, i)
        if self.dval[eng][i] > 0:
            self._wait(eng, key, self.dval[eng][i])
        self.dval[eng][i] += 16
        v = self.dval[eng][i]
        so = self.dsem[eng][i]
        self.q[eng].append(lambda E, so=so, out=out, in_=in_: E.dma_start(out=out, in_=in_).then_inc(so, 16))
        self.ninstr += 1
        tok = (key, v)
        self._record(tok, reads, writes)
        return tok

    def wait_all(self, eng):
        for e in self.ENGS:
            if self.cnt[e] > 0 and e != eng:
                self._wait(eng, ("c", e), self.cnt[e])
        for e in self.dq:
            for i in range(self.NDMA):
                if self.dval[e][i] > 0:
                    self._wait(eng, ("d", e, i), self.dval[e][i])

    def barrier(self):
        for e in self.ENGS:
            self.wait_all(e)

    def emit(self):
        nc = self.nc
        with nc.Block() as block:
            @block.tensor
            def _(E):
                for f in self.q["tensor"]:
                    f(E)

            @block.vector
            def _(E):
                for f in self.q["vector"]:
                    f(E)

            @block.scalar
            def _(E):
                for f in self.q["scalar"]:
                    f(E)

            @block.gpsimd
            def _(E):
                for f in self.q["gpsimd"]:
                    f(E)

            @block.sync
            def _(E):
                for f in self.q["sync"]:
                    f(E)


class _DryP:
    def op(self, *a, **k):
        pass

    def dma(self, *a, **k):
        pass


class KB:
    def __init__(self, nc):
        self.nc = nc
        self._P = Prog(nc)
        self._dry = _DryP()
        self.dry = False

    @property
    def P(self):
        return self._dry if self.dry else self._P

    def mm(self, out, lhsT, rhs, start, stop, r, w):
        self.P.op("tensor", lambda E: E.matmul(out, lhsT=lhsT, rhs=rhs, start=start, stop=stop), r, w)

    def tr(self, out, in_, ident, r, w):
        self.P.op("tensor", lambda E: E.transpose(out, in_, ident), r, w)

    def act(self, out, in_, func, r, w, bias=None, scale=None, accum_out=None):
        kw = {}
        if bias is not None:
            kw["bias"] = bias
        if scale is not None:
            kw["scale"] = scale
        if accum_out is not None:
            kw["accum_out"] = accum_out
        self.P.op("scalar", lambda E: E.activation(out=out, in_=in_, func=func, **kw), r, w)

    def tt(self, eng, out, in0, in1, op, r, w):
        self.P.op(eng, lambda E: E.tensor_tensor(out=out, in0=in0, in1=in1, op=op), r, w)

    def ts(self, eng, out, in0, s1, s2, op0, op1, r, w):
        if op1 is None:
            self.P.op(eng, lambda E: E.tensor_scalar(out=out, in0=in0, scalar1=s1, scalar2=None, op0=op0), r, w)
        else:
            self.P.op(eng, lambda E: E.tensor_scalar(out=out, in0=in0, scalar1=s1, scalar2=s2, op0=op0, op1=op1), r, w)

    def stt(self, out, in0, scalar, in1, op0, op1, r, w):
        self.P.op("vector", lambda E: E.scalar_tensor_tensor(out=out, in0=in0, scalar=scalar, in1=in1, op0=op0, op1=op1), r, w)

    def cp(self, eng, out, in_, r, w):
        if eng == "scalar":
            self.P.op("scalar", lambda E: E.copy(out=out, in_=in_), r, w)
        else:
            self.P.op(eng, lambda E: E.tensor_copy(out=out, in_=in_), r, w)

    def memset(self, eng, ap, val, w):
        self.P.op(eng, lambda E: E.memset(ap, val), (), w)

    def recip(self, out, in_, r, w):
        self.P.op("vector", lambda E: E.reciprocal(out=out, in_=in_), r, w)

    def dma(self, eng, out, in_, r, w):
        self.P.dma(eng, out, in_, r, w)


def build(n_layers=L, stop_after=None, dbg=False):
    nc = bass.Bass("TRN2", target_bir_lowering=False)
    K = KB(nc)
    P = K._P

    def din(name, shape, dt=F32):
        return nc.dram_tensor(name, list(shape), dt, kind="ExternalInput").ap()

    def dscr(name, shape, dt):
        return nc.dram_tensor(name, list(shape), dt, kind="Internal").ap()

    x_d = din("x", [T, D])
    w_in_d = din("w_in", [L, D, DIN])
    w_glu_d = din("w_glu", [L, 1024, 1024])
    w_out_d = din("w_out", [L, D, D])
    lnp_d = din("lnp", [128, 2 + 2 * L, D])
    bglu_d = din("bglu", [128, L, 1024])
    angg_d = din("angg", [128, L, 128])
    lamv_d = din("lamv", [128, L, 4, 64])
    ssms_d = din("ssm_s", [128, L, 3, G])
    ssmb_d = din("ssm_b", [128, L, 2, G, 16])
    ssmc_d = din("ssm_c", [128, L, 2, G, 16])
    dcol_d = din("ssm_dcol", [128, L, G])
    rope_d = din("rope", [T, 16])
    cst_d = din("cst", [128, 4, 128])
    out_d = nc.dram_tensor("out", [T, D], F32, kind="ExternalOutput").ap()

    xres_d = dscr("xres", [T, D], F32)
    gs_d = dscr("gs", [T, 1024], F32)
    ga_d = dscr("ga", [T, 1024], F32)
    qT_d = dscr("qT", [8, 128, T], BF16)
    kT_d = dscr("kT", [8, 128, T], BF16)
    v_d = dscr("v", [T, 1024], BF16)
    ytok_d = dscr("ytok", [T, 1024], F32)
    mixT_d = dscr("mixT", [D, T], BF16)
    Ud_d = dscr("Ud", [128, G, 256], BF16)
    sTT_d = dscr("sTT", [L, 128, G, 128], BF16)
    sWA_d = dscr("sWA", [L, 128, G, 2, 128], BF16)
    sCT_d = dscr("sCT", [L, 128, G, 2, 128], BF16)
    dbg_d = {}

    def ddbg(name, shape, dt=F32):
        dbg_d[name] = nc.dram_tensor("dbg_" + name, list(shape), dt, kind="ExternalOutput").ap()
        return dbg_d[name]

    uid = [0]

    def nm(s):
        uid[0] += 1
        return f"{s}_{uid[0]}"

    with ExitStack() as gstack:
        def sb(stack, name, shape, dt):
            return stack.enter_context(nc.sbuf_tensor(nm(name), list(shape), dt))

        identb = sb(gstack, "identb", [128, 128], BF16)
        identf = sb(gstack, "identf", [128, 128], F32)
        maskF = sb(gstack, "maskF", [128, 128], F32)
        maskB = sb(gstack, "maskB", [128, 128], F32)
        hsel = sb(gstack, "hsel", [128, 2], F32)
        A2 = sb(gstack, "A2", [128, L, G, 2], F32)
        A3 = sb(gstack, "A3", [128, L, G, 2], F32)
        neglam = sb(gstack, "neglam", [128, L], F32)
        psA = gstack.enter_context(nc.psum_tensor(nm("psA"), [128, 2048], F32))
        ps = [psA[:, i * 512:(i + 1) * 512] for i in range(4)]
        ps += [gstack.enter_context(nc.psum_tensor(nm("ps"), [128, 512], F32))[:, :] for _ in range(2)]
        psB = gstack.enter_context(nc.psum_tensor(nm("psB"), [128, 1024], F32))
        ps += [psB[:, 0:512], psB[:, 512:1024]]

        K.dma("gpsimd", identb[:], cst_d[:, 0, :], [], ["identb"])
        K.dma("sync", identf[:], cst_d[:, 0, :], [], ["identf"])
        K.dma("sync", maskF[:], cst_d[:, 1, :], [], ["maskF"])
        K.dma("sync", maskB[:], cst_d[:, 2, :], [], ["maskB"])
        K.dma("sync", hsel[:], cst_d[:, 3, 0:2], [], ["hsel"])

        def ssm_pre(l):
            with ExitStack() as st:
                S = sb(st, "S", [128, 3, G], F32)
                Bq = sb(st, "Bq", [128, 2, G, 16], F32)
                Cq = sb(st, "Cq", [128, 2, G, 16], F32)
                Dc = sb(st, "Dc", [128, G], F32)
                K.dma("sync", S[:], ssms_d[:, l], [], ["S"])
                K.dma("sync", Bq[:], ssmb_d[:, l], [], ["Bq"])
                K.dma("sync", Cq[:], ssmc_d[:, l], [], ["Cq"])
                K.dma("sync", Dc[:], dcol_d[:, l], [], ["Dc"])
                sm = {}

                def small(name):
                    sm[name] = sb(st, name, [128, G], F32)
                    return sm[name]

                for n_ in ["step", "zr", "zi", "mag", "ws", "wc", "nn", "sinv", "cosv", "ar", "ai", "nr", "den",
                           "t1", "t2", "t3", "t4", "cre", "cim", "m2", "ir", "ii", "abr", "abi", "acr", "aci"]:
                    small(n_)
                V = "vector"

                def tt(o, a, b, op):
                    K.tt(V, sm[o][:], sm[a][:], sm[b][:], op, [a, b], [o])

                lamr, lami, lstep = S[:, 0, :], S[:, 1, :], S[:, 2, :]
                K.act(sm["step"][:], lstep, AF.Exp, ["S"], ["step"])
                K.tt(V, sm["zr"][:], lamr, sm["step"][:], ALU.mult, ["S", "step"], ["zr"])
                K.tt(V, sm["zi"][:], lami, sm["step"][:], ALU.mult, ["S", "step"], ["zi"])
                K.act(sm["mag"][:], sm["zr"][:], AF.Exp, ["zr"], ["mag"])
                K.ts(V, sm["ws"][:], sm["zi"][:], 1.0 / TWO_PI, None, ALU.mult, None, ["zi"], ["ws"])
                K.ts(V, sm["wc"][:], sm["zi"][:], 1.0 / TWO_PI, 0.25, ALU.mult, ALU.add, ["zi"], ["wc"])
                for wname, oname in (("ws", "sinv"), ("wc", "cosv")):
                    K.memset(V, sm["nn"][:], 0.0, ["nn"])
                    for j in range(1, 7):
                        K.stt(sm["nn"][:], sm[wname][:], j - 0.5, sm["nn"][:], ALU.is_gt, ALU.add, [wname, "nn"], ["nn"])
                    tt("t1", wname, "nn", ALU.subtract)
                    K.act(sm[oname][:], sm["t1"][:], AF.Sin, ["t1"], [oname], scale=TWO_PI)
                tt("ar", "mag", "cosv", ALU.mult)
                tt("ai", "mag", "sinv", ALU.mult)
                K.ts(V, sm["nr"][:], sm["ar"][:], -1.0, None, ALU.add, None, ["ar"], ["nr"])
                K.tt(V, sm["t1"][:], lamr, lamr, ALU.mult, ["S"], ["t1"])
                K.tt(V, sm["t2"][:], lami, lami, ALU.mult, ["S"], ["t2"])
                tt("den", "t1", "t2", ALU.add)
                K.recip(sm["den"][:], sm["den"][:], ["den"], ["den"])
                K.tt(V, sm["t1"][:], sm["nr"][:], lamr, ALU.mult, ["nr", "S"], ["t1"])
                K.tt(V, sm["t2"][:], sm["ai"][:], lami, ALU.mult, ["ai", "S"], ["t2"])
                tt("t3", "t1", "t2", ALU.add)
                tt("cre", "t3", "den", ALU.mult)
                K.tt(V, sm["t1"][:], sm["ai"][:], lamr, ALU.mult, ["ai", "S"], ["t1"])
                K.tt(V, sm["t2"][:], sm["nr"][:], lami, ALU.mult, ["nr", "S"], ["t2"])
                tt("t3", "t1", "t2", ALU.subtract)
                tt("cim", "t3", "den", ALU.mult)
                BB = sb(st, "BB", [128, 2, G, 16], F32)
                tmpA = sb(st, "tmpA", [128, G, 16], F32)
                tmpB = sb(st, "tmpB", [128, G, 16], F32)
                creb = sm["cre"][:].unsqueeze(2).broadcast_to([128, G, 16])
                cimb = sm["cim"][:].unsqueeze(2).broadcast_to([128, G, 16])
                K.tt(V, tmpA[:], Bq[:, 0], creb, ALU.mult, ["Bq", "cre"], ["tmpA"])
                K.tt(V, tmpB[:], Bq[:, 1], cimb, ALU.mult, ["Bq", "cim"], ["tmpB"])
                K.tt(V, BB[:, 0], tmpA[:], tmpB[:], ALU.subtract, ["tmpA", "tmpB"], ["BB0"])
                K.tt(V, tmpA[:], Bq[:, 1], creb, ALU.mult, ["Bq", "cre"], ["tmpA"])
                K.tt(V, tmpB[:], Bq[:, 0], cimb, ALU.mult, ["Bq", "cim"], ["tmpB"])
                K.tt(V, BB[:, 1], tmpA[:], tmpB[:], ALU.add, ["tmpA", "tmpB"], ["BB1"])
                tt("t1", "ar", "ar", ALU.mult)
                tt("t2", "ai", "ai", ALU.mult)
                tt("m2", "t1", "t2", ALU.add)
                K.recip(sm["m2"][:], sm["m2"][:], ["m2"], ["m2"])
                tt("ir", "ar", "m2", ALU.mult)
                tt("t1", "ai", "m2", ALU.mult)
                K.ts(V, sm["ii"][:], sm["t1"][:], -1.0, None, ALU.mult, None, ["t1"], ["ii"])
                for (o, f_, b_) in (("abr", "ir", "ar"), ("abi", "ii", "ai"), ("acr", "ar", "ir"), ("aci", "ai", "ii")):
                    K.cp(V, sm[o][0:64, :], sm[f_][0:64, :], [f_], [o + "f"])
                    K.cp(V, sm[o][64:128, :], sm[b_][64:128, :], [b_], [o + "b"])
                PWB = sb(st, "PWB", [128, 2, 8, G], F32)
                PWC = sb(st, "PWC", [128, 2, 8, G], F32)
                for (PW, br_, bi_, nmk) in ((PWB, "abr", "abi", "PWB"), (PWC, "acr", "aci", "PWC")):
                    K.memset(V, PW[:, 0, 0, :], 1.0, [nmk + "r0"])
                    K.memset(V, PW[:, 1, 0, :], 0.0, [nmk + "i0"])
                    for k in range(1, 8):
                        pr, pi = PW[:, 0, k - 1, :], PW[:, 1, k - 1, :]
                        K.tt(V, sm["t1"][:], pr, sm[br_][:], ALU.mult, [nmk + f"r{k-1}", br_ + "f", br_ + "b"], ["t1"])
                        K.tt(V, sm["t2"][:], pi, sm[bi_][:], ALU.mult, [nmk + f"i{k-1}", bi_ + "f", bi_ + "b"], ["t2"])
                        K.tt(V, PW[:, 0, k, :], sm["t1"][:], sm["t2"][:], ALU.subtract, ["t1", "t2"], [nmk + f"r{k}"])
                        K.tt(V, sm["t3"][:], pr, sm[bi_][:], ALU.mult, [nmk + f"r{k-1}", bi_ + "f", bi_ + "b"], ["t3"])
                        K.tt(V, sm["t4"][:], pi, sm[br_][:], ALU.mult, [nmk + f"i{k-1}", br_ + "f", br_ + "b"], ["t4"])
                        K.tt(V, PW[:, 1, k, :], sm["t3"][:], sm["t4"][:], ALU.add, ["t3", "t4"], [nmk + f"i{k}"])
                sq_r, sq_i = "ar", "ai"
                for it in range(3):
                    tt("t1", sq_r, sq_r, ALU.mult)
                    tt("t2", sq_i, sq_i, ALU.mult)
                    tt("t3", sq_r, sq_i, ALU.mult)
                    nr_, ni_ = ("zr", "zi") if it % 2 == 0 else ("ws", "wc")
                    tt(nr_, "t1", "t2", ALU.subtract)
                    K.ts(V, sm[ni_][:], sm["t3"][:], 2.0, None, ALU.mult, None, ["t3"], [ni_])
                    sq_r, sq_i = nr_, ni_
                K.cp(V, A2[:, l, :, 0], sm[sq_r][:], [sq_r], [f"A2a{l}"])
                K.cp(V, A2[:, l, :, 1], sm[sq_r][:], [sq_r], [f"A2b{l}"])
                K.ts(V, A3[:, l, :, 0], sm[sq_i][:], -1.0, None, ALU.mult, None, [sq_i], [f"A3a{l}"])
                K.cp(V, A3[:, l, :, 1], sm[sq_i][:], [sq_i], [f"A3b{l}"])

                Bt = sb(st, "Bt", [128, 2, GB, 8, 16], F32)
                Ct = sb(st, "Ct", [128, 2, GB, 8, 16], F32)
                CtF = sb(st, "CtF", [128, 2, GB, 8, 16], F32)
                CtB = sb(st, "CtB", [128, 2, GB, 8, 16], F32)
                tA = sb(st, "tA", [128, GB, 8, 16], F32)
                tB = sb(st, "tB", [128, GB, 8, 16], F32)
                tC = sb(st, "tC", [128, GB, 8, 16], F32)
                tD = sb(st, "tD", [128, GB, 8, 16], F32)
                TTst = sb(st, "TTst", [128, GB, 128], BF16)
                WAst = sb(st, "WAst", [128, GB, 2, 128], BF16)
                CTst = sb(st, "CTst", [128, GB, 2, 128], BF16)
                tm1 = sb(st, "tm1", [128, 128], F32)
                tm2 = sb(st, "tm2", [128, 128], F32)
                pwb_res = [f"PWBr{k}" for k in range(8)] + [f"PWBi{k}" for k in range(8)]
                pwc_res = [f"PWCr{k}" for k in range(8)] + [f"PWCi{k}" for k in range(8)]
                for b in range(NB):
                    g0 = b * GB

                    def pwv(PW, ri):
                        return PW[:, ri, :, g0:g0 + GB].rearrange("p k g -> p g k").unsqueeze(3).broadcast_to([128, GB, 8, 16])

                    def xv(X, ri):
                        return X[:, ri, g0:g0 + GB, :].unsqueeze(2).broadcast_to([128, GB, 8, 16])

                    K.tt(V, tA[:], pwv(PWB, 0), xv(BB, 0), ALU.mult, pwb_res + ["BB0"], ["tA"])
                    K.tt(V, tB[:], pwv(PWB, 1), xv(BB, 1), ALU.mult, pwb_res + ["BB1"], ["tB"])
                    K.tt(V, Bt[:, 0], tA[:], tB[:], ALU.subtract, ["tA", "tB"], ["Bt0"])
                    K.tt(V, tA[:], pwv(PWB, 0), xv(BB, 1), ALU.mult, pwb_res + ["BB1"], ["tA"])
                    K.tt(V, tB[:], pwv(PWB, 1), xv(BB, 0), ALU.mult, pwb_res + ["BB0"], ["tB"])
                    K.tt(V, Bt[:, 1], tA[:], tB[:], ALU.add, ["tA", "tB"], ["Bt1"])
                    GP = "gpsimd"
                    K.tt(GP, tC[:], pwv(PWC, 0), xv(Cq, 0), ALU.mult, pwc_res + ["Cq"], ["tC"])
                    K.tt(GP, tD[:], pwv(PWC, 1), xv(Cq, 1), ALU.mult, pwc_res + ["Cq"], ["tD"])
                    K.tt(GP, Ct[:, 0], tC[:], tD[:], ALU.subtract, ["tC", "tD"], ["Ct0"])
                    K.tt(GP, tC[:], pwv(PWC, 0), xv(Cq, 1), ALU.mult, pwc_res + ["Cq"], ["tC"])
                    K.tt(GP, tD[:], pwv(PWC, 1), xv(Cq, 0), ALU.mult, pwc_res + ["Cq"], ["tD"])
                    K.tt(GP, tC[:], tC[:], tD[:], ALU.add, ["tC", "tD"], ["tC"])
                    K.act(Ct[:, 1].rearrange("p g k c -> p (g k c)"), tC[:].rearrange("p g k c -> p (g k c)"), AF.Copy, ["tC"], ["Ct1"], scale=-1.0)
                    for ri in range(2):
                        K.act(CtF[:, ri].rearrange("p g k c -> p (g k c)"), Ct[:, ri].rearrange("p g k c -> p (g k c)"), AF.Copy,
                              [f"Ct{ri}", "hsel"], [f"CtF{ri}"], scale=hsel[:, 0:1])
                        K.act(CtB[:, ri].rearrange("p g k c -> p (g k c)"), Ct[:, ri].rearrange("p g k c -> p (g k c)"), AF.Copy,
                              [f"Ct{ri}", "hsel"], [f"CtB{ri}"], scale=hsel[:, 1:2])
                        K.cp("scalar", CTst[:, :, ri, :], Ct[:, ri].rearrange("p g k c -> p g (k c)"), [f"Ct{ri}"], ["CTst"])
                    for gl in range(GB):
                        g = g0 + gl
                        pf, pb, pw = ps[(2 * gl) % 4], ps[(2 * gl + 1) % 4], ps[4 + gl % 2]
                        pfn, pbn, pwn = f"ps{(2*gl)%4}", f"ps{(2*gl+1)%4}", f"ps{4+gl%2}"

                        def fl(X, ri):
                            return X[:, ri, gl].rearrange("p k c -> p (k c)")

                        K.mm(pf[:, 0:128], fl(Bt, 0), fl(CtF, 0), True, False, ["Bt0", "CtF0"], [pfn])
                        K.mm(pf[:, 0:128], fl(Bt, 1), fl(CtF, 1), False, True, ["Bt1", "CtF1"], [pfn])
                        K.mm(pb[:, 0:128], fl(Bt, 0), fl(CtB, 0), True, False, ["Bt0", "CtB0"], [pbn])
                        K.mm(pb[:, 0:128], fl(Bt, 1), fl(CtB, 1), False, True, ["Bt1", "CtB1"], [pbn])
                        K.tt(V, tm1[:], pf[:, 0:128], maskF[:], ALU.mult, [pfn, "maskF"], ["tm1"])
                        K.tt(V, tm2[:], pb[:, 0:128], maskB[:], ALU.mult, [pbn, "maskB"], ["tm2"])
                        K.tt(V, tm1[:], tm1[:], tm2[:], ALU.add, ["tm1", "tm2"], ["tm1"])
                        K.stt(TTst[:, gl, :], identf[:], Dc[:, g:g + 1], tm1[:], ALU.mult, ALU.add, ["identf", "Dc", "tm1"], ["TTst"])
                        for ri in range(2):
                            K.tr(pw[:, ri * 128:(ri + 1) * 128], fl(Bt, ri), identf[:], [f"Bt{ri}", "identf"], [pwn])
                        K.cp("scalar", WAst[:, gl, :, :], pw[:, 0:256].rearrange("p (r c) -> p r c", r=2), [pwn], ["WAst"])
                    K.dma("sync", sTT_d[l, :, g0:g0 + GB, :], TTst[:], ["TTst"], [f"sTT{l}"])
                    K.dma("sync", sWA_d[l, :, g0:g0 + GB], WAst[:], ["WAst"], [f"sWA{l}"])
                    K.dma("sync", sCT_d[l, :, g0:g0 + GB], CTst[:], ["CTst"], [f"sCT{l}"])
                if dbg:
                    K.dma("sync", ddbg(f"A2_{l}", [128, G, 2]), A2[:, l], [f"A2a{l}", f"A2b{l}"], ["dbgA2"])
                    K.dma("sync", ddbg(f"A3_{l}", [128, G, 2]), A3[:, l], [f"A3a{l}", f"A3b{l}"], ["dbgA3"])
            P.barrier()

        def lam_pre():
            with ExitStack() as st:
                lv = sb(st, "lv", [128, L, 4, 64], F32)
                pr = sb(st, "pr", [128, L, 2, 64], F32)
                sm_ = sb(st, "lsum", [128, L, 2], F32)
                K.dma("sync", lv[:], lamv_d, [], ["lv"])
                for l in range(L):
                    for j in range(2):
                        K.tt("vector", pr[:, l, j, :], lv[:, l, 2 * j, :], lv[:, l, 2 * j + 1, :], ALU.mult, ["lv"], ["pr"])
                        K.P.op("vector", lambda E, o=sm_[:, l, j:j + 1], i=pr[:, l, j, :]: E.tensor_reduce(out=o, in_=i, axis=AX.X, op=ALU.add), ["pr"], ["lsum"])
                    K.act(sm_[:, l, :], sm_[:, l, :], AF.Exp, ["lsum"], ["lsum"])
                    lam_init = 0.8 - 0.6 * math.exp(-0.3 * l)
                    K.tt("vector", neglam[:, l:l + 1], sm_[:, l, 1:2], sm_[:, l, 0:1], ALU.subtract, ["lsum"], ["neglam"])
                    K.ts("vector", neglam[:, l:l + 1], neglam[:, l:l + 1], -lam_init, None, ALU.add, None, ["neglam"], ["neglam"])
            P.barrier()

        def layer_norm_tile(xt, xn, gt, bt, stats, mv, rstd, tagx):
            for c in range(4):
                K.P.op("vector", lambda E, o=stats[:, c, :], i=xt[:, c * 512:(c + 1) * 512]: E.bn_stats(out=o, in_=i), [tagx], ["lnstats"])
            K.P.op("vector", lambda E: E.bn_aggr(out=mv[:], in_=stats[:].rearrange("p a b -> p (a b)")), ["lnstats"], ["lnmv"])
            K.ts("vector", rstd[:], mv[:, 1:2], LN_EPS, None, ALU.add, None, ["lnmv"], ["lnrstd"])
            K.act(rstd[:], rstd[:], AF.Sqrt, ["lnrstd"], ["lnrstd"])
            K.recip(rstd[:], rstd[:], ["lnrstd"], ["lnrstd"])
            K.ts("vector", xt[:], xt[:], mv[:, 0:1], rstd[:, 0:1], ALU.subtract, ALU.mult, [tagx, "lnmv", "lnrstd"], [tagx])
            K.tt("gpsimd", xt[:], xt[:], gt, ALU.mult, [tagx, "lng"], [tagx])
            K.tt("gpsimd", xt[:], xt[:], bt, ALU.add, [tagx, "lnb"], [tagx])

        def phase_ln_emb():
            with ExitStack() as st:
                gt = sb(st, "lng", [128, D], F32)
                bt = sb(st, "lnb", [128, D], F32)
                xt2 = [sb(st, "xt", [128, D], F32) for _ in range(2)]
                stats = sb(st, "stats", [128, 4, 6], F32)
                mv = sb(st, "mv", [128, 2], F32)
                rstd = sb(st, "rstd", [128, 1], F32)
                K.dma("sync", gt[:], lnp_d[:, 0, :], [], ["lng"])
                K.dma("sync", bt[:], lnp_d[:, 1, :], [], ["lnb"])
                for i in range(NT):
                    xt = xt2[i % 2]
                    tag = f"xt{i%2}"
                    K.dma("sync", xt[:], x_d[i * 128:(i + 1) * 128, :], [], [tag])
                    layer_norm_tile(xt, None, gt[:], bt[:], stats, mv, rstd, tag)
                    K.dma("sync", xres_d[i * 128:(i + 1) * 128, :], xt[:], [tag], [f"xres{i}"])
            P.barrier()

        def phase_in(l, Z):
            with ExitStack() as st:
                xT = sb(st, "xT", [128, 16, T], BF16)
                Wb = [sb(st, "Wb", [128, 16, 512], BF16) for _ in range(2)]
                xin = [sb(st, "xin", [128, D], F32) for _ in range(2)]
                xbf = [sb(st, "xbf", [128, D], BF16) for _ in range(2)]
                ropeT = sb(st, "ropeT", [128, NT, 16], F32)
                stg = [sb(st, "stg", [128, 512], F32) for _ in range(2)]
                stb = [sb(st, "stb", [128, 512], BF16) for _ in range(2)]
                stT = [sb(st, "stT", [128, 4, 128], BF16) for _ in range(2)]
                rt = [sb(st, "rt", [128, 8, 8], F32) for _ in range(4)]
                K.dma("sync", ropeT[:], rope_d.rearrange("(i p) c -> p i c", p=128), [], ["ropeT"])
                for i in range(NT):
                    xi, xb = xin[i % 2], xbf[i % 2]
                    K.dma("sync", xi[:], xres_d[i * 128:(i + 1) * 128, :], [f"xres{i}"], [f"xin{i%2}"])
                    K.cp("scalar", xb[:], xi[:], [f"xin{i%2}"], [f"xbf{i%2}"])
                    for h in range(4):
                        pt = ps[(i * 4 + h) % 4]
                        ptn = f"ps{(i*4+h)%4}"
                        for j in range(4):
                            dc = h * 4 + j
                            K.mm(pt[:, j * 128:(j + 1) * 128], xb[:, dc * 128:(dc + 1) * 128], identb[:], True, True,
                                 [f"xbf{i%2}", "identb"], [ptn])
                        eng = "vector" if h % 2 == 0 else "scalar"
                        K.cp(eng, xT[:, h * 4:(h + 1) * 4, i * 128:(i + 1) * 128],
                             pt[:, :].rearrange("p (j c) -> p j c", j=4), [ptn], [f"xT{i}"])
                for cb in range(12):
                    W = Wb[cb % 2]
                    wn = f"Wb{cb%2}"
                    src = w_in_d[l].rearrange("(dc p) n -> p dc n", p=128)
                    for hh in range(2):
                        K.dma("gpsimd", W[:, hh * 8:(hh + 1) * 8, :], src[:, hh * 8:(hh + 1) * 8, cb * 512:(cb + 1) * 512], [], [wn])
                    for i in range(NT):
                        pk = 4 + (cb * NT + i) % 2
                        pt, ptn = ps[pk], f"ps{pk}"
                        for dc in range(16):
                            K.mm(pt[:, :], xT[:, dc, i * 128:(i + 1) * 128], W[:, dc, :], dc == 0, dc == 15, [f"xT{i}", wn], [ptn])
                        j0, mh = i // 2, i % 2
                        part = cb // 2
                        c0 = (cb % 2) * 512
                        k2 = i % 2
                        if part == 0:
                            eng = "vector" if i % 2 == 0 else "scalar"
                            K.cp(eng, Z[:, mh, (cb % 2) * 32:(cb % 2) * 32 + 32, j0, :],
                                 pt[:, :].rearrange("p (g c) -> p g c", c=16), [ptn], [f"Z{cb%2}"])
                        elif part in (1, 5):
                            K.act(stg[k2][:], pt[:, :], AF.Silu, [ptn], [f"stg{k2}"])
                            dst = gs_d if part == 1 else ga_d
                            K.dma("sync", dst[i * 128:(i + 1) * 128, c0:c0 + 512], stg[k2][:], [f"stg{k2}"],
                                  [f"{'gs' if part == 1 else 'ga'}{i}"])
                        elif part == 4:
                            K.cp("scalar", stb[k2][:], pt[:, :], [ptn], [f"stb{k2}"])
                            K.dma("sync", v_d[i * 128:(i + 1) * 128, c0:c0 + 512], stb[k2][:], [f"stb{k2}"], [f"v{i}"])
                        else:
                            pv = pt[:, :].rearrange("p (a d) -> p a d", d=64)
                            ob = stb[k2][:, :].rearrange("p (a d) -> p a d", d=64)
                            cosb = ropeT[:, i, 0:8].unsqueeze(1).broadcast_to([128, 8, 8])
                            sinb = ropeT[:, i, 8:16].unsqueeze(1).broadcast_to([128, 8, 8])
                            r1, r2 = pv[:, :, 0:8], pv[:, :, 8:16]
                            V = "vector"
                            K.tt(V, rt[0][:], r1, cosb, ALU.mult, [ptn, "ropeT"], ["rt0"])
                            K.tt(V, rt[1][:], r2, sinb, ALU.mult, [ptn, "ropeT"], ["rt1"])
                            K.tt(V, ob[:, :, 0:8], rt[0][:], rt[1][:], ALU.subtract, ["rt0", "rt1"], [f"stb{k2}"])
                            K.tt(V, rt[2][:], r2, cosb, ALU.mult, [ptn, "ropeT"], ["rt2"])
                            K.tt(V, rt[3][:], r1, sinb, ALU.mult, [ptn, "ropeT"], ["rt3"])
                            K.tt(V, ob[:, :, 8:16], rt[2][:], rt[3][:], ALU.add, ["rt2", "rt3"], [f"stb{k2}"])
                            K.cp("scalar", ob[:, :, 16:64], pv[:, :, 16:64], [ptn], [f"stb{k2}"])
                            p2k = 6 + (cb * NT + i) % 2
                            p2, p2n = ps[p2k], f"ps{p2k}"
                            for hh in range(4):
                                K.mm(p2[:, hh * 128:(hh + 1) * 128], stb[k2][:, hh * 128:(hh + 1) * 128], identb[:], True, True,
                                     [f"stb{k2}", "identb"], [p2n])
                            K.cp("vector", stT[k2][:], p2[:, :].rearrange("p (h c) -> p h c", h=4), [p2n], [f"stT{k2}"])
                            dst = qT_d if part == 2 else kT_d
                            h0 = (cb % 2) * 4
                            K.dma("sync", dst[h0:h0 + 4, :, i * 128:(i + 1) * 128].rearrange("h p t -> p h t"), stT[k2][:],
                                  [f"stT{k2}"], [f"{'qT' if part == 2 else 'kT'}"])
                Ust = [stb[0], stb[1]]
                for gp in range(G // 2):
                    pk = gp % 4
                    pt, ptn = ps[pk], f"ps{pk}"
                    for q in range(4):
                        g, mh = 2 * gp + q // 2, q % 2
                        K.mm(pt[:, q * 128:(q + 1) * 128], Z[:, mh, g].rearrange("p j c -> p (j c)"), identb[:], True, True,
                             [f"Z{g//32}", "identb"], [ptn])
                    eng = "vector" if gp % 2 == 0 else "scalar"
                    K.cp(eng, Ust[gp % 2][:], pt[:, :], [ptn], [f"stb{gp%2}"])
                    K.dma("sync", Ud_d[:, 2 * gp:2 * gp + 2, :], Ust[gp % 2][:].rearrange("p (g m) -> p g m", g=2), [f"stb{gp%2}"], ["Ud"])
            P.barrier()

        def alloc_ssm(st):
            B_ = {}
            B_["U"] = [sb(st, "U", [128, GB, 256], BF16) for _ in range(2)]
            B_["TTb"] = [sb(st, "TTb", [128, GB, 128], BF16) for _ in range(2)]
            B_["WAb"] = [sb(st, "WAb", [128, GB, 2, 128], BF16) for _ in range(2)]
            B_["CTb"] = B_["WAb"]
            B_["Vb"] = [sb(st, "Vb", [128, GB, 2, 256], BF16) for _ in range(2)]
            B_["V32"] = sb(st, "V32", [128, 2 * GB, 2, 16], F32)
            B_["X"] = sb(st, "X", [128, 2 * GB, 2], F32)
            B_["t1"] = sb(st, "rt1", [128, 2 * GB, 2], F32)
            B_["t2"] = sb(st, "rt2", [128, 2 * GB, 2], F32)
            B_["S0sb"] = None
            B_["Ysb"] = sb(st, "Ysb", [128, 2, 256], F32)
            B_["Yst"] = [sb(st, "Yst", [128, 2, 8, 8, 16], F32)] * 2
            return B_

        def gen_ssm(l, B_):
            U, TTb, WAb, CTb, Vb, V32, X, t1, t2 = (B_[n_] for n_ in ["U", "TTb", "WAb", "CTb", "Vb", "V32", "X", "t1", "t2"])
            Ysb, Yst, S0sb = B_["Ysb"], B_["Yst"], B_["S0sb"]
            V = "vector"
            ares = [f"A2a{l}", f"A2b{l}", f"A3a{l}", f"A3b{l}"]
            SB = [6, 7]
            for bp in range(NB // 2):
                for k in range(2):
                    g0 = (2 * bp + k) * GB
                    K.dma("sync", TTb[k][:], sTT_d[l, :, g0:g0 + GB, :], [f"sTT{l}"], [f"TTb{k}"])
                    K.dma("sync", WAb[k][:], sWA_d[l, :, g0:g0 + GB], [f"sWA{l}"], [f"WAb{k}"])
                    K.dma("sync", U[k][:], Ud_d[:, g0:g0 + GB, :], ["Ud"], [f"U{k}"])
                K.memset(CHAIN_ENG, X[:], 0.0, ["X"])
                yield
                for mb in range(16):
                    spvs = []
                    for k in range(2):
                        sp, spn = ps[SB[k]], f"ps{SB[k]}"
                        spv = sp[:, :].rearrange("p (g r m) -> p g r m", g=GB, r=2)
                        spvs.append((spv, spn))
                        for gl in range(GB):
                            for ri in range(2):
                                K.mm(spv[0:64, gl, ri, :], WAb[k][:, gl, ri, 0:64], U[k][:, gl, 16 * mb:16 * mb + 16], True, True,
                                     [f"WAb{k}", f"U{k}"], [spn])
                                lo = 255 - 16 * mb
                                rhs_b = U[k][:, gl, lo - 15:lo + 1][:, ::-1]
                                K.mm(spv[64:128, gl, ri, :], WAb[k][:, gl, ri, 64:128], rhs_b, True, True, [f"WAb{k}", f"U{k}"], [spn])
                            if gl % 4 == 3:
                                yield
                    gA = 2 * bp * GB
                    s0v = psB[:, :].rearrange("p (g r m) -> p g r m", g=2 * GB, r=2)
                    for j in range(16):
                        K.tt(V, t1[:], X[:], A2[:, l, gA:gA + 2 * GB, :], ALU.mult, ["X"] + ares, ["rt1"])
                        K.tt(V, t2[:], X[:, :, ::-1], A3[:, l, gA:gA + 2 * GB, :], ALU.mult, ["X"] + ares, ["rt2"])
                        K.tt(V, V32[:, :, :, j], t1[:], t2[:], ALU.add, ["rt1", "rt2"], ["V32"])
                        K.tt(V, X[:], V32[:, :, :, j], s0v[:, :, :, j], ALU.add, ["V32", "ps6", "ps7"], ["X"])
                        yield
                    for k in range(2):
                        v32 = V32[:, k * GB:(k + 1) * GB]
                        K.cp("gpsimd", Vb[k][0:64, :, :, 16 * mb:16 * mb + 16], v32[0:64], ["V32"], [f"Vb{k}"])
                        lo = 255 - 16 * mb
                        K.cp("gpsimd", Vb[k][64:128, :, :, lo - 15:lo + 1], v32[64:128][:, :, :, ::-1], ["V32"], [f"Vb{k}"])
                    yield
                for k in range(2):
                    g0 = (2 * bp + k) * GB
                    K.dma("sync", CTb[k][:], sCT_d[l, :, g0:g0 + GB], [f"sCT{l}"], [f"WAb{k}"])
                for k in range(2):
                    g0 = (2 * bp + k) * GB
                    Ub, Tb_, Cb_, Vbb = U[k], TTb[k], CTb[k], Vb[k]
                    un, tn, cn, vn = f"U{k}", f"TTb{k}", f"WAb{k}", f"Vb{k}"
                    for gp in range(GB // 2):
                        yp, ypn = ps[6], "ps6"
                        ys, ysn = Ysb, "Ysb"
                        for q in range(2):
                            gl = 2 * gp + q
                            o = yp[:, q * 256:(q + 1) * 256]
                            K.mm(o, Tb_[:, gl, :], Ub[:, gl, :], True, False, [tn, un], [ypn])
                            K.mm(o, Cb_[:, gl, 0, :], Vbb[:, gl, 0, :], False, False, [cn, vn], [ypn])
                            K.mm(o, Cb_[:, gl, 1, :], Vbb[:, gl, 1, :], False, True, [cn, vn], [ypn])
                        K.cp("scalar", ys[:].rearrange("p a b -> p (a b)"), yp[:, :], [ypn], [ysn])
                        tp, tpn = ps[7], "ps7"
                        for q in range(2):
                            for mh in range(2):
                                K.tr(tp[:, (q * 2 + mh) * 128:(q * 2 + mh + 1) * 128], ys[:, q, mh * 128:(mh + 1) * 128], identf[:],
                                     [ysn, "identf"], [tpn])
                        g8 = (g0 + 2 * gp) // 8
                        yst, ystn = Yst[0], "Yst"
                        for q in range(2):
                            gin8 = (g0 + 2 * gp + q) % 8
                            K.cp(V, yst[:, :, :, gin8, :],
                                 tp[:, q * 256:(q + 1) * 256].rearrange("p (m i c) -> p m i c", m=2, i=8), [tpn], [ystn])
                        if (g0 + 2 * gp + 1) % 8 == 7:
                            dst = ytok_d[:, g8 * 128:(g8 + 1) * 128].rearrange("(i m p) c -> p m i c", i=8, m=2)
                            for mh in range(2):
                                K.dma("sync", dst[:, mh], yst[:, mh].rearrange("p i g c -> p i (g c)"), [ystn], ["ytok"])
                        yield

        def phase_glu_out(l, last):
            LAG = 6
            with ExitStack() as st:
                mixT = sb(st, "mixT", [128, 16, T], BF16)
                Wo = sb(st, "Wo", [128, 16, D], BF16)
                Wg = sb(st, "Wg", [128, 8, 1024], BF16)
                bg = sb(st, "bg", [128, 1024], F32)
                yt = sb(st, "yt", [128, 1024], F32)
                gst = sb(st, "gst", [128, 1024], F32)
                ybf = [sb(st, "ybf", [128, 1024], BF16) for _ in range(2)]
                yT = [sb(st, "yT", [128, 8, 128], BF16) for _ in range(2)]
                z = sb(st, "z", [128, 1024], F32)
                gt = sb(st, "lng", [128, D], F32)
                bt = sb(st, "lnb", [128, D], F32)
                xt2 = [sb(st, "xt", [128, D], F32) for _ in range(2)]
                stats = sb(st, "stats", [128, 4, 6], F32)
                mv = sb(st, "mv", [128, 2], F32)
                rstd = sb(st, "rstd", [128, 1], F32)
                msrc = mixT_d.rearrange("(ec p) t -> p ec t", p=128)
                K.dma("gpsimd", Wg[:], w_glu_d[l].rearrange("(kc p) n -> p kc n", p=128), [], ["Wg"])
                wsrc = w_out_d[l].rearrange("(ec p) n -> p ec n", p=128)
                for ec in range(16):
                    K.dma("gpsimd", Wo[:, ec, :], wsrc[:, ec, :], [], [f"Wo{ec}"])
                K.dma("sync", bg[:], bglu_d[:, l, :], [], ["bg"])
                for ec in range(8, 16):
                    K.dma("sync", mixT[:, ec, :], msrc[:, ec, :], ["mixT_a"], [f"mixT{ec}"])
                K.dma("sync", gt[:], lnp_d[:, 2 + 2 * l, :], [], ["lng"])
                K.dma("sync", bt[:], lnp_d[:, 3 + 2 * l, :], [], ["lnb"])

                def glu_tile(i):
                    k = i % 2
                    K.dma("sync", yt[:], ytok_d[i * 128:(i + 1) * 128, :], ["ytok"], ["yt"])
                    K.dma("sync", gst[:], gs_d[i * 128:(i + 1) * 128, :], [f"gs{i}"], ["gst"])
                    K.act(yt[:], yt[:], AF.Gelu_apprx_tanh, ["yt"], ["yt"])
                    K.cp("vector", ybf[k][:], yt[:], ["yt"], [f"ybf{k}"])
                    for h in range(2):
                        pt, ptn = ps[h], f"ps{h}"
                        for j in range(4):
                            kc = h * 4 + j
                            K.mm(pt[:, j * 128:(j + 1) * 128], ybf[k][:, kc * 128:(kc + 1) * 128], identb[:], True, True,
                                 [f"ybf{k}", "identb"], [ptn])
                        K.cp("scalar" if h == 0 else "vector", yT[k][:, h * 4:(h + 1) * 4, :], pt[:, :].rearrange("p (j c) -> p j c", j=4),
                             [ptn], [f"yT{k}{h}"])
                    for cb in range(2):
                        pz, pzn = ps[2 + cb], f"ps{2+cb}"
                        for kc in range(8):
                            K.mm(pz[:, :], yT[k][:, kc, :], Wg[:, kc, cb * 512:(cb + 1) * 512], kc == 0, kc == 7,
                                 [f"yT{k}0", f"yT{k}1", "Wg"], [pzn])
                        K.tt("vector", z[:, cb * 512:(cb + 1) * 512], pz[:, :], bg[:, cb * 512:(cb + 1) * 512], ALU.add, [pzn, "bg"], [f"z{cb}"])
                    K.act(z[:], z[:], AF.Sigmoid, ["z0", "z1"], ["z"])
                    K.tt("gpsimd", z[:], z[:], yt[:], ALU.mult, ["z", "yt"], ["z"])
                    K.tt("gpsimd", ybf[k][:], z[:], gst[:], ALU.mult, ["z", "gst"], [f"ybf{k}"])
                    for h in range(2):
                        pt, ptn = ps[4 + h], f"ps{4+h}"
                        for j in range(4):
                            kc = h * 4 + j
                            K.mm(pt[:, j * 128:(j + 1) * 128], ybf[k][:, kc * 128:(kc + 1) * 128], identb[:], True, True,
                                 [f"ybf{k}", "identb"], [ptn])
                        K.cp("scalar" if h == 0 else "vector", mixT[:, h * 4:(h + 1) * 4, i * 128:(i + 1) * 128],
                             pt[:, :].rearrange("p (j c) -> p j c", j=4), [ptn], [f"mixT{ec}" for ec in range(h * 4, h * 4 + 4)])

                def out_tile(i):
                    xt = xt2[i % 2]
                    tag = f"xt{i%2}"
                    K.dma("sync", xt[:], xres_d[i * 128:(i + 1) * 128, :], [f"xres{i}"], [tag])
                    for cb in range(4):
                        pk = 6 + cb % 2
                        pt, ptn = ps[pk], f"ps{pk}"
                        for ec in range(16):
                            K.mm(pt[:, :], mixT[:, ec, i * 128:(i + 1) * 128], Wo[:, ec, cb * 512:(cb + 1) * 512], ec == 0, ec == 15,
                                 [f"mixT{ec}", f"Wo{ec}"], [ptn])
                        K.stt(xt[:, cb * 512:(cb + 1) * 512], xt[:, cb * 512:(cb + 1) * 512], ALPHA, pt[:, :], ALU.mult, ALU.add,
                              [tag, ptn], [tag])
                    layer_norm_tile(xt, None, gt[:], bt[:], stats, mv, rstd, tag)
                    dst = out_d if last else xres_d
                    K.dma("sync", dst[i * 128:(i + 1) * 128, :], xt[:], [tag], [f"xres{i}"])

                for i in range(NT + LAG):
                    if i < NT:
                        glu_tile(i)
                    if i >= LAG:
                        out_tile(i - LAG)
            P.barrier()

        def alloc_attn(st):
            B_ = {}
            B_["qT"] = [sb(st, "qT", [128, T], BF16) for _ in range(2)]
            B_["kT"] = [sb(st, "kT", [128, T], BF16) for _ in range(2)]
            B_["Va"] = [sb(st, "Va", [128, 16, 132], BF16) for _ in range(2)]
            B_["GA"] = [sb(st, "GA", [128, 4, 128], F32) for _ in range(2)]
            B_["ET"] = [[sb(st, "ET", [128, 16, 512], BF16) for _ in range(2)] for _ in range(2)]
            B_["gN"] = sb(st, "gN", [128, 128], F32)
            B_["sm"] = {n_: sb(st, n_, [128, 1], F32) for n_ in ["r1", "r2", "c2", "ss", "rr"]}
            B_["tA"] = sb(st, "atA", [128, 128], F32)
            B_["o"] = sb(st, "ao", [128, 128], F32)
            B_["sq"] = sb(st, "asq", [128, 128], F32)
            B_["obf"] = sb(st, "aobf", [128, 128], BF16)
            B_["yaT"] = [sb(st, "yaT", [128, 512], BF16) for _ in range(2)]
            return B_

        def gen_attn(l, B_):
            lam_init = 0.8 - 0.6 * math.exp(-0.3 * l)
            qT, kT, Va, GA, ET, gN, sm_ = (B_[n_] for n_ in ["qT", "kT", "Va", "GA", "ET", "gN", "sm"])
            tA, o_, sq, obf, yaT = (B_[n_] for n_ in ["tA", "o", "sq", "obf", "yaT"])
            K.dma("sync", gN[:], angg_d[:, l, :], [], ["gN"])
            for k in range(2):
                K.memset("vector", Va[k][:, :, 128:129], 1.0, [f"Va1{k}"])
            V = "vector"

            def load_head(h):
                k = h % 2
                K.dma("sync", qT[k][:], qT_d[h], ["qT"], [f"qT{k}"])
                K.dma("sync", kT[k][:], kT_d[h], ["kT"], [f"kT{k}"])
                K.dma("sync", Va[k][:, :, 0:128], v_d[:, h * 128:(h + 1) * 128].rearrange("(i p) c -> p i c", p=128),
                      [f"v{i}" for i in range(NT)], [f"Va{k}"])

            def load_ga(blk):
                h, qb = blk // 4, blk % 4
                K.dma("sync", GA[blk % 2][:], ga_d[qb * 512:(qb + 1) * 512, h * 128:(h + 1) * 128].rearrange("(i p) c -> p i c", p=128),
                      [f"ga{i}" for i in range(NT)], [f"GA{blk%2}"])

            def scores(blk, kc2):
                h, qb = blk // 4, blk % 4
                k, eb = h % 2, blk % 2
                for mp in range(2):
                    for u_ in range(2):
                        kc = 2 * kc2 + u_
                        sk = mp * 2 + u_
                        K.mm(ps[sk][:, :], kT[k][64 * mp:64 * mp + 64, kc * 128:(kc + 1) * 128],
                             qT[k][64 * mp:64 * mp + 64, qb * 512:(qb + 1) * 512], True, True, [f"kT{k}", f"qT{k}"], [f"psc{mp}"])
                    K.act(ET[eb][mp][:, 2 * kc2:2 * kc2 + 2, :].rearrange("p a b -> p (a b)"), psA[:, mp * 1024:(mp + 1) * 1024], AF.Exp,
                          [f"psc{mp}"], [f"ET{eb}{mp}_{kc2}"], scale=0.125)

            def pv(blk, step):
                h, qb = blk // 4, blk % 4
                k, eb = h % 2, blk % 2
                qt, mp = step // 2, step % 2
                ob, obn = ps[4 + qt % 2], f"ps{4+qt%2}"
                for kc in range(16):
                    K.mm(ob[:, mp * 129:mp * 129 + 129], ET[eb][mp][:, kc, qt * 128:(qt + 1) * 128], Va[k][:, kc, 0:129], kc == 0, kc == 15,
                         [f"ET{eb}{mp}_{kc//2}", f"Va{k}", f"Va1{k}"], [obn])
                if mp == 0:
                    return
                O1, O2 = ob[:, 0:129], ob[:, 129:258]
                K.recip(sm_["r1"][:], O1[:, 128:129], [obn], ["r1"])
                K.recip(sm_["r2"][:], O2[:, 128:129], [obn], ["r2"])
                K.tt(V, sm_["c2"][:], sm_["r2"][:], neglam[:, l:l + 1], ALU.mult, ["r2", "neglam"], ["c2"])
                K.ts(V, tA[:], O2[:, 0:128], sm_["c2"][:, 0:1], None, ALU.mult, None, [obn, "c2"], ["atA"])
                K.stt(o_[:], O1[:, 0:128], sm_["r1"][:, 0:1], tA[:], ALU.mult, ALU.add, [obn, "r1", "atA"], ["ao"])
                K.P.op("vector", lambda E, o=sq[:], a=o_[:], acc=sm_["ss"][:]: E.scalar_tensor_tensor(
                    out=o, in0=a, scalar=1.0, in1=a, op0=ALU.mult, op1=ALU.mult, accum_out=acc), ["ao"], ["asq", "ss"])
                K.ts(V, sm_["rr"][:], sm_["ss"][:], 1.0 / 128.0, RMS_EPS, ALU.mult, ALU.add, ["ss"], ["rr"])
                K.act(sm_["rr"][:], sm_["rr"][:], AF.Ln, ["rr"], ["rr"])
                K.act(sm_["rr"][:], sm_["rr"][:], AF.Exp, ["rr"], ["rr"], scale=-0.5)
                K.stt(o_[:], o_[:], sm_["rr"][:, 0:1], gN[:], ALU.mult, ALU.mult, ["ao", "rr", "gN"], ["ao"])
                K.stt(obf[:], o_[:], 1.0 - lam_init, GA[blk % 2][:, qt, :], ALU.mult, ALU.mult, ["ao", f"GA{blk%2}"], ["aobf"])
                K.mm(ob[:, 384:512], obf[:], identb[:], True, True, ["aobf", "identb"], [obn])
                ya, yan = yaT[qb % 2], f"yaT{qb%2}"
                K.cp("vector", ya[:, qt * 128:(qt + 1) * 128], ob[:, 384:512], [obn], [yan])
                if qt == 3:
                    K.dma("sync", mixT_d[1024 + h * 128:1024 + (h + 1) * 128, qb * 512:(qb + 1) * 512], ya[:], [yan], ["mixT_a"])

            load_head(0)
            for blk in range(33):
                if blk < 32 and blk % 4 == 1 and blk // 4 + 1 < 8:
                    load_head(blk // 4 + 1)
                if blk < 32:
                    load_ga(blk)
                for kc2 in range(8):
                    if blk < 32:
                        scores(blk, kc2)
                    if blk >= 1:
                        pv(blk - 1, kc2)
                    yield

        def phase_out(l, last):
            with ExitStack() as st:
                mixT = sb(st, "mixT", [128, 16, T], BF16)
                Wo = sb(st, "Wo", [128, 16, D], BF16)
                gt = sb(st, "lng", [128, D], F32)
                bt = sb(st, "lnb", [128, D], F32)
                xt2 = [sb(st, "xt", [128, D], F32) for _ in range(2)]
                stats = sb(st, "stats", [128, 4, 6], F32)
                mv = sb(st, "mv", [128, 2], F32)
                rstd = sb(st, "rstd", [128, 1], F32)
                K.dma("sync", gt[:], lnp_d[:, 2 + 2 * l, :], [], ["lng"])
                K.dma("sync", bt[:], lnp_d[:, 3 + 2 * l, :], [], ["lnb"])
                msrc = mixT_d.rearrange("(ec p) t -> p ec t", p=128)
                for ec in range(16):
                    K.dma("sync", mixT[:, ec, :], msrc[:, ec, :], ["mixT_s", "mixT_a"], [f"mixT{ec}"])
                wsrc = w_out_d[l].rearrange("(ec p) n -> p ec n", p=128)
                for ec in range(16):
                    K.dma("gpsimd", Wo[:, ec, :], wsrc[:, ec, :], [], [f"Wo{ec}"])
                for i in range(NT):
                    xt = xt2[i % 2]
                    tag = f"xt{i%2}"
                    K.dma("sync", xt[:], xres_d[i * 128:(i + 1) * 128, :], [f"xres{i}"], [tag])
                    for cb in range(4):
                        pk = (i * 4 + cb) % 8
                        pt, ptn = ps[pk], f"ps{pk}"
                        for ec in range(16):
                            K.mm(pt[:, :], mixT[:, ec, i * 128:(i + 1) * 128], Wo[:, ec, cb * 512:(cb + 1) * 512], ec == 0, ec == 15,
                                 [f"mixT{ec}", f"Wo{ec}"], [ptn])
                        K.stt(xt[:, cb * 512:(cb + 1) * 512], xt[:, cb * 512:(cb + 1) * 512], ALPHA, pt[:, :], ALU.mult, ALU.add,
                              [tag, ptn], [tag])
                    layer_norm_tile(xt, None, gt[:], bt[:], stats, mv, rstd, tag)
                    dst = out_d if last else xres_d
                    K.dma("sync", dst[i * 128:(i + 1) * 128, :], xt[:], [tag], [f"xres{i}"])
            P.barrier()

        def run(gen):
            for _ in gen:
                pass

        def count(genf):
            K.dry = True
            n = sum(1 for _ in genf())
            K.dry = False
            return n

        def co_run(ga, na, gb, nb):
            ia = ib = 0
            da = db = False
            while not (da and db):
                if not da and (db or ia * nb <= ib * na):
                    try:
                        next(ga)
                        ia += 1
                    except StopIteration:
                        da = True
                else:
                    try:
                        next(gb)
                        ib += 1
                    except StopIteration:
                        db = True

        for l in range(n_layers):
            ssm_pre(l)
        lam_pre()
        phase_ln_emb()
        for l in range(n_layers):
            with ExitStack() as lst:
                Z = sb(lst, "Z", [128, 2, G, 8, 16], BF16)
                phase_in(l, Z)
            if stop_after == ("in", l):
                break
            with ExitStack() as lst:
                BA = alloc_attn(lst)
                BS = alloc_ssm(lst)
                na = count(lambda: gen_attn(l, BA))
                ns = count(lambda: gen_ssm(l, BS))
                if COVERLAP:
                    co_run(gen_attn(l, BA), na, gen_ssm(l, BS), ns)
                else:
                    run(gen_ssm(l, BS))
                    run(gen_attn(l, BA))
                P.barrier()
            if stop_after == ("attn", l):
                break
            phase_glu_out(l, l == n_layers - 1)
        if dbg:
            for nme, (src, shp, dt) in {"xres": (xres_d, [T, D], F32), "gs": (gs_d, [T, 1024], F32), "ga": (ga_d, [T, 1024], F32),
                                        "qT": (qT_d, [8, 128, T], BF16), "kT": (kT_d, [8, 128, T], BF16), "v": (v_d, [T, 1024], BF16),
                                        "ytok": (ytok_d, [T, 1024], F32), "mixT": (mixT_d, [D, T], BF16),
                                        "sTT": (sTT_d, [L, 128, G, 128], BF16), "sWA": (sWA_d, [L, 128, G, 2, 128], BF16),
                                        "sCT": (sCT_d, [L, 128, G, 2, 128], BF16)}.items():
                dd = ddbg(nme, shp, dt)
                P.barrier()
                if len(shp) == 2:
                    for i in range(0, shp[0], 512):
                        K.dma("sync", dd[i:i + 512], src[i:i + 512], [], ["dbgout"])
                else:
                    for i in range(shp[0]):
                        K.dma("sync", dd[i], src[i], [], ["dbgout"])
        P.wait_all("sync")
        P.emit()
    return nc


def _tok_perm():
    tau = np.arange(T)
    j0, m = tau // 256, tau % 256
    return 8 * m + j0


def prep_shared(inp):
    f = np.float32
    perm = _tok_perm()
    sh = {}
    sh["w_in"] = np.ascontiguousarray(inp["w_in"], dtype=f)
    sh["w_glu"] = np.ascontiguousarray(inp["w_glu"], dtype=f)
    sh["w_out"] = np.ascontiguousarray(inp["w_out"], dtype=f)
    rows = [inp["ln_emb_g"], inp["ln_emb_b"]]
    for l in range(L):
        rows += [inp["ln_g"][l], inp["ln_b"][l]]
    lnp = np.stack(rows, 0).astype(f)
    sh["lnp"] = np.ascontiguousarray(np.broadcast_to(lnp[None], (128,) + lnp.shape))
    sh["bglu"] = np.ascontiguousarray(np.broadcast_to(np.asarray(inp["b_glu"], f)[None], (128, L, 1024)))
    sh["angg"] = np.ascontiguousarray(np.broadcast_to(np.asarray(inp["attn_norm_g"], f)[None], (128, L, 128)))
    lamv = np.stack([inp["lambda_q1"], inp["lambda_k1"], inp["lambda_q2"], inp["lambda_k2"]], 1).astype(f)
    sh["lamv"] = np.ascontiguousarray(np.broadcast_to(lamv[None], (128, L, 4, 64)))
    lr = np.asarray(inp["ssm_lam_re"], f)
    li = np.asarray(inp["ssm_lam_im"], f)
    ls = np.asarray(inp["ssm_log_step"], f)
    s = np.zeros((2, 64, L, 3, G), f)
    s[:, :, :, 0, :] = lr.transpose(1, 3, 0, 2)
    s[:, :, :, 1, :] = li.transpose(1, 3, 0, 2)
    s[:, :, :, 2, :] = np.broadcast_to(ls.transpose(1, 0, 2)[:, None], (2, 64, L, G))
    sh["ssm_s"] = np.ascontiguousarray(s.reshape(128, L, 3, G))
    br = np.asarray(inp["ssm_b_re"], f)
    bi = np.asarray(inp["ssm_b_im"], f)
    bq = np.stack([br, bi], 0)
    sh["ssm_b"] = np.ascontiguousarray(bq.transpose(2, 4, 1, 0, 3, 5).reshape(128, L, 2, G, 16))
    cr = np.asarray(inp["ssm_c_re"], f)
    ci = np.asarray(inp["ssm_c_im"], f)
    cq = np.stack([cr, ci], 0)
    sh["ssm_c"] = np.ascontiguousarray(cq.transpose(2, 5, 1, 0, 3, 4).reshape(128, L, 2, G, 16))
    d = np.asarray(inp["ssm_d"], f).reshape(L, G, 16)
    dcol = np.broadcast_to(d.transpose(2, 0, 1)[None], (8, 16, L, G)).reshape(128, L, G)
    sh["ssm_dcol"] = np.ascontiguousarray(dcol)
    pos = perm.astype(np.float32)
    inv_freq = (np.float32(500000.0) ** (-np.arange(0, 16, 2, dtype=np.float32) / np.float32(16))).astype(np.float32)
    ang = (pos[:, None] * inv_freq[None, :]).astype(np.float32)
    sh["rope"] = np.ascontiguousarray(np.concatenate([np.cos(ang), np.sin(ang)], 1).astype(f))
    cst = np.zeros((128, 4, 128), f)
    cst[:, 0, :] = np.eye(128, dtype=f)
    jj = np.arange(128) // 16
    cst[:, 1, :] = (jj[None, :] >= jj[:, None]).astype(f)
    cst[:, 2, :] = (jj[None, :] <= jj[:, None]).astype(f)
    cst[0:64, 3, 0] = 1.0
    cst[64:128, 3, 1] = 1.0
    sh["cst"] = cst
    return sh, perm


_NC_CACHE = {}


def kernel(**inputs):
    inp = {k: np.asarray(v) for k, v in inputs.items()}
    sh, perm = prep_shared(inp)
    x = np.asarray(inp["x"], np.float32)
    in_maps = []
    for c in range(8):
        m = dict(sh)
        m["x"] = np.ascontiguousarray(x[c][perm])
        in_maps.append(m)
    if "nc" not in _NC_CACHE:
        _NC_CACHE["nc"] = build()
    nc = _NC_CACHE["nc"]
    res = run_bass_kernel_spmd(nc, in_maps, core_ids=list(range(8)))
    out = np.empty((8, T, D), np.float32)
    for c in range(8):
        out[c][perm] = res.results[c]["out"]
    return out
```

```python
import math
from contextlib import ExitStack
import numpy as np
import ml_dtypes
import concourse.bass as bass
import concourse.mybir as mybir
from concourse.bass_utils import run_bass_kernel_spmd

F32 = mybir.dt.float32
BF16 = mybir.dt.bfloat16
AF = mybir.ActivationFunctionType
ALU = mybir.AluOpType
AX = mybir.AxisListType

L = 2
T = 2048
D = 2048
NT = 16
G = 64
DIN = 6144
LN_EPS = 1e-5
RMS_EPS = 1e-5
ALPHA = (2.0 * L) ** 0.25
TWO_PI = 2.0 * math.pi
GB = 16
NB = G // GB

COVERLAP = True
CHAIN_ENG = "vector"
SAME_ENGINE_SYNC = True


class Prog:
    ENGS = ["tensor", "vector", "scalar", "gpsimd", "sync"]
    NDMA = 10

    def __init__(self, nc):
        self.nc = nc
        self.q = {e: [] for e in self.ENGS}
        self.sem = {e: nc.alloc_semaphore(name=f"sem_{e}") for e in self.ENGS}
        self.cnt = {e: 0 for e in self.ENGS}
        self.dq = ["sync", "gpsimd", "scalar"]
        self.dsem = {e: [nc.alloc_semaphore(name=f"dsem_{e}_{i}") for i in range(self.NDMA)] for e in self.dq}
        self.dval = {e: [0] * self.NDMA for e in self.dq}
        self.dnext = {e: 0 for e in self.dq}
        self.semobj = {}
        for e in self.ENGS:
            self.semobj[("c", e)] = self.sem[e]
        for e in self.dq:
            for i in range(self.NDMA):
                self.semobj[("d", e, i)] = self.dsem[e][i]
        self.known = {e: {} for e in self.ENGS}
        self.res = {}
        self.ninstr = 0

    def _wait(self, eng, key, val):
        if self.known[eng].get(key, 0) >= val:
            return
        self.known[eng][key] = val
        so = self.semobj[key]
        self.q[eng].append(lambda E, so=so, val=val: E.wait_ge(so, val))

    def _deps(self, eng, reads, writes):
        deps = {}

        def add(tok):
            if tok is None:
                return
            k, v = tok
            if deps.get(k, 0) < v:
                deps[k] = v

        for r in reads:
            st = self.res.get(r)
            if st:
                add(st["w"])
        for w in writes:
            st = self.res.get(w)
            if st:
                add(st["w"])
                for t in st["r"]:
                    add(t)
        for k, v in deps.items():
            if k == ("c", eng) and (eng == "tensor" or not SAME_ENGINE_SYNC):
                continue
            self._wait(eng, k, v)

    def _record(self, tok, reads, writes):
        for r in reads:
            st = self.res.setdefault(r, {"w": None, "r": []})
            st["r"] = [t for t in st["r"] if t[0] != tok[0]] + [tok]
        for w in writes:
            self.res[w] = {"w": tok, "r": []}

    def op(self, eng, fn, reads=(), writes=()):
        self._deps(eng, reads, writes)
        self.cnt[eng] += 1
        c = self.cnt[eng]
        so = self.sem[eng]
        self.q[eng].append(lambda E, fn=fn, so=so: fn(E).then_inc(so, 1))
        self.ninstr += 1
        tok = (("c", eng), c)
        self._record(tok, reads, writes)
        return tok

    def dma(self, eng, out, in_, reads=(), writes=()):
        self._deps(eng, reads, writes)
        i = self.dnext[eng]
        self.dnext[eng] = (i + 1) % self.NDMA
        key = ("d", eng, i)
        if self.dval[eng][i] > 0:
            self._wait(eng, key, self.dval[eng][i])
        self.dval[eng][i] += 16
        v = self.dval[eng][i]
        so = self.dsem[eng][i]
        self.q[eng].append(lambda E, so=so, out=out, in_=in_: E.dma_start(out=out, in_=in_).then_inc(so, 16))
        self.ninstr += 1
        tok = (key, v)
        self._record(tok, reads, writes)
        return tok

    def wait_all(self, eng):
        for e in self.ENGS:
            if self.cnt[e] > 0 and e != eng:
                self._wait(eng, ("c", e), self.cnt[e])
        for e in self.dq:
            for i in range(self.NDMA):
                if self.dval[e][i] > 0:
                    self._wait(eng, ("d", e, i), self.dval[e][i])

    def barrier(self):
        for e in self.ENGS:
            self.wait_all(e)

    def emit(self):
        nc = self.nc
        with nc.Block() as block:
            @block.tensor
            def _(E):
                for f in self.q["tensor"]:
                    f(E)

            @block.vector
            def _(E):
                for f in self.q["vector"]:
                    f(E)

            @block.scalar
            def _(E):
                for f in self.q["scalar"]:
                    f(E)

            @block.gpsimd
            def _(E):
                for f in self.q["gpsimd"]:
                    f(E)

            @block.sync
            def _(E):
                for f in self.q["sync"]:
                    f(E)


class _DryP:
    def op(self, *a, **k):
        pass

    def dma(self, *a, **k):
        pass


class KB:
    def __init__(self, nc):
        self.nc = nc
        self._P = Prog(nc)
        self._dry = _DryP()
        self.dry = False

    @property
    def P(self):
        return self._dry if self.dry else self._P

    def mm(self, out, lhsT, rhs, start, stop, r, w):
        self.P.op("tensor", lambda E: E.matmul(out, lhsT=lhsT, rhs=rhs, start=start, stop=stop), r, w)

    def tr(self, out, in_, ident, r, w):
        self.P.op("tensor", lambda E: E.transpose(out, in_, ident), r, w)

    def act(self, out, in_, func, r, w, bias=None, scale=None, accum_out=None):
        kw = {}
        if bias is not None:
            kw["bias"] = bias
        if scale is not None:
            kw["scale"] = scale
        if accum_out is not None:
            kw["accum_out"] = accum_out
        self.P.op("scalar", lambda E: E.activation(out=out, in_=in_, func=func, **kw), r, w)

    def tt(self, eng, out, in0, in1, op, r, w):
        self.P.op(eng, lambda E: E.tensor_tensor(out=out, in0=in0, in1=in1, op=op), r, w)

    def ts(self, eng, out, in0, s1, s2, op0, op1, r, w):
        if op1 is None:
            self.P.op(eng, lambda E: E.tensor_scalar(out=out, in0=in0, scalar1=s1, scalar2=None, op0=op0), r, w)
        else:
            self.P.op(eng, lambda E: E.tensor_scalar(out=out, in0=in0, scalar1=s1, scalar2=s2, op0=op0, op1=op1), r, w)

    def stt(self, out, in0, scalar, in1, op0, op1, r, w):
        self.P.op("vector", lambda E: E.scalar_tensor_tensor(out=out, in0=in0, scalar=scalar, in1=in1, op0=op0, op1=op1), r, w)

    def cp(self, eng, out, in_, r, w):
        if eng == "scalar":
            self.P.op("scalar", lambda E: E.copy(out=out, in_=in_), r, w)
        else:
            self.P.op(eng, lambda E: E.tensor_copy(out=out, in_=in_), r, w)

    def memset(self, eng, ap, val, w):
        self.P.op(eng, lambda E: E.memset(ap, val), (), w)

    def recip(self, out, in_, r, w):
        self.P.op("vector", lambda E: E.reciprocal(out=out, in_=in_), r, w)

    def dma(self, eng, out, in_, r, w):
        self.P.dma(eng, out, in_, r, w)


def build(n_layers=L, stop_after=None, dbg=False):
    nc = bass.Bass("TRN2", target_bir_lowering=False)
    K = KB(nc)
    P = K._P

    def din(name, shape, dt=F32):
        return nc.dram_tensor(name, list(shape), dt, kind="ExternalInput").ap()

    def dscr(name, shape, dt):
        return nc.dram_tensor(name, list(shape), dt, kind="Internal").ap()

    x_d = din("x", [T, D])
    w_in_d = din("w_in", [L, D, DIN])
    w_glu_d = din("w_glu", [L, 1024, 1024])
    w_out_d = din("w_out", [L, D, D])
    lnp_d = din("lnp", [128, 2 + 2 * L, D])
    bglu_d = din("bglu", [128, L, 1024])
    angg_d = din("angg", [128, L, 128])
    lamv_d = din("lamv", [128, L, 4, 64])
    ssms_d = din("ssm_s", [128, L, 3, G])
    ssmb_d = din("ssm_b", [128, L, 2, G, 16])
    ssmc_d = din("ssm_c", [128, L, 2, G, 16])
    dcol_d = din("ssm_dcol", [128, L, G])
    rope_d = din("rope", [T, 16])
    cst_d = din("cst", [128, 4, 128])
    out_d = nc.dram_tensor("out", [T, D], F32, kind="ExternalOutput").ap()

    xres_d = dscr("xres", [T, D], F32)
    gs_d = dscr("gs", [T, 1024], F32)
    ga_d = dscr("ga", [T, 1024], F32)
    qT_d = dscr("qT", [8, 128, T], BF16)
    kT_d = dscr("kT", [8, 128, T], BF16)
    v_d = dscr("v", [T, 1024], BF16)
    ytok_d = dscr("ytok", [T, 1024], F32)
    mixT_d = dscr("mixT", [D, T], BF16)
    Ud_d = dscr("Ud", [128, G, 256], BF16)
    ya_d = dscr("ya", [T, 1024], BF16)
    sTT_d = dscr("sTT", [L, 128, G, 128], BF16)
    sWA_d = dscr("sWA", [L, 128, G, 2, 128], BF16)
    sCT_d = dscr("sCT", [L, 128, G, 2, 128], BF16)
    dbg_d = {}

    def ddbg(name, shape, dt=F32):
        dbg_d[name] = nc.dram_tensor("dbg_" + name, list(shape), dt, kind="ExternalOutput").ap()
        return dbg_d[name]

    uid = [0]

    def nm(s):
        uid[0] += 1
        return f"{s}_{uid[0]}"

    with ExitStack() as gstack:
        def sb(stack, name, shape, dt):
            return stack.enter_context(nc.sbuf_tensor(nm(name), list(shape), dt))

        identb = sb(gstack, "identb", [128, 128], BF16)
        identf = sb(gstack, "identf", [128, 128], F32)
        maskF = sb(gstack, "maskF", [128, 128], F32)
        maskB = sb(gstack, "maskB", [128, 128], F32)
        hsel = sb(gstack, "hsel", [128, 2], F32)
        A2 = sb(gstack, "A2", [128, L, G, 2], F32)
        A3 = sb(gstack, "A3", [128, L, G, 2], F32)
        neglam = sb(gstack, "neglam", [128, L], F32)
        psA = gstack.enter_context(nc.psum_tensor(nm("psA"), [128, 2048], F32))
        ps = [psA[:, i * 512:(i + 1) * 512] for i in range(4)]
        ps += [gstack.enter_context(nc.psum_tensor(nm("ps"), [128, 512], F32))[:, :] for _ in range(2)]
        psB = gstack.enter_context(nc.psum_tensor(nm("psB"), [128, 1024], F32))
        ps += [psB[:, 0:512], psB[:, 512:1024]]

        K.dma("gpsimd", identb[:], cst_d[:, 0, :], [], ["identb"])
        K.dma("sync", identf[:], cst_d[:, 0, :], [], ["identf"])
        K.dma("sync", maskF[:], cst_d[:, 1, :], [], ["maskF"])
        K.dma("sync", maskB[:], cst_d[:, 2, :], [], ["maskB"])
        K.dma("sync", hsel[:], cst_d[:, 3, 0:2], [], ["hsel"])

        def ssm_pre(l):
            with ExitStack() as st:
                S = sb(st, "S", [128, 3, G], F32)
                Bq = sb(st, "Bq", [128, 2, G, 16], F32)
                Cq = sb(st, "Cq", [128, 2, G, 16], F32)
                Dc = sb(st, "Dc", [128, G], F32)
                K.dma("sync", S[:], ssms_d[:, l], [], ["S"])
                K.dma("sync", Bq[:], ssmb_d[:, l], [], ["Bq"])
                K.dma("sync", Cq[:], ssmc_d[:, l], [], ["Cq"])
                K.dma("sync", Dc[:], dcol_d[:, l], [], ["Dc"])
                sm = {}

                def small(name):
                    sm[name] = sb(st, name, [128, G], F32)
                    return sm[name]

                for n_ in ["step", "zr", "zi", "mag", "ws", "wc", "nn", "sinv", "cosv", "ar", "ai", "nr", "den",
                           "t1", "t2", "t3", "t4", "cre", "cim", "m2", "ir", "ii", "abr", "abi", "acr", "aci"]:
                    small(n_)
                V = "vector"

                def tt(o, a, b, op):
                    K.tt(V, sm[o][:], sm[a][:], sm[b][:], op, [a, b], [o])

                lamr, lami, lstep = S[:, 0, :], S[:, 1, :], S[:, 2, :]
                K.act(sm["step"][:], lstep, AF.Exp, ["S"], ["step"])
                K.tt(V, sm["zr"][:], lamr, sm["step"][:], ALU.mult, ["S", "step"], ["zr"])
                K.tt(V, sm["zi"][:], lami, sm["step"][:], ALU.mult, ["S", "step"], ["zi"])
                K.act(sm["mag"][:], sm["zr"][:], AF.Exp, ["zr"], ["mag"])
                K.ts(V, sm["ws"][:], sm["zi"][:], 1.0 / TWO_PI, None, ALU.mult, None, ["zi"], ["ws"])
                K.ts(V, sm["wc"][:], sm["zi"][:], 1.0 / TWO_PI, 0.25, ALU.mult, ALU.add, ["zi"], ["wc"])
                for wname, oname in (("ws", "sinv"), ("wc", "cosv")):
                    K.memset(V, sm["nn"][:], 0.0, ["nn"])
                    for j in range(1, 7):
                        K.stt(sm["nn"][:], sm[wname][:], j - 0.5, sm["nn"][:], ALU.is_gt, ALU.add, [wname, "nn"], ["nn"])
                    tt("t1", wname, "nn", ALU.subtract)
                    K.act(sm[oname][:], sm["t1"][:], AF.Sin, ["t1"], [oname], scale=TWO_PI)
                tt("ar", "mag", "cosv", ALU.mult)
                tt("ai", "mag", "sinv", ALU.mult)
                K.ts(V, sm["nr"][:], sm["ar"][:], -1.0, None, ALU.add, None, ["ar"], ["nr"])
                K.tt(V, sm["t1"][:], lamr, lamr, ALU.mult, ["S"], ["t1"])
                K.tt(V, sm["t2"][:], lami, lami, ALU.mult, ["S"], ["t2"])
                tt("den", "t1", "t2", ALU.add)
                K.recip(sm["den"][:], sm["den"][:], ["den"], ["den"])
                K.tt(V, sm["t1"][:], sm["nr"][:], lamr, ALU.mult, ["nr", "S"], ["t1"])
                K.tt(V, sm["t2"][:], sm["ai"][:], lami, ALU.mult, ["ai", "S"], ["t2"])
                tt("t3", "t1", "t2", ALU.add)
                tt("cre", "t3", "den", ALU.mult)
                K.tt(V, sm["t1"][:], sm["ai"][:], lamr, ALU.mult, ["ai", "S"], ["t1"])
                K.tt(V, sm["t2"][:], sm["nr"][:], lami, ALU.mult, ["nr", "S"], ["t2"])
                tt("t3", "t1", "t2", ALU.subtract)
                tt("cim", "t3", "den", ALU.mult)
                BB = sb(st, "BB", [128, 2, G, 16], F32)
                tmpA = sb(st, "tmpA", [128, G, 16], F32)
                tmpB = sb(st, "tmpB", [128, G, 16], F32)
                creb = sm["cre"][:].unsqueeze(2).broadcast_to([128, G, 16])
                cimb = sm["cim"][:].unsqueeze(2).broadcast_to([128, G, 16])
                K.tt(V, tmpA[:], Bq[:, 0], creb, ALU.mult, ["Bq", "cre"], ["tmpA"])
                K.tt(V, tmpB[:], Bq[:, 1], cimb, ALU.mult, ["Bq", "cim"], ["tmpB"])
                K.tt(V, BB[:, 0], tmpA[:], tmpB[:], ALU.subtract, ["tmpA", "tmpB"], ["BB0"])
                K.tt(V, tmpA[:], Bq[:, 1], creb, ALU.mult, ["Bq", "cre"], ["tmpA"])
                K.tt(V, tmpB[:], Bq[:, 0], cimb, ALU.mult, ["Bq", "cim"], ["tmpB"])
                K.tt(V, BB[:, 1], tmpA[:], tmpB[:], ALU.add, ["tmpA", "tmpB"], ["BB1"])
                tt("t1", "ar", "ar", ALU.mult)
                tt("t2", "ai", "ai", ALU.mult)
                tt("m2", "t1", "t2", ALU.add)
                K.recip(sm["m2"][:], sm["m2"][:], ["m2"], ["m2"])
                tt("ir", "ar", "m2", ALU.mult)
                tt("t1", "ai", "m2", ALU.mult)
                K.ts(V, sm["ii"][:], sm["t1"][:], -1.0, None, ALU.mult, None, ["t1"], ["ii"])
                for (o, f_, b_) in (("abr", "ir", "ar"), ("abi", "ii", "ai"), ("acr", "ar", "ir"), ("aci", "ai", "ii")):
                    K.cp(V, sm[o][0:64, :], sm[f_][0:64, :], [f_], [o + "f"])
                    K.cp(V, sm[o][64:128, :], sm[b_][64:128, :], [b_], [o + "b"])
                PWB = sb(st, "PWB", [128, 2, 8, G], F32)
                PWC = sb(st, "PWC", [128, 2, 8, G], F32)
                for (PW, br_, bi_, nmk) in ((PWB, "abr", "abi", "PWB"), (PWC, "acr", "aci", "PWC")):
                    K.memset(V, PW[:, 0, 0, :], 1.0, [nmk + "r0"])
                    K.memset(V, PW[:, 1, 0, :], 0.0, [nmk + "i0"])
                    for k in range(1, 8):
                        pr, pi = PW[:, 0, k - 1, :], PW[:, 1, k - 1, :]
                        K.tt(V, sm["t1"][:], pr, sm[br_][:], ALU.mult, [nmk + f"r{k-1}", br_ + "f", br_ + "b"], ["t1"])
                        K.tt(V, sm["t2"][:], pi, sm[bi_][:], ALU.mult, [nmk + f"i{k-1}", bi_ + "f", bi_ + "b"], ["t2"])
                        K.tt(V, PW[:, 0, k, :], sm["t1"][:], sm["t2"][:], ALU.subtract, ["t1", "t2"], [nmk + f"r{k}"])
                        K.tt(V, sm["t3"][:], pr, sm[bi_][:], ALU.mult, [nmk + f"r{k-1}", bi_ + "f", bi_ + "b"], ["t3"])
                        K.tt(V, sm["t4"][:], pi, sm[br_][:], ALU.mult, [nmk + f"i{k-1}", br_ + "f", br_ + "b"], ["t4"])
                        K.tt(V, PW[:, 1, k, :], sm["t3"][:], sm["t4"][:], ALU.add, ["t3", "t4"], [nmk + f"i{k}"])
                sq_r, sq_i = "ar", "ai"
                for it in range(3):
                    tt("t1", sq_r, sq_r, ALU.mult)
                    tt("t2", sq_i, sq_i, ALU.mult)
                    tt("t3", sq_r, sq_i, ALU.mult)
                    nr_, ni_ = ("zr", "zi") if it % 2 == 0 else ("ws", "wc")
                    tt(nr_, "t1", "t2", ALU.subtract)
                    K.ts(V, sm[ni_][:], sm["t3"][:], 2.0, None, ALU.mult, None, ["t3"], [ni_])
                    sq_r, sq_i = nr_, ni_
                K.cp(V, A2[:, l, :, 0], sm[sq_r][:], [sq_r], [f"A2a{l}"])
                K.cp(V, A2[:, l, :, 1], sm[sq_r][:], [sq_r], [f"A2b{l}"])
                K.ts(V, A3[:, l, :, 0], sm[sq_i][:], -1.0, None, ALU.mult, None, [sq_i], [f"A3a{l}"])
                K.cp(V, A3[:, l, :, 1], sm[sq_i][:], [sq_i], [f"A3b{l}"])

                Bt = sb(st, "Bt", [128, 2, GB, 8, 16], F32)
                Ct = sb(st, "Ct", [128, 2, GB, 8, 16], F32)
                CtF = sb(st, "CtF", [128, 2, GB, 8, 16], F32)
                CtB = sb(st, "CtB", [128, 2, GB, 8, 16], F32)
                tA = sb(st, "tA", [128, GB, 8, 16], F32)
                tB = sb(st, "tB", [128, GB, 8, 16], F32)
                tC = sb(st, "tC", [128, GB, 8, 16], F32)
                tD = sb(st, "tD", [128, GB, 8, 16], F32)
                TTst = sb(st, "TTst", [128, GB, 128], BF16)
                WAst = sb(st, "WAst", [128, GB, 2, 128], BF16)
                CTst = sb(st, "CTst", [128, GB, 2, 128], BF16)
                tm1 = sb(st, "tm1", [128, 128], F32)
                tm2 = sb(st, "tm2", [128, 128], F32)
                pwb_res = [f"PWBr{k}" for k in range(8)] + [f"PWBi{k}" for k in range(8)]
                pwc_res = [f"PWCr{k}" for k in range(8)] + [f"PWCi{k}" for k in range(8)]
                for b in range(NB):
                    g0 = b * GB

                    def pwv(PW, ri):
                        return PW[:, ri, :, g0:g0 + GB].rearrange("p k g -> p g k").unsqueeze(3).broadcast_to([128, GB, 8, 16])

                    def xv(X, ri):
                        return X[:, ri, g0:g0 + GB, :].unsqueeze(2).broadcast_to([128, GB, 8, 16])

                    K.tt(V, tA[:], pwv(PWB, 0), xv(BB, 0), ALU.mult, pwb_res + ["BB0"], ["tA"])
                    K.tt(V, tB[:], pwv(PWB, 1), xv(BB, 1), ALU.mult, pwb_res + ["BB1"], ["tB"])
                    K.tt(V, Bt[:, 0], tA[:], tB[:], ALU.subtract, ["tA", "tB"], ["Bt0"])
                    K.tt(V, tA[:], pwv(PWB, 0), xv(BB, 1), ALU.mult, pwb_res + ["BB1"], ["tA"])
                    K.tt(V, tB[:], pwv(PWB, 1), xv(BB, 0), ALU.mult, pwb_res + ["BB0"], ["tB"])
                    K.tt(V, Bt[:, 1], tA[:], tB[:], ALU.add, ["tA", "tB"], ["Bt1"])
                    GP = "gpsimd"
                    K.tt(GP, tC[:], pwv(PWC, 0), xv(Cq, 0), ALU.mult, pwc_res + ["Cq"], ["tC"])
                    K.tt(GP, tD[:], pwv(PWC, 1), xv(Cq, 1), ALU.mult, pwc_res + ["Cq"], ["tD"])
                    K.tt(GP, Ct[:, 0], tC[:], tD[:], ALU.subtract, ["tC", "tD"], ["Ct0"])
                    K.tt(GP, tC[:], pwv(PWC, 0), xv(Cq, 1), ALU.mult, pwc_res + ["Cq"], ["tC"])
                    K.tt(GP, tD[:], pwv(PWC, 1), xv(Cq, 0), ALU.mult, pwc_res + ["Cq"], ["tD"])
                    K.tt(GP, tC[:], tC[:], tD[:], ALU.add, ["tC", "tD"], ["tC"])
                    K.act(Ct[:, 1].rearrange("p g k c -> p (g k c)"), tC[:].rearrange("p g k c -> p (g k c)"), AF.Copy, ["tC"], ["Ct1"], scale=-1.0)
                    for ri in range(2):
                        K.act(CtF[:, ri].rearrange("p g k c -> p (g k c)"), Ct[:, ri].rearrange("p g k c -> p (g k c)"), AF.Copy,
                              [f"Ct{ri}", "hsel"], [f"CtF{ri}"], scale=hsel[:, 0:1])
                        K.act(CtB[:, ri].rearrange("p g k c -> p (g k c)"), Ct[:, ri].rearrange("p g k c -> p (g k c)"), AF.Copy,
                              [f"Ct{ri}", "hsel"], [f"CtB{ri}"], scale=hsel[:, 1:2])
                        K.cp("scalar", CTst[:, :, ri, :], Ct[:, ri].rearrange("p g k c -> p g (k c)"), [f"Ct{ri}"], ["CTst"])
                    for gl in range(GB):
                        g = g0 + gl
                        pf, pb, pw = ps[(2 * gl) % 4], ps[(2 * gl + 1) % 4], ps[4 + gl % 2]
                        pfn, pbn, pwn = f"ps{(2*gl)%4}", f"ps{(2*gl+1)%4}", f"ps{4+gl%2}"

                        def fl(X, ri):
                            return X[:, ri, gl].rearrange("p k c -> p (k c)")

                        K.mm(pf[:, 0:128], fl(Bt, 0), fl(CtF, 0), True, False, ["Bt0", "CtF0"], [pfn])
                        K.mm(pf[:, 0:128], fl(Bt, 1), fl(CtF, 1), False, True, ["Bt1", "CtF1"], [pfn])
                        K.mm(pb[:, 0:128], fl(Bt, 0), fl(CtB, 0), True, False, ["Bt0", "CtB0"], [pbn])
                        K.mm(pb[:, 0:128], fl(Bt, 1), fl(CtB, 1), False, True, ["Bt1", "CtB1"], [pbn])
                        K.tt(V, tm1[:], pf[:, 0:128], maskF[:], ALU.mult, [pfn, "maskF"], ["tm1"])
                        K.tt(V, tm2[:], pb[:, 0:128], maskB[:], ALU.mult, [pbn, "maskB"], ["tm2"])
                        K.tt(V, tm1[:], tm1[:], tm2[:], ALU.add, ["tm1", "tm2"], ["tm1"])
                        K.stt(TTst[:, gl, :], identf[:], Dc[:, g:g + 1], tm1[:], ALU.mult, ALU.add, ["identf", "Dc", "tm1"], ["TTst"])
                        for ri in range(2):
                            K.tr(pw[:, ri * 128:(ri + 1) * 128], fl(Bt, ri), identf[:], [f"Bt{ri}", "identf"], [pwn])
                        K.cp("scalar", WAst[:, gl, :, :], pw[:, 0:256].rearrange("p (r c) -> p r c", r=2), [pwn], ["WAst"])
                    K.dma("sync", sTT_d[l, :, g0:g0 + GB, :], TTst[:], ["TTst"], [f"sTT{l}"])
                    K.dma("sync", sWA_d[l, :, g0:g0 + GB], WAst[:], ["WAst"], [f"sWA{l}"])
                    K.dma("sync", sCT_d[l, :, g0:g0 + GB], CTst[:], ["CTst"], [f"sCT{l}"])
                if dbg:
                    K.dma("sync", ddbg(f"A2_{l}", [128, G, 2]), A2[:, l], [f"A2a{l}", f"A2b{l}"], ["dbgA2"])
                    K.dma("sync", ddbg(f"A3_{l}", [128, G, 2]), A3[:, l], [f"A3a{l}", f"A3b{l}"], ["dbgA3"])
            P.barrier()

        def lam_pre():
            with ExitStack() as st:
                lv = sb(st, "lv", [128, L, 4, 64], F32)
                pr = sb(st, "pr", [128, L, 2, 64], F32)
                sm_ = sb(st, "lsum", [128, L, 2], F32)
                K.dma("sync", lv[:], lamv_d, [], ["lv"])
                for l in range(L):
                    for j in range(2):
                        K.tt("vector", pr[:, l, j, :], lv[:, l, 2 * j, :], lv[:, l, 2 * j + 1, :], ALU.mult, ["lv"], ["pr"])
                        K.P.op("vector", lambda E, o=sm_[:, l, j:j + 1], i=pr[:, l, j, :]: E.tensor_reduce(out=o, in_=i, axis=AX.X, op=ALU.add), ["pr"], ["lsum"])
                    K.act(sm_[:, l, :], sm_[:, l, :], AF.Exp, ["lsum"], ["lsum"])
                    lam_init = 0.8 - 0.6 * math.exp(-0.3 * l)
                    K.tt("vector", neglam[:, l:l + 1], sm_[:, l, 1:2], sm_[:, l, 0:1], ALU.subtract, ["lsum"], ["neglam"])
                    K.ts("vector", neglam[:, l:l + 1], neglam[:, l:l + 1], -lam_init, None, ALU.add, None, ["neglam"], ["neglam"])
            P.barrier()

        def layer_norm_tile(xt, xn, gt, bt, stats, mv, rstd, tagx):
            for c in range(4):
                K.P.op("vector", lambda E, o=stats[:, c, :], i=xt[:, c * 512:(c + 1) * 512]: E.bn_stats(out=o, in_=i), [tagx], ["lnstats"])
            K.P.op("vector", lambda E: E.bn_aggr(out=mv[:], in_=stats[:].rearrange("p a b -> p (a b)")), ["lnstats"], ["lnmv"])
            K.ts("vector", rstd[:], mv[:, 1:2], LN_EPS, None, ALU.add, None, ["lnmv"], ["lnrstd"])
            K.act(rstd[:], rstd[:], AF.Sqrt, ["lnrstd"], ["lnrstd"])
            K.recip(rstd[:], rstd[:], ["lnrstd"], ["lnrstd"])
            K.ts("vector", xt[:], xt[:], mv[:, 0:1], rstd[:, 0:1], ALU.subtract, ALU.mult, [tagx, "lnmv", "lnrstd"], [tagx])
            K.tt("gpsimd", xt[:], xt[:], gt, ALU.mult, [tagx, "lng"], [tagx])
            K.tt("gpsimd", xt[:], xt[:], bt, ALU.add, [tagx, "lnb"], [tagx])

        def phase_ln_emb():
            with ExitStack() as st:
                gt = sb(st, "lng", [128, D], F32)
                bt = sb(st, "lnb", [128, D], F32)
                xt2 = [sb(st, "xt", [128, D], F32) for _ in range(2)]
                stats = sb(st, "stats", [128, 4, 6], F32)
                mv = sb(st, "mv", [128, 2], F32)
                rstd = sb(st, "rstd", [128, 1], F32)
                K.dma("sync", gt[:], lnp_d[:, 0, :], [], ["lng"])
                K.dma("sync", bt[:], lnp_d[:, 1, :], [], ["lnb"])
                for i in range(NT):
                    xt = xt2[i % 2]
                    tag = f"xt{i%2}"
                    K.dma("sync", xt[:], x_d[i * 128:(i + 1) * 128, :], [], [tag])
                    layer_norm_tile(xt, None, gt[:], bt[:], stats, mv, rstd, tag)
                    K.dma("sync", xres_d[i * 128:(i + 1) * 128, :], xt[:], [tag], [f"xres{i}"])
            P.barrier()

        def phase_in(l, Z):
            with ExitStack() as st:
                xT = sb(st, "xT", [128, 16, T], BF16)
                Wb = [sb(st, "Wb", [128, 16, 512], BF16) for _ in range(2)]
                xin = [sb(st, "xin", [128, D], F32) for _ in range(2)]
                xbf = [sb(st, "xbf", [128, D], BF16) for _ in range(2)]
                ropeT = sb(st, "ropeT", [128, NT, 16], F32)
                stg = [sb(st, "stg", [128, 512], F32) for _ in range(2)]
                stb = [sb(st, "stb", [128, 512], BF16) for _ in range(2)]
                stT = [sb(st, "stT", [128, 4, 128], BF16) for _ in range(2)]
                rt = [sb(st, "rt", [128, 8, 8], F32) for _ in range(4)]
                K.dma("sync", ropeT[:], rope_d.rearrange("(i p) c -> p i c", p=128), [], ["ropeT"])
                for i in range(NT):
                    xi, xb = xin[i % 2], xbf[i % 2]
                    K.dma("sync", xi[:], xres_d[i * 128:(i + 1) * 128, :], [f"xres{i}"], [f"xin{i%2}"])
                    K.cp("scalar", xb[:], xi[:], [f"xin{i%2}"], [f"xbf{i%2}"])
                    for h in range(4):
                        pt = ps[(i * 4 + h) % 4]
                        ptn = f"ps{(i*4+h)%4}"
                        for j in range(4):
                            dc = h * 4 + j
                            K.mm(pt[:, j * 128:(j + 1) * 128], xb[:, dc * 128:(dc + 1) * 128], identb[:], True, True,
                                 [f"xbf{i%2}", "identb"], [ptn])
                        eng = "vector" if h % 2 == 0 else "scalar"
                        K.cp(eng, xT[:, h * 4:(h + 1) * 4, i * 128:(i + 1) * 128],
                             pt[:, :].rearrange("p (j c) -> p j c", j=4), [ptn], [f"xT{i}"])
                for cb in range(12):
                    W = Wb[cb % 2]
                    wn = f"Wb{cb%2}"
                    src = w_in_d[l].rearrange("(dc p) n -> p dc n", p=128)
                    for hh in range(2):
                        K.dma("gpsimd", W[:, hh * 8:(hh + 1) * 8, :], src[:, hh * 8:(hh + 1) * 8, cb * 512:(cb + 1) * 512], [], [wn])
                    for i in range(NT):
                        pk = 4 + (cb * NT + i) % 2
                        pt, ptn = ps[pk], f"ps{pk}"
                        for dc in range(16):
                            K.mm(pt[:, :], xT[:, dc, i * 128:(i + 1) * 128], W[:, dc, :], dc == 0, dc == 15, [f"xT{i}", wn], [ptn])
                        j0, mh = i // 2, i % 2
                        part = cb // 2
                        c0 = (cb % 2) * 512
                        k2 = i % 2
                        if part == 0:
                            eng = "vector" if i % 2 == 0 else "scalar"
                            K.cp(eng, Z[:, mh, (cb % 2) * 32:(cb % 2) * 32 + 32, j0, :],
                                 pt[:, :].rearrange("p (g c) -> p g c", c=16), [ptn], [f"Z{cb%2}"])
                        elif part in (1, 5):
                            K.act(stg[k2][:], pt[:, :], AF.Silu, [ptn], [f"stg{k2}"])
                            dst = gs_d if part == 1 else ga_d
                            K.dma("sync", dst[i * 128:(i + 1) * 128, c0:c0 + 512], stg[k2][:], [f"stg{k2}"],
                                  [f"{'gs' if part == 1 else 'ga'}{i}"])
                        elif part == 4:
                            K.cp("scalar", stb[k2][:], pt[:, :], [ptn], [f"stb{k2}"])
                            K.dma("sync", v_d[i * 128:(i + 1) * 128, c0:c0 + 512], stb[k2][:], [f"stb{k2}"], [f"v{i}"])
                        else:
                            pv = pt[:, :].rearrange("p (a d) -> p a d", d=64)
                            ob = stb[k2][:, :].rearrange("p (a d) -> p a d", d=64)
                            cosb = ropeT[:, i, 0:8].unsqueeze(1).broadcast_to([128, 8, 8])
                            sinb = ropeT[:, i, 8:16].unsqueeze(1).broadcast_to([128, 8, 8])
                            r1, r2 = pv[:, :, 0:8], pv[:, :, 8:16]
                            V = "vector"
                            K.tt(V, rt[0][:], r1, cosb, ALU.mult, [ptn, "ropeT"], ["rt0"])
                            K.tt(V, rt[1][:], r2, sinb, ALU.mult, [ptn, "ropeT"], ["rt1"])
                            K.tt(V, ob[:, :, 0:8], rt[0][:], rt[1][:], ALU.subtract, ["rt0", "rt1"], [f"stb{k2}"])
                            K.tt(V, rt[2][:], r2, cosb, ALU.mult, [ptn, "ropeT"], ["rt2"])
                            K.tt(V, rt[3][:], r1, sinb, ALU.mult, [ptn, "ropeT"], ["rt3"])
                            K.tt(V, ob[:, :, 8:16], rt[2][:], rt[3][:], ALU.add, ["rt2", "rt3"], [f"stb{k2}"])
                            K.cp("scalar", ob[:, :, 16:64], pv[:, :, 16:64], [ptn], [f"stb{k2}"])
                            p2k = 6 + (cb * NT + i) % 2
                            p2, p2n = ps[p2k], f"ps{p2k}"
                            for hh in range(4):
                                K.mm(p2[:, hh * 128:(hh + 1) * 128], stb[k2][:, hh * 128:(hh + 1) * 128], identb[:], True, True,
                                     [f"stb{k2}", "identb"], [p2n])
                            K.cp("vector", stT[k2][:], p2[:, :].rearrange("p (h c) -> p h c", h=4), [p2n], [f"stT{k2}"])
                            dst = qT_d if part == 2 else kT_d
                            h0 = (cb % 2) * 4
                            K.dma("sync", dst[h0:h0 + 4, :, i * 128:(i + 1) * 128].rearrange("h p t -> p h t"), stT[k2][:],
                                  [f"stT{k2}"], [f"{'qT' if part == 2 else 'kT'}"])
                Ust = [stb[0], stb[1]]
                for gp in range(G // 2):
                    pk = gp % 4
                    pt, ptn = ps[pk], f"ps{pk}"
                    for q in range(4):
                        g, mh = 2 * gp + q // 2, q % 2
                        K.mm(pt[:, q * 128:(q + 1) * 128], Z[:, mh, g].rearrange("p j c -> p (j c)"), identb[:], True, True,
                             [f"Z{g//32}", "identb"], [ptn])
                    eng = "vector" if gp % 2 == 0 else "scalar"
                    K.cp(eng, Ust[gp % 2][:], pt[:, :], [ptn], [f"stb{gp%2}"])
                    K.dma("sync", Ud_d[:, 2 * gp:2 * gp + 2, :], Ust[gp % 2][:].rearrange("p (g m) -> p g m", g=2), [f"stb{gp%2}"], ["Ud"])
            P.barrier()

        def alloc_ssm(st):
            B_ = {}
            B_["U"] = [sb(st, "U", [128, GB, 256], BF16) for _ in range(2)]
            B_["TTb"] = [sb(st, "TTb", [128, GB, 128], BF16) for _ in range(2)]
            B_["WAb"] = [sb(st, "WAb", [128, GB, 2, 128], BF16) for _ in range(2)]
            B_["CTb"] = B_["WAb"]
            B_["Vb"] = [sb(st, "Vb", [128, GB, 2, 256], BF16) for _ in range(2)]
            B_["V32"] = sb(st, "V32", [128, 2 * GB, 2, 16], F32)
            B_["X"] = sb(st, "X", [128, 2 * GB, 2], F32)
            B_["t1"] = sb(st, "rt1", [128, 2 * GB, 2], F32)
            B_["t2"] = sb(st, "rt2", [128, 2 * GB, 2], F32)
            B_["S0sb"] = None
            B_["Ysb"] = sb(st, "Ysb", [128, 2, 256], F32)
            B_["Yst"] = [sb(st, "Yst", [128, 2, 8, 8, 16], F32)] * 2
            return B_

        def gen_ssm(l, B_):
            U, TTb, WAb, CTb, Vb, V32, X, t1, t2 = (B_[n_] for n_ in ["U", "TTb", "WAb", "CTb", "Vb", "V32", "X", "t1", "t2"])
            Ysb, Yst, S0sb = B_["Ysb"], B_["Yst"], B_["S0sb"]
            V = "vector"
            ares = [f"A2a{l}", f"A2b{l}", f"A3a{l}", f"A3b{l}"]
            SB = [6, 7]
            for bp in range(NB // 2):
                for k in range(2):
                    g0 = (2 * bp + k) * GB
                    K.dma("sync", TTb[k][:], sTT_d[l, :, g0:g0 + GB, :], [f"sTT{l}"], [f"TTb{k}"])
                    K.dma("sync", WAb[k][:], sWA_d[l, :, g0:g0 + GB], [f"sWA{l}"], [f"WAb{k}"])
                    K.dma("sync", U[k][:], Ud_d[:, g0:g0 + GB, :], ["Ud"], [f"U{k}"])
                K.memset(CHAIN_ENG, X[:], 0.0, ["X"])
                yield
                S0B = [(psB, ["ps6", "ps7"]), (psA[:, 1024:2048], ["ps2", "ps3"])]

                def s0_mm(mb):
                    sbuf, sres = S0B[mb % 2]
                    for k in range(2):
                        spv = sbuf[:, k * 512:(k + 1) * 512].rearrange("p (g r m) -> p g r m", g=GB, r=2)
                        for gl in range(GB):
                            for ri in range(2):
                                K.mm(spv[0:64, gl, ri, :], WAb[k][:, gl, ri, 0:64], U[k][:, gl, 16 * mb:16 * mb + 16], True, True,
                                     [f"WAb{k}", f"U{k}"], [sres[k]])
                                lo = 255 - 16 * mb
                                rhs_b = U[k][:, gl, lo - 15:lo + 1][:, ::-1]
                                K.mm(spv[64:128, gl, ri, :], WAb[k][:, gl, ri, 64:128], rhs_b, True, True, [f"WAb{k}", f"U{k}"], [sres[k]])
                            if gl % 4 == 3:
                                yield

                yield from s0_mm(0)
                for mb in range(16):
                    if mb + 1 < 16:
                        yield from s0_mm(mb + 1)
                    gA = 2 * bp * GB
                    sbuf, sres = S0B[mb % 2]
                    s0v = sbuf[:, :].rearrange("p (g r m) -> p g r m", g=2 * GB, r=2)
                    for j in range(16):
                        K.tt(V, t1[:], X[:], A2[:, l, gA:gA + 2 * GB, :], ALU.mult, ["X"] + ares, ["rt1"])
                        K.tt(V, t2[:], X[:, :, ::-1], A3[:, l, gA:gA + 2 * GB, :], ALU.mult, ["X"] + ares, ["rt2"])
                        K.tt(V, V32[:, :, :, j], t1[:], t2[:], ALU.add, ["rt1", "rt2"], ["V32"])
                        K.tt(V, X[:], V32[:, :, :, j], s0v[:, :, :, j], ALU.add, ["V32"] + sres, ["X"])
                        yield
                    for k in range(2):
                        v32 = V32[:, k * GB:(k + 1) * GB]
                        K.cp("gpsimd", Vb[k][0:64, :, :, 16 * mb:16 * mb + 16], v32[0:64], ["V32"], [f"Vb{k}"])
                        lo = 255 - 16 * mb
                        K.cp("gpsimd", Vb[k][64:128, :, :, lo - 15:lo + 1], v32[64:128][:, :, :, ::-1], ["V32"], [f"Vb{k}"])
                    yield
                for k in range(2):
                    g0 = (2 * bp + k) * GB
                    K.dma("sync", CTb[k][:], sCT_d[l, :, g0:g0 + GB], [f"sCT{l}"], [f"WAb{k}"])
                for k in range(2):
                    g0 = (2 * bp + k) * GB
                    Ub, Tb_, Cb_, Vbb = U[k], TTb[k], CTb[k], Vb[k]
                    un, tn, cn, vn = f"U{k}", f"TTb{k}", f"WAb{k}", f"Vb{k}"
                    for gp in range(GB // 2):
                        yp, ypn = ps[6], "ps6"
                        ys, ysn = Ysb, "Ysb"
                        for q in range(2):
                            gl = 2 * gp + q
                            o = yp[:, q * 256:(q + 1) * 256]
                            K.mm(o, Tb_[:, gl, :], Ub[:, gl, :], True, False, [tn, un], [ypn])
                            K.mm(o, Cb_[:, gl, 0, :], Vbb[:, gl, 0, :], False, False, [cn, vn], [ypn])
                            K.mm(o, Cb_[:, gl, 1, :], Vbb[:, gl, 1, :], False, True, [cn, vn], [ypn])
                        K.cp("scalar", ys[:].rearrange("p a b -> p (a b)"), yp[:, :], [ypn], [ysn])
                        tp, tpn = ps[7], "ps7"
                        for q in range(2):
                            for mh in range(2):
                                K.tr(tp[:, (q * 2 + mh) * 128:(q * 2 + mh + 1) * 128], ys[:, q, mh * 128:(mh + 1) * 128], identf[:],
                                     [ysn, "identf"], [tpn])
                        g8 = (g0 + 2 * gp) // 8
                        yst, ystn = Yst[0], "Yst"
                        for q in range(2):
                            gin8 = (g0 + 2 * gp + q) % 8
                            K.cp(V, yst[:, :, :, gin8, :],
                                 tp[:, q * 256:(q + 1) * 256].rearrange("p (m i c) -> p m i c", m=2, i=8), [tpn], [ystn])
                        if (g0 + 2 * gp + 1) % 8 == 7:
                            dst = ytok_d[:, g8 * 128:(g8 + 1) * 128].rearrange("(i m p) c -> p m i c", i=8, m=2)
                            for mh in range(2):
                                K.dma("sync", dst[:, mh], yst[:, mh].rearrange("p i g c -> p i (g c)"), [ystn], ["ytok"])
                        yield

        def phase_glu_out(l, last):
            LAG = 6
            with ExitStack() as st:
                mixT = sb(st, "mixT", [128, 16, T], BF16)
                Wo = sb(st, "Wo", [128, 16, D], BF16)
                Wg = sb(st, "Wg", [128, 8, 1024], BF16)
                bg = sb(st, "bg", [128, 1024], F32)
                yt = sb(st, "yt", [128, 1024], F32)
                gst = sb(st, "gst", [128, 1024], F32)
                ybf = [sb(st, "ybf", [128, 1024], BF16) for _ in range(2)]
                yT = [sb(st, "yT", [128, 8, 128], BF16)] * 2
                z = sb(st, "z", [128, 1024], F32)
                gt = sb(st, "lng", [128, D], F32)
                bt = sb(st, "lnb", [128, D], F32)
                xt2 = [sb(st, "xt", [128, D], F32) for _ in range(2)]
                stats = sb(st, "stats", [128, 4, 6], F32)
                mv = sb(st, "mv", [128, 2], F32)
                rstd = sb(st, "rstd", [128, 1], F32)
                msrc = mixT_d.rearrange("(ec p) t -> p ec t", p=128)
                K.dma("gpsimd", Wg[:], w_glu_d[l].rearrange("(kc p) n -> p kc n", p=128), [], ["Wg"])
                wsrc = w_out_d[l].rearrange("(ec p) n -> p ec n", p=128)
                for ec in range(16):
                    K.dma("gpsimd", Wo[:, ec, :], wsrc[:, ec, :], [], [f"Wo{ec}"])
                K.dma("sync", bg[:], bglu_d[:, l, :], [], ["bg"])
                yab = [sb(st, "yab", [128, 1024], BF16)] * 2
                K.dma("sync", gt[:], lnp_d[:, 2 + 2 * l, :], [], ["lng"])
                K.dma("sync", bt[:], lnp_d[:, 3 + 2 * l, :], [], ["lnb"])

                def glu_tile(i):
                    k = i % 2
                    K.dma("sync", yt[:], ytok_d[i * 128:(i + 1) * 128, :], ["ytok"], ["yt"])
                    K.dma("sync", gst[:], gs_d[i * 128:(i + 1) * 128, :], [f"gs{i}"], ["gst"])
                    K.act(yt[:], yt[:], AF.Gelu_apprx_tanh, ["yt"], ["yt"])
                    K.cp("vector", ybf[k][:], yt[:], ["yt"], [f"ybf{k}"])
                    for h in range(2):
                        pt, ptn = ps[h], f"ps{h}"
                        for j in range(4):
                            kc = h * 4 + j
                            K.mm(pt[:, j * 128:(j + 1) * 128], ybf[k][:, kc * 128:(kc + 1) * 128], identb[:], True, True,
                                 [f"ybf{k}", "identb"], [ptn])
                        K.cp("scalar" if h == 0 else "vector", yT[k][:, h * 4:(h + 1) * 4, :], pt[:, :].rearrange("p (j c) -> p j c", j=4),
                             [ptn], [f"yT{h}"])
                    for cb in range(2):
                        pz, pzn = ps[2 + cb], f"ps{2+cb}"
                        for kc in range(8):
                            K.mm(pz[:, :], yT[k][:, kc, :], Wg[:, kc, cb * 512:(cb + 1) * 512], kc == 0, kc == 7,
                                 ["yT0", "yT1", "Wg"], [pzn])
                        K.tt("vector", z[:, cb * 512:(cb + 1) * 512], pz[:, :], bg[:, cb * 512:(cb + 1) * 512], ALU.add, [pzn, "bg"], [f"z{cb}"])
                    K.act(z[:], z[:], AF.Sigmoid, ["z0", "z1"], ["z"])
                    K.tt("gpsimd", z[:], z[:], yt[:], ALU.mult, ["z", "yt"], ["z"])
                    K.tt("gpsimd", ybf[k][:], z[:], gst[:], ALU.mult, ["z", "gst"], [f"ybf{k}"])
                    for h in range(2):
                        pt, ptn = ps[4 + h], f"ps{4+h}"
                        for j in range(4):
                            kc = h * 4 + j
                            K.mm(pt[:, j * 128:(j + 1) * 128], ybf[k][:, kc * 128:(kc + 1) * 128], identb[:], True, True,
                                 [f"ybf{k}", "identb"], [ptn])
                        K.cp("scalar" if h == 0 else "vector", mixT[:, h * 4:(h + 1) * 4, i * 128:(i + 1) * 128],
                             pt[:, :].rearrange("p (j c) -> p j c", j=4), [ptn], [f"mixT{ec}" for ec in range(h * 4, h * 4 + 4)])

                def ya_tile(i):
                    k = i % 2
                    K.dma("sync", yab[k][:], ya_d[i * 128:(i + 1) * 128, :], [f"ya{i}"], ["yab"])
                    for h in range(2):
                        pt, ptn = ps[h], f"ps{h}"
                        for j in range(4):
                            kc = h * 4 + j
                            K.mm(pt[:, j * 128:(j + 1) * 128], yab[k][:, kc * 128:(kc + 1) * 128], identb[:], True, True,
                                 ["yab", "identb"], [ptn])
                        K.cp("vector" if h == 0 else "scalar", mixT[:, 8 + h * 4:8 + (h + 1) * 4, i * 128:(i + 1) * 128],
                             pt[:, :].rearrange("p (j c) -> p j c", j=4), [ptn], [f"mixT{ec}" for ec in range(8 + h * 4, 12 + h * 4)])

                def out_tile(i):
                    xt = xt2[i % 2]
                    tag = f"xt{i%2}"
                    K.dma("sync", xt[:], xres_d[i * 128:(i + 1) * 128, :], [f"xres{i}"], [tag])
                    for cb in range(4):
                        pk = 6 + cb % 2
                        pt, ptn = ps[pk], f"ps{pk}"
                        for ec in range(16):
                            K.mm(pt[:, :], mixT[:, ec, i * 128:(i + 1) * 128], Wo[:, ec, cb * 512:(cb + 1) * 512], ec == 0, ec == 15,
                                 [f"mixT{ec}", f"Wo{ec}"], [ptn])
                        K.stt(xt[:, cb * 512:(cb + 1) * 512], xt[:, cb * 512:(cb + 1) * 512], ALPHA, pt[:, :], ALU.mult, ALU.add,
                              [tag, ptn], [tag])
                    layer_norm_tile(xt, None, gt[:], bt[:], stats, mv, rstd, tag)
                    dst = out_d if last else xres_d
                    K.dma("sync", dst[i * 128:(i + 1) * 128, :], xt[:], [tag], [f"xres{i}"])

                for i in range(NT + LAG):
                    if i < NT:
                        glu_tile(i)
                        ya_tile(i)
                    if i >= LAG:
                        out_tile(i - LAG)
            P.barrier()

        def alloc_attn(st):
            B_ = {}
            B_["qT"] = [sb(st, "qT", [128, T], BF16) for _ in range(2)]
            B_["kT"] = [sb(st, "kT", [128, T], BF16) for _ in range(2)]
            B_["Va"] = [sb(st, "Va", [128, 16, 132], BF16) for _ in range(2)]
            B_["GA"] = [sb(st, "GA", [128, 4, 128], F32) for _ in range(2)]
            B_["ET"] = [[sb(st, "ET", [128, 16, 512], BF16) for _ in range(2)] for _ in range(2)]
            B_["gN"] = sb(st, "gN", [128, 128], F32)
            B_["sm"] = [{n_: sb(st, n_, [128, 1], F32) for n_ in ["r1", "r2", "c2", "ss", "rr"]} for _ in range(2)]
            B_["tA"] = [sb(st, "atA", [128, 128], F32) for _ in range(2)]
            B_["o"] = [sb(st, "ao", [128, 128], F32) for _ in range(2)]
            B_["sq"] = sb(st, "asq", [128, 128], F32)
            B_["obf"] = [sb(st, "aobf", [128, 128], BF16) for _ in range(2)]
            return B_

        def gen_attn(l, B_):
            lam_init = 0.8 - 0.6 * math.exp(-0.3 * l)
            qT, kT, Va, GA, ET, gN, smL = (B_[n_] for n_ in ["qT", "kT", "Va", "GA", "ET", "gN", "sm"])
            tAL, oL, sq, obfL = (B_[n_] for n_ in ["tA", "o", "sq", "obf"])
            K.dma("sync", gN[:], angg_d[:, l, :], [], ["gN"])
            for k in range(2):
                K.memset("vector", Va[k][:, :, 128:129], 1.0, [f"Va1{k}"])
            V = "vector"

            def load_head(h):
                k = h % 2
                K.dma("sync", qT[k][:], qT_d[h], ["qT"], [f"qT{k}"])
                K.dma("sync", kT[k][:], kT_d[h], ["kT"], [f"kT{k}"])
                K.dma("sync", Va[k][:, :, 0:128], v_d[:, h * 128:(h + 1) * 128].rearrange("(i p) c -> p i c", p=128),
                      [f"v{i}" for i in range(NT)], [f"Va{k}"])

            def load_ga(blk):
                h, qb = blk // 4, blk % 4
                K.dma("sync", GA[blk % 2][:], ga_d[qb * 512:(qb + 1) * 512, h * 128:(h + 1) * 128].rearrange("(i p) c -> p i c", p=128),
                      [f"ga{i}" for i in range(NT)], [f"GA{blk%2}"])

            def scores(blk, kc):
                h, qb = blk // 4, blk % 4
                k, eb = h % 2, blk % 2
                for mp in range(2):
                    K.mm(ps[mp][:, :], kT[k][64 * mp:64 * mp + 64, kc * 128:(kc + 1) * 128],
                         qT[k][64 * mp:64 * mp + 64, qb * 512:(qb + 1) * 512], True, True, [f"kT{k}", f"qT{k}"], [f"ps{mp}"])
                    K.act(ET[eb][mp][:, kc, :], ps[mp][:, :], AF.Exp, [f"ps{mp}"], [f"ET{eb}{mp}_{kc}"], scale=0.125)

            def pv(blk, step):
                h, qb = blk // 4, blk % 4
                k, eb = h % 2, blk % 2
                qt, mp, half = step // 4, (step % 4) // 2, step % 2
                ob, obn = ps[4 + qt % 2], f"ps{4+qt%2}"
                for kc in range(half * 8, half * 8 + 8):
                    K.mm(ob[:, mp * 129:mp * 129 + 129], ET[eb][mp][:, kc, qt * 128:(qt + 1) * 128], Va[k][:, kc, 0:129], kc == 0, kc == 15,
                         [f"ET{eb}{mp}_{kc}", f"Va{k}", f"Va1{k}"], [obn])
                if step % 4 != 3:
                    return
                e = qt % 2
                sm_, tA, o_, obf = smL[e], tAL[e], oL[e], obfL[e]
                O1, O2 = ob[:, 0:129], ob[:, 129:258]
                K.recip(sm_["r1"][:], O1[:, 128:129], [obn], [f"r1{e}"])
                K.recip(sm_["r2"][:], O2[:, 128:129], [obn], [f"r2{e}"])
                K.tt(V, sm_["c2"][:], sm_["r2"][:], neglam[:, l:l + 1], ALU.mult, [f"r2{e}", "neglam"], [f"c2{e}"])
                K.ts(V, tA[:], O2[:, 0:128], sm_["c2"][:, 0:1], None, ALU.mult, None, [obn, f"c2{e}"], [f"atA{e}"])
                K.stt(o_[:], O1[:, 0:128], sm_["r1"][:, 0:1], tA[:], ALU.mult, ALU.add, [obn, f"r1{e}", f"atA{e}"], [f"ao{e}"])
                K.P.op("vector", lambda E, o=sq[:], a=o_[:], acc=sm_["ss"][:]: E.scalar_tensor_tensor(
                    out=o, in0=a, scalar=1.0, in1=a, op0=ALU.mult, op1=ALU.mult, accum_out=acc), [f"ao{e}"], ["asq", f"ss{e}"])
                K.ts(V, sm_["rr"][:], sm_["ss"][:], 1.0 / 128.0, RMS_EPS, ALU.mult, ALU.add, [f"ss{e}"], [f"rr{e}"])
                K.act(sm_["rr"][:], sm_["rr"][:], AF.Ln, [f"rr{e}"], [f"rr{e}"])
                K.act(sm_["rr"][:], sm_["rr"][:], AF.Exp, [f"rr{e}"], [f"rr{e}"], scale=-0.5)
                K.stt(o_[:], o_[:], sm_["rr"][:, 0:1], gN[:], ALU.mult, ALU.mult, [f"ao{e}", f"rr{e}", "gN"], [f"ao{e}"])
                K.stt(obf[:], o_[:], 1.0 - lam_init, GA[blk % 2][:, qt, :], ALU.mult, ALU.mult, [f"ao{e}", f"GA{blk%2}"], [f"aobf{e}"])
                r0 = qb * 512 + qt * 128
                K.dma("sync", ya_d[r0:r0 + 128, h * 128:(h + 1) * 128], obf[:], [f"aobf{e}"], [f"ya{r0//128}"])

            load_head(0)
            for blk in range(33):
                if blk < 32 and blk % 4 == 1 and blk // 4 + 1 < 8:
                    load_head(blk // 4 + 1)
                if blk < 32:
                    load_ga(blk)
                for kc in range(16):
                    if blk < 32:
                        scores(blk, kc)
                    if blk >= 1:
                        pv(blk - 1, kc)
                    yield

        def phase_out(l, last):
            with ExitStack() as st:
                mixT = sb(st, "mixT", [128, 16, T], BF16)
                Wo = sb(st, "Wo", [128, 16, D], BF16)
                gt = sb(st, "lng", [128, D], F32)
                bt = sb(st, "lnb", [128, D], F32)
                xt2 = [sb(st, "xt", [128, D], F32) for _ in range(2)]
                stats = sb(st, "stats", [128, 4, 6], F32)
                mv = sb(st, "mv", [128, 2], F32)
                rstd = sb(st, "rstd", [128, 1], F32)
                K.dma("sync", gt[:], lnp_d[:, 2 + 2 * l, :], [], ["lng"])
                K.dma("sync", bt[:], lnp_d[:, 3 + 2 * l, :], [], ["lnb"])
                msrc = mixT_d.rearrange("(ec p) t -> p ec t", p=128)
                for ec in range(16):
                    K.dma("sync", mixT[:, ec, :], msrc[:, ec, :], ["mixT_s", "mixT_a"], [f"mixT{ec}"])
                wsrc = w_out_d[l].rearrange("(ec p) n -> p ec n", p=128)
                for ec in range(16):
                    K.dma("gpsimd", Wo[:, ec, :], wsrc[:, ec, :], [], [f"Wo{ec}"])
                for i in range(NT):
                    xt = xt2[i % 2]
                    tag = f"xt{i%2}"
                    K.dma("sync", xt[:], xres_d[i * 128:(i + 1) * 128, :], [f"xres{i}"], [tag])
                    for cb in range(4):
                        pk = (i * 4 + cb) % 8
                        pt, ptn = ps[pk], f"ps{pk}"
                        for ec in range(16):
                            K.mm(pt[:, :], mixT[:, ec, i * 128:(i + 1) * 128], Wo[:, ec, cb * 512:(cb + 1) * 512], ec == 0, ec == 15,
                                 [f"mixT{ec}", f"Wo{ec}"], [ptn])
                        K.stt(xt[:, cb * 512:(cb + 1) * 512], xt[:, cb * 512:(cb + 1) * 512], ALPHA, pt[:, :], ALU.mult, ALU.add,
                              [tag, ptn], [tag])
                    layer_norm_tile(xt, None, gt[:], bt[:], stats, mv, rstd, tag)
                    dst = out_d if last else xres_d
                    K.dma("sync", dst[i * 128:(i + 1) * 128, :], xt[:], [tag], [f"xres{i}"])
            P.barrier()

        def run(gen):
            for _ in gen:
                pass

        def count(genf):
            K.dry = True
            n = sum(1 for _ in genf())
            K.dry = False
            return n

        def co_run(ga, na, gb, nb):
            ia = ib = 0
            da = db = False
            while not (da and db):
                if not da and (db or ia * nb <= ib * na):
                    try:
                        next(ga)
                        ia += 1
                    except StopIteration:
                        da = True
                else:
                    try:
                        next(gb)
                        ib += 1
                    except StopIteration:
                        db = True

        for l in range(n_layers):
            ssm_pre(l)
        lam_pre()
        phase_ln_emb()
        for l in range(n_layers):
            with ExitStack() as lst:
                Z = sb(lst, "Z", [128, 2, G, 8, 16], BF16)
                phase_in(l, Z)
            if stop_after == ("in", l):
                break
            with ExitStack() as lst:
                BA = alloc_attn(lst)
                BS = alloc_ssm(lst)
                na = count(lambda: gen_attn(l, BA))
                ns = count(lambda: gen_ssm(l, BS))
                if COVERLAP:
                    co_run(gen_attn(l, BA), na, gen_ssm(l, BS), ns)
                else:
                    run(gen_ssm(l, BS))
                    run(gen_attn(l, BA))
                P.barrier()
            if stop_after == ("attn", l):
                break
            phase_glu_out(l, l == n_layers - 1)
        if dbg:
            for nme, (src, shp, dt) in {"xres": (xres_d, [T, D], F32), "gs": (gs_d, [T, 1024], F32), "ga": (ga_d, [T, 1024], F32),
                                        "qT": (qT_d, [8, 128, T], BF16), "kT": (kT_d, [8, 128, T], BF16), "v": (v_d, [T, 1024], BF16),
                                        "ytok": (ytok_d, [T, 1024], F32), "mixT": (mixT_d, [D, T], BF16),
                                        "sTT": (sTT_d, [L, 128, G, 128], BF16), "sWA": (sWA_d, [L, 128, G, 2, 128], BF16),
                                        "sCT": (sCT_d, [L, 128, G, 2, 128], BF16)}.items():
                dd = ddbg(nme, shp, dt)
                P.barrier()
                if len(shp) == 2:
                    for i in range(0, shp[0], 512):
                        K.dma("sync", dd[i:i + 512], src[i:i + 512], [], ["dbgout"])
                else:
                    for i in range(shp[0]):
                        K.dma("sync", dd[i], src[i], [], ["dbgout"])
        P.wait_all("sync")
        P.emit()
    return nc


def _tok_perm():
    tau = np.arange(T)
    j0, m = tau // 256, tau % 256
    return 8 * m + j0


def prep_shared(inp):
    f = np.float32
    perm = _tok_perm()
    sh = {}
    sh["w_in"] = np.ascontiguousarray(inp["w_in"], dtype=f)
    sh["w_glu"] = np.ascontiguousarray(inp["w_glu"], dtype=f)
    sh["w_out"] = np.ascontiguousarray(inp["w_out"], dtype=f)
    rows = [inp["ln_emb_g"], inp["ln_emb_b"]]
    for l in range(L):
        rows += [inp["ln_g"][l], inp["ln_b"][l]]
    lnp = np.stack(rows, 0).astype(f)
    sh["lnp"] = np.ascontiguousarray(np.broadcast_to(lnp[None], (128,) + lnp.shape))
    sh["bglu"] = np.ascontiguousarray(np.broadcast_to(np.asarray(inp["b_glu"], f)[None], (128, L, 1024)))
    sh["angg"] = np.ascontiguousarray(np.broadcast_to(np.asarray(inp["attn_norm_g"], f)[None], (128, L, 128)))
    lamv = np.stack([inp["lambda_q1"], inp["lambda_k1"], inp["lambda_q2"], inp["lambda_k2"]], 1).astype(f)
    sh["lamv"] = np.ascontiguousarray(np.broadcast_to(lamv[None], (128, L, 4, 64)))
    lr = np.asarray(inp["ssm_lam_re"], f)
    li = np.asarray(inp["ssm_lam_im"], f)
    ls = np.asarray(inp["ssm_log_step"], f)
    s = np.zeros((2, 64, L, 3, G), f)
    s[:, :, :, 0, :] = lr.transpose(1, 3, 0, 2)
    s[:, :, :, 1, :] = li.transpose(1, 3, 0, 2)
    s[:, :, :, 2, :] = np.broadcast_to(ls.transpose(1, 0, 2)[:, None], (2, 64, L, G))
    sh["ssm_s"] = np.ascontiguousarray(s.reshape(128, L, 3, G))
    br = np.asarray(inp["ssm_b_re"], f)
    bi = np.asarray(inp["ssm_b_im"], f)
    bq = np.stack([br, bi], 0)
    sh["ssm_b"] = np.ascontiguousarray(bq.transpose(2, 4, 1, 0, 3, 5).reshape(128, L, 2, G, 16))
    cr = np.asarray(inp["ssm_c_re"], f)
    ci = np.asarray(inp["ssm_c_im"], f)
    cq = np.stack([cr, ci], 0)
    sh["ssm_c"] = np.ascontiguousarray(cq.transpose(2, 5, 1, 0, 3, 4).reshape(128, L, 2, G, 16))
    d = np.asarray(inp["ssm_d"], f).reshape(L, G, 16)
    dcol = np.broadcast_to(d.transpose(2, 0, 1)[None], (8, 16, L, G)).reshape(128, L, G)
    sh["ssm_dcol"] = np.ascontiguousarray(dcol)
    pos = perm.astype(np.float32)
    inv_freq = (np.float32(500000.0) ** (-np.arange(0, 16, 2, dtype=np.float32) / np.float32(16))).astype(np.float32)
    ang = (pos[:, None] * inv_freq[None, :]).astype(np.float32)
    sh["rope"] = np.ascontiguousarray(np.concatenate([np.cos(ang), np.sin(ang)], 1).astype(f))
    cst = np.zeros((128, 4, 128), f)
    cst[:, 0, :] = np.eye(128, dtype=f)
    jj = np.arange(128) // 16
    cst[:, 1, :] = (jj[None, :] >= jj[:, None]).astype(f)
    cst[:, 2, :] = (jj[None, :] <= jj[:, None]).astype(f)
    cst[0:64, 3, 0] = 1.0
    cst[64:128, 3, 1] = 1.0
    sh["cst"] = cst
    return sh, perm


_NC_CACHE = {}


def kernel(**inputs):
    inp = {k: np.asarray(v) for k, v in inputs.items()}
    sh, perm = prep_shared(inp)
    x = np.asarray(inp["x"], np.float32)
    in_maps = []
    for c in range(8):
        m = dict(sh)
        m["x"] = np.ascontiguousarray(x[c][perm])
        in_maps.append(m)
    if "nc" not in _NC_CACHE:
        _NC_CACHE["nc"] = build()
    nc = _NC_CACHE["nc"]
    res = run_bass_kernel_spmd(nc, in_maps, core_ids=list(range(8)))
    out = np.empty((8, T, D), np.float32)
    for c in range(8):
        out[c][perm] = res.results[c]["out"]
    return out
```

```python
import math
from contextlib import ExitStack
import numpy as np
import ml_dtypes
import concourse.bass as bass
import concourse.mybir as mybir
from concourse.bass_utils import run_bass_kernel_spmd

F32 = mybir.dt.float32
BF16 = mybir.dt.bfloat16
AF = mybir.ActivationFunctionType
ALU = mybir.AluOpType
AX = mybir.AxisListType

L = 2
T = 2048
D = 2048
NT = 16
G = 64
DIN = 6144
LN_EPS = 1e-5
RMS_EPS = 1e-5
ALPHA = (2.0 * L) ** 0.25
TWO_PI = 2.0 * math.pi
GB = 16
NB = G // GB

COVERLAP = True
CHAIN_ENG = "vector"
SAME_ENGINE_SYNC = True


class Prog:
    ENGS = ["tensor", "vector", "scalar", "gpsimd", "sync"]
    NDMA = 10

    def __init__(self, nc):
        self.nc = nc
        self.q = {e: [] for e in self.ENGS}
        self.sem = {e: nc.alloc_semaphore(name=f"sem_{e}") for e in self.ENGS}
        self.cnt = {e: 0 for e in self.ENGS}
        self.dq = ["sync", "gpsimd", "scalar"]
        self.dsem = {e: [nc.alloc_semaphore(name=f"dsem_{e}_{i}") for i in range(self.NDMA)] for e in self.dq}
        self.dval = {e: [0] * self.NDMA for e in self.dq}
        self.dnext = {e: 0 for e in self.dq}
        self.semobj = {}
        for e in self.ENGS:
            self.semobj[("c", e)] = self.sem[e]
        for e in self.dq:
            for i in range(self.NDMA):
                self.semobj[("d", e, i)] = self.dsem[e][i]
        self.known = {e: {} for e in self.ENGS}
        self.res = {}
        self.ninstr = 0

    def _wait(self, eng, key, val):
        if self.known[eng].get(key, 0) >= val:
            return
        self.known[eng][key] = val
        so = self.semobj[key]
        self.q[eng].append(lambda E, so=so, val=val: E.wait_ge(so, val))

    def _deps(self, eng, reads, writes):
        deps = {}

        def add(tok):
            if tok is None:
                return
            k, v = tok
            if deps.get(k, 0) < v:
                deps[k] = v

        for r in reads:
            st = self.res.get(r)
            if st:
                add(st["w"])
        for w in writes:
            st = self.res.get(w)
            if st:
                add(st["w"])
                for t in st["r"]:
                    add(t)
        for k, v in deps.items():
            if k == ("c", eng) and (eng == "tensor" or not SAME_ENGINE_SYNC):
                continue
            self._wait(eng, k, v)

    def _record(self, tok, reads, writes):
        for r in reads:
            st = self.res.setdefault(r, {"w": None, "r": []})
            st["r"] = [t for t in st["r"] if t[0] != tok[0]] + [tok]
        for w in writes:
            self.res[w] = {"w": tok, "r": []}

    def op(self, eng, fn, reads=(), writes=()):
        self._deps(eng, reads, writes)
        self.cnt[eng] += 1
        c = self.cnt[eng]
        so = self.sem[eng]
        self.q[eng].append(lambda E, fn=fn, so=so: fn(E).then_inc(so, 1))
        self.ninstr += 1
        tok = (("c", eng), c)
        self._record(tok, reads, writes)
        return tok

    def dma(self, eng, out, in_, reads=(), writes=()):
        self._deps(eng, reads, writes)
        i = self.dnext[eng]
        self.dnext[eng] = (i + 1) % self.NDMA
        key = ("d", eng, i)
        if self.dval[eng][i] > 0:
            self._wait(eng, key, self.dval[eng][i])
        self.dval[eng][i] += 16
        v = self.dval[eng][i]
        so = self.dsem[eng][i]
        self.q[eng].append(lambda E, so=so, out=out, in_=in_: E.dma_start(out=out, in_=in_).then_inc(so, 16))
        self.ninstr += 1
        tok = (key, v)
        self._record(tok, reads, writes)
        return tok

    def wait_all(self, eng):
        for e in self.ENGS:
            if self.cnt[e] > 0 and e != eng:
                self._wait(eng, ("c", e), self.cnt[e])
        for e in self.dq:
            for i in range(self.NDMA):
                if self.dval[e][i] > 0:
                    self._wait(eng, ("d", e, i), self.dval[e][i])

    def barrier(self):
        for e in self.ENGS:
            self.wait_all(e)

    def emit(self):
        nc = self.nc
        with nc.Block() as block:
            @block.tensor
            def _(E):
                for f in self.q["tensor"]:
                    f(E)

            @block.vector
            def _(E):
                for f in self.q["vector"]:
                    f(E)

            @block.scalar
            def _(E):
                for f in self.q["scalar"]:
                    f(E)

            @block.gpsimd
            def _(E):
                for f in self.q["gpsimd"]:
                    f(E)

            @block.sync
            def _(E):
                for f in self.q["sync"]:
                    f(E)


class _DryP:
    def op(self, *a, **k):
        pass

    def dma(self, *a, **k):
        pass


class KB:
    def __init__(self, nc):
        self.nc = nc
        self._P = Prog(nc)
        self._dry = _DryP()
        self.dry = False

    @property
    def P(self):
        return self._dry if self.dry else self._P

    def mm(self, out, lhsT, rhs, start, stop, r, w):
        self.P.op("tensor", lambda E: E.matmul(out, lhsT=lhsT, rhs=rhs, start=start, stop=stop), r, w)

    def tr(self, out, in_, ident, r, w):
        self.P.op("tensor", lambda E: E.transpose(out, in_, ident), r, w)

    def act(self, out, in_, func, r, w, bias=None, scale=None, accum_out=None):
        kw = {}
        if bias is not None:
            kw["bias"] = bias
        if scale is not None:
            kw["scale"] = scale
        if accum_out is not None:
            kw["accum_out"] = accum_out
        self.P.op("scalar", lambda E: E.activation(out=out, in_=in_, func=func, **kw), r, w)

    def tt(self, eng, out, in0, in1, op, r, w):
        self.P.op(eng, lambda E: E.tensor_tensor(out=out, in0=in0, in1=in1, op=op), r, w)

    def ts(self, eng, out, in0, s1, s2, op0, op1, r, w):
        if op1 is None:
            self.P.op(eng, lambda E: E.tensor_scalar(out=out, in0=in0, scalar1=s1, scalar2=None, op0=op0), r, w)
        else:
            self.P.op(eng, lambda E: E.tensor_scalar(out=out, in0=in0, scalar1=s1, scalar2=s2, op0=op0, op1=op1), r, w)

    def stt(self, out, in0, scalar, in1, op0, op1, r, w):
        self.P.op("vector", lambda E: E.scalar_tensor_tensor(out=out, in0=in0, scalar=scalar, in1=in1, op0=op0, op1=op1), r, w)

    def cp(self, eng, out, in_, r, w):
        if eng == "scalar":
            self.P.op("scalar", lambda E: E.copy(out=out, in_=in_), r, w)
        else:
            self.P.op(eng, lambda E: E.tensor_copy(out=out, in_=in_), r, w)

    def memset(self, eng, ap, val, w):
        self.P.op(eng, lambda E: E.memset(ap, val), (), w)

    def recip(self, out, in_, r, w):
        self.P.op("vector", lambda E: E.reciprocal(out=out, in_=in_), r, w)

    def dma(self, eng, out, in_, r, w):
        self.P.dma(eng, out, in_, r, w)


def build(n_layers=L, stop_after=None, dbg=False):
    nc = bass.Bass("TRN2", target_bir_lowering=False)
    K = KB(nc)
    P = K._P

    def din(name, shape, dt=F32):
        return nc.dram_tensor(name, list(shape), dt, kind="ExternalInput").ap()

    def dscr(name, shape, dt):
        return nc.dram_tensor(name, list(shape), dt, kind="Internal").ap()

    x_d = din("x", [T, D])
    w_in_d = din("w_in", [L, D, DIN])
    w_glu_d = din("w_glu", [L, 1024, 1024])
    w_out_d = din("w_out", [L, D, D])
    lnp_d = din("lnp", [128, 2 + 2 * L, D])
    bglu_d = din("bglu", [128, L, 1024])
    angg_d = din("angg", [128, L, 128])
    lamv_d = din("lamv", [128, L, 4, 64])
    ssms_d = din("ssm_s", [128, L, 3, G])
    ssmb_d = din("ssm_b", [128, L, 2, G, 16])
    ssmc_d = din("ssm_c", [128, L, 2, G, 16])
    dcol_d = din("ssm_dcol", [128, L, G])
    rope_d = din("rope", [T, 16])
    cst_d = din("cst", [128, 4, 128])
    out_d = nc.dram_tensor("out", [T, D], F32, kind="ExternalOutput").ap()

    xres_d = dscr("xres", [T, D], F32)
    gs_d = dscr("gs", [T, 1024], F32)
    ga_d = dscr("ga", [T, 1024], F32)
    qT_d = dscr("qT", [8, 128, T], BF16)
    kT_d = dscr("kT", [8, 128, T], BF16)
    v_d = dscr("v", [T, 1024], BF16)
    ytok_d = dscr("ytok", [T, 1024], F32)
    mixT_d = dscr("mixT", [D, T], BF16)
    Ud_d = dscr("Ud", [128, G, 256], BF16)
    ya_d = dscr("ya", [T, 1024], BF16)
    sTT_d = dscr("sTT", [L, 128, G, 128], BF16)
    sWA_d = dscr("sWA", [L, 128, G, 2, 128], BF16)
    sCT_d = dscr("sCT", [L, 128, G, 2, 128], BF16)
    dbg_d = {}

    def ddbg(name, shape, dt=F32):
        dbg_d[name] = nc.dram_tensor("dbg_" + name, list(shape), dt, kind="ExternalOutput").ap()
        return dbg_d[name]

    uid = [0]

    def nm(s):
        uid[0] += 1
        return f"{s}_{uid[0]}"

    with ExitStack() as gstack:
        def sb(stack, name, shape, dt):
            return stack.enter_context(nc.sbuf_tensor(nm(name), list(shape), dt))

        identb = sb(gstack, "identb", [128, 128], BF16)
        identf = sb(gstack, "identf", [128, 128], F32)
        maskF = sb(gstack, "maskF", [128, 128], F32)
        maskB = sb(gstack, "maskB", [128, 128], F32)
        hsel = sb(gstack, "hsel", [128, 2], F32)
        A2 = sb(gstack, "A2", [128, L, G, 2], F32)
        A3 = sb(gstack, "A3", [128, L, G, 2], F32)
        neglam = sb(gstack, "neglam", [128, L], F32)
        psA = gstack.enter_context(nc.psum_tensor(nm("psA"), [128, 2048], F32))
        ps = [psA[:, i * 512:(i + 1) * 512] for i in range(4)]
        ps += [gstack.enter_context(nc.psum_tensor(nm("ps"), [128, 512], F32))[:, :] for _ in range(2)]
        psB = gstack.enter_context(nc.psum_tensor(nm("psB"), [128, 1024], F32))
        ps += [psB[:, 0:512], psB[:, 512:1024]]

        K.dma("gpsimd", identb[:], cst_d[:, 0, :], [], ["identb"])
        K.dma("sync", identf[:], cst_d[:, 0, :], [], ["identf"])
        K.dma("sync", maskF[:], cst_d[:, 1, :], [], ["maskF"])
        K.dma("sync", maskB[:], cst_d[:, 2, :], [], ["maskB"])
        K.dma("sync", hsel[:], cst_d[:, 3, 0:2], [], ["hsel"])

        def ssm_pre(l):
            with ExitStack() as st:
                S = sb(st, "S", [128, 3, G], F32)
                Bq = sb(st, "Bq", [128, 2, G, 16], F32)
                Cq = sb(st, "Cq", [128, 2, G, 16], F32)
                Dc = sb(st, "Dc", [128, G], F32)
                K.dma("sync", S[:], ssms_d[:, l], [], ["S"])
                K.dma("sync", Bq[:], ssmb_d[:, l], [], ["Bq"])
                K.dma("sync", Cq[:], ssmc_d[:, l], [], ["Cq"])
                K.dma("sync", Dc[:], dcol_d[:, l], [], ["Dc"])
                sm = {}

                def small(name):
                    sm[name] = sb(st, name, [128, G], F32)
                    return sm[name]

                for n_ in ["step", "zr", "zi", "mag", "ws", "wc", "nn", "sinv", "cosv", "ar", "ai", "nr", "den",
                           "t1", "t2", "t3", "t4", "cre", "cim", "m2", "ir", "ii", "abr", "abi", "acr", "aci"]:
                    small(n_)
                V = "vector"

                def tt(o, a, b, op):
                    K.tt(V, sm[o][:], sm[a][:], sm[b][:], op, [a, b], [o])

                lamr, lami, lstep = S[:, 0, :], S[:, 1, :], S[:, 2, :]
                K.act(sm["step"][:], lstep, AF.Exp, ["S"], ["step"])
                K.tt(V, sm["zr"][:], lamr, sm["step"][:], ALU.mult, ["S", "step"], ["zr"])
                K.tt(V, sm["zi"][:], lami, sm["step"][:], ALU.mult, ["S", "step"], ["zi"])
                K.act(sm["mag"][:], sm["zr"][:], AF.Exp, ["zr"], ["mag"])
                K.ts(V, sm["ws"][:], sm["zi"][:], 1.0 / TWO_PI, None, ALU.mult, None, ["zi"], ["ws"])
                K.ts(V, sm["wc"][:], sm["zi"][:], 1.0 / TWO_PI, 0.25, ALU.mult, ALU.add, ["zi"], ["wc"])
                for wname, oname in (("ws", "sinv"), ("wc", "cosv")):
                    K.memset(V, sm["nn"][:], 0.0, ["nn"])
                    for j in range(1, 7):
                        K.stt(sm["nn"][:], sm[wname][:], j - 0.5, sm["nn"][:], ALU.is_gt, ALU.add, [wname, "nn"], ["nn"])
                    tt("t1", wname, "nn", ALU.subtract)
                    K.act(sm[oname][:], sm["t1"][:], AF.Sin, ["t1"], [oname], scale=TWO_PI)
                tt("ar", "mag", "cosv", ALU.mult)
                tt("ai", "mag", "sinv", ALU.mult)
                K.ts(V, sm["nr"][:], sm["ar"][:], -1.0, None, ALU.add, None, ["ar"], ["nr"])
                K.tt(V, sm["t1"][:], lamr, lamr, ALU.mult, ["S"], ["t1"])
                K.tt(V, sm["t2"][:], lami, lami, ALU.mult, ["S"], ["t2"])
                tt("den", "t1", "t2", ALU.add)
                K.recip(sm["den"][:], sm["den"][:], ["den"], ["den"])
                K.tt(V, sm["t1"][:], sm["nr"][:], lamr, ALU.mult, ["nr", "S"], ["t1"])
                K.tt(V, sm["t2"][:], sm["ai"][:], lami, ALU.mult, ["ai", "S"], ["t2"])
                tt("t3", "t1", "t2", ALU.add)
                tt("cre", "t3", "den", ALU.mult)
                K.tt(V, sm["t1"][:], sm["ai"][:], lamr, ALU.mult, ["ai", "S"], ["t1"])
                K.tt(V, sm["t2"][:], sm["nr"][:], lami, ALU.mult, ["nr", "S"], ["t2"])
                tt("t3", "t1", "t2", ALU.subtract)
                tt("cim", "t3", "den", ALU.mult)
                BB = sb(st, "BB", [128, 2, G, 16], F32)
                tmpA = sb(st, "tmpA", [128, G, 16], F32)
                tmpB = sb(st, "tmpB", [128, G, 16], F32)
                creb = sm["cre"][:].unsqueeze(2).broadcast_to([128, G, 16])
                cimb = sm["cim"][:].unsqueeze(2).broadcast_to([128, G, 16])
                K.tt(V, tmpA[:], Bq[:, 0], creb, ALU.mult, ["Bq", "cre"], ["tmpA"])
                K.tt(V, tmpB[:], Bq[:, 1], cimb, ALU.mult, ["Bq", "cim"], ["tmpB"])
                K.tt(V, BB[:, 0], tmpA[:], tmpB[:], ALU.subtract, ["tmpA", "tmpB"], ["BB0"])
                K.tt(V, tmpA[:], Bq[:, 1], creb, ALU.mult, ["Bq", "cre"], ["tmpA"])
                K.tt(V, tmpB[:], Bq[:, 0], cimb, ALU.mult, ["Bq", "cim"], ["tmpB"])
                K.tt(V, BB[:, 1], tmpA[:], tmpB[:], ALU.add, ["tmpA", "tmpB"], ["BB1"])
                tt("t1", "ar", "ar", ALU.mult)
                tt("t2", "ai", "ai", ALU.mult)
                tt("m2", "t1", "t2", ALU.add)
                K.recip(sm["m2"][:], sm["m2"][:], ["m2"], ["m2"])
                tt("ir", "ar", "m2", ALU.mult)
                tt("t1", "ai", "m2", ALU.mult)
                K.ts(V, sm["ii"][:], sm["t1"][:], -1.0, None, ALU.mult, None, ["t1"], ["ii"])
                for (o, f_, b_) in (("abr", "ir", "ar"), ("abi", "ii", "ai"), ("acr", "ar", "ir"), ("aci", "ai", "ii")):
                    K.cp(V, sm[o][0:64, :], sm[f_][0:64, :], [f_], [o + "f"])
                    K.cp(V, sm[o][64:128, :], sm[b_][64:128, :], [b_], [o + "b"])
                PWB = sb(st, "PWB", [128, 2, 8, G], F32)
                PWC = sb(st, "PWC", [128, 2, 8, G], F32)
                for (PW, br_, bi_, nmk) in ((PWB, "abr", "abi", "PWB"), (PWC, "acr", "aci", "PWC")):
                    K.memset(V, PW[:, 0, 0, :], 1.0, [nmk + "r0"])
                    K.memset(V, PW[:, 1, 0, :], 0.0, [nmk + "i0"])
                    for k in range(1, 8):
                        pr, pi = PW[:, 0, k - 1, :], PW[:, 1, k - 1, :]
                        K.tt(V, sm["t1"][:], pr, sm[br_][:], ALU.mult, [nmk + f"r{k-1}", br_ + "f", br_ + "b"], ["t1"])
                        K.tt(V, sm["t2"][:], pi, sm[bi_][:], ALU.mult, [nmk + f"i{k-1}", bi_ + "f", bi_ + "b"], ["t2"])
                        K.tt(V, PW[:, 0, k, :], sm["t1"][:], sm["t2"][:], ALU.subtract, ["t1", "t2"], [nmk + f"r{k}"])
                        K.tt(V, sm["t3"][:], pr, sm[bi_][:], ALU.mult, [nmk + f"r{k-1}", bi_ + "f", bi_ + "b"], ["t3"])
                        K.tt(V, sm["t4"][:], pi, sm[br_][:], ALU.mult, [nmk + f"i{k-1}", br_ + "f", br_ + "b"], ["t4"])
                        K.tt(V, PW[:, 1, k, :], sm["t3"][:], sm["t4"][:], ALU.add, ["t3", "t4"], [nmk + f"i{k}"])
                sq_r, sq_i = "ar", "ai"
                for it in range(3):
                    tt("t1", sq_r, sq_r, ALU.mult)
                    tt("t2", sq_i, sq_i, ALU.mult)
                    tt("t3", sq_r, sq_i, ALU.mult)
                    nr_, ni_ = ("zr", "zi") if it % 2 == 0 else ("ws", "wc")
                    tt(nr_, "t1", "t2", ALU.subtract)
                    K.ts(V, sm[ni_][:], sm["t3"][:], 2.0, None, ALU.mult, None, ["t3"], [ni_])
                    sq_r, sq_i = nr_, ni_
                K.cp(V, A2[:, l, :, 0], sm[sq_r][:], [sq_r], [f"A2a{l}"])
                K.cp(V, A2[:, l, :, 1], sm[sq_r][:], [sq_r], [f"A2b{l}"])
                K.ts(V, A3[:, l, :, 0], sm[sq_i][:], -1.0, None, ALU.mult, None, [sq_i], [f"A3a{l}"])
                K.cp(V, A3[:, l, :, 1], sm[sq_i][:], [sq_i], [f"A3b{l}"])

                Bt = sb(st, "Bt", [128, 2, GB, 8, 16], F32)
                Ct = sb(st, "Ct", [128, 2, GB, 8, 16], F32)
                CtF = sb(st, "CtF", [128, 2, GB, 8, 16], F32)
                CtB = sb(st, "CtB", [128, 2, GB, 8, 16], F32)
                tA = sb(st, "tA", [128, GB, 8, 16], F32)
                tB = sb(st, "tB", [128, GB, 8, 16], F32)
                tC = sb(st, "tC", [128, GB, 8, 16], F32)
                tD = sb(st, "tD", [128, GB, 8, 16], F32)
                TTst = sb(st, "TTst", [128, GB, 128], BF16)
                WAst = sb(st, "WAst", [128, GB, 2, 128], BF16)
                CTst = sb(st, "CTst", [128, GB, 2, 128], BF16)
                tm1 = sb(st, "tm1", [128, 128], F32)
                tm2 = sb(st, "tm2", [128, 128], F32)
                pwb_res = [f"PWBr{k}" for k in range(8)] + [f"PWBi{k}" for k in range(8)]
                pwc_res = [f"PWCr{k}" for k in range(8)] + [f"PWCi{k}" for k in range(8)]
                for b in range(NB):
                    g0 = b * GB

                    def pwv(PW, ri):
                        return PW[:, ri, :, g0:g0 + GB].rearrange("p k g -> p g k").unsqueeze(3).broadcast_to([128, GB, 8, 16])

                    def xv(X, ri):
                        return X[:, ri, g0:g0 + GB, :].unsqueeze(2).broadcast_to([128, GB, 8, 16])

                    K.tt(V, tA[:], pwv(PWB, 0), xv(BB, 0), ALU.mult, pwb_res + ["BB0"], ["tA"])
                    K.tt(V, tB[:], pwv(PWB, 1), xv(BB, 1), ALU.mult, pwb_res + ["BB1"], ["tB"])
                    K.tt(V, Bt[:, 0], tA[:], tB[:], ALU.subtract, ["tA", "tB"], ["Bt0"])
                    K.tt(V, tA[:], pwv(PWB, 0), xv(BB, 1), ALU.mult, pwb_res + ["BB1"], ["tA"])
                    K.tt(V, tB[:], pwv(PWB, 1), xv(BB, 0), ALU.mult, pwb_res + ["BB0"], ["tB"])
                    K.tt(V, Bt[:, 1], tA[:], tB[:], ALU.add, ["tA", "tB"], ["Bt1"])
                    GP = "gpsimd"
                    K.tt(GP, tC[:], pwv(PWC, 0), xv(Cq, 0), ALU.mult, pwc_res + ["Cq"], ["tC"])
                    K.tt(GP, tD[:], pwv(PWC, 1), xv(Cq, 1), ALU.mult, pwc_res + ["Cq"], ["tD"])
                    K.tt(GP, Ct[:, 0], tC[:], tD[:], ALU.subtract, ["tC", "tD"], ["Ct0"])
                    K.tt(GP, tC[:], pwv(PWC, 0), xv(Cq, 1), ALU.mult, pwc_res + ["Cq"], ["tC"])
                    K.tt(GP, tD[:], pwv(PWC, 1), xv(Cq, 0), ALU.mult, pwc_res + ["Cq"], ["tD"])
                    K.tt(GP, tC[:], tC[:], tD[:], ALU.add, ["tC", "tD"], ["tC"])
                    K.act(Ct[:, 1].rearrange("p g k c -> p (g k c)"), tC[:].rearrange("p g k c -> p (g k c)"), AF.Copy, ["tC"], ["Ct1"], scale=-1.0)
                    for ri in range(2):
                        K.act(CtF[:, ri].rearrange("p g k c -> p (g k c)"), Ct[:, ri].rearrange("p g k c -> p (g k c)"), AF.Copy,
                              [f"Ct{ri}", "hsel"], [f"CtF{ri}"], scale=hsel[:, 0:1])
                        K.act(CtB[:, ri].rearrange("p g k c -> p (g k c)"), Ct[:, ri].rearrange("p g k c -> p (g k c)"), AF.Copy,
                              [f"Ct{ri}", "hsel"], [f"CtB{ri}"], scale=hsel[:, 1:2])
                        K.cp("scalar", CTst[:, :, ri, :], Ct[:, ri].rearrange("p g k c -> p g (k c)"), [f"Ct{ri}"], ["CTst"])
                    for gl in range(GB):
                        g = g0 + gl
                        pf, pb, pw = ps[(2 * gl) % 4], ps[(2 * gl + 1) % 4], ps[4 + gl % 2]
                        pfn, pbn, pwn = f"ps{(2*gl)%4}", f"ps{(2*gl+1)%4}", f"ps{4+gl%2}"

                        def fl(X, ri):
                            return X[:, ri, gl].rearrange("p k c -> p (k c)")

                        K.mm(pf[:, 0:128], fl(Bt, 0), fl(CtF, 0), True, False, ["Bt0", "CtF0"], [pfn])
                        K.mm(pf[:, 0:128], fl(Bt, 1), fl(CtF, 1), False, True, ["Bt1", "CtF1"], [pfn])
                        K.mm(pb[:, 0:128], fl(Bt, 0), fl(CtB, 0), True, False, ["Bt0", "CtB0"], [pbn])
                        K.mm(pb[:, 0:128], fl(Bt, 1), fl(CtB, 1), False, True, ["Bt1", "CtB1"], [pbn])
                        K.tt(V, tm1[:], pf[:, 0:128], maskF[:], ALU.mult, [pfn, "maskF"], ["tm1"])
                        K.tt(V, tm2[:], pb[:, 0:128], maskB[:], ALU.mult, [pbn, "maskB"], ["tm2"])
                        K.tt(V, tm1[:], tm1[:], tm2[:], ALU.add, ["tm1", "tm2"], ["tm1"])
                        K.stt(TTst[:, gl, :], identf[:], Dc[:, g:g + 1], tm1[:], ALU.mult, ALU.add, ["identf", "Dc", "tm1"], ["TTst"])
                        for ri in range(2):
                            K.tr(pw[:, ri * 128:(ri + 1) * 128], fl(Bt, ri), identf[:], [f"Bt{ri}", "identf"], [pwn])
                        K.cp("scalar", WAst[:, gl, :, :], pw[:, 0:256].rearrange("p (r c) -> p r c", r=2), [pwn], ["WAst"])
                    K.dma("sync", sTT_d[l, :, g0:g0 + GB, :], TTst[:], ["TTst"], [f"sTT{l}"])
                    K.dma("sync", sWA_d[l, :, g0:g0 + GB], WAst[:], ["WAst"], [f"sWA{l}"])
                    K.dma("sync", sCT_d[l, :, g0:g0 + GB], CTst[:], ["CTst"], [f"sCT{l}"])
                if dbg:
                    K.dma("sync", ddbg(f"A2_{l}", [128, G, 2]), A2[:, l], [f"A2a{l}", f"A2b{l}"], ["dbgA2"])
                    K.dma("sync", ddbg(f"A3_{l}", [128, G, 2]), A3[:, l], [f"A3a{l}", f"A3b{l}"], ["dbgA3"])
            P.barrier()

        def lam_pre():
            with ExitStack() as st:
                lv = sb(st, "lv", [128, L, 4, 64], F32)
                pr = sb(st, "pr", [128, L, 2, 64], F32)
                sm_ = sb(st, "lsum", [128, L, 2], F32)
                K.dma("sync", lv[:], lamv_d, [], ["lv"])
                for l in range(L):
                    for j in range(2):
                        K.tt("vector", pr[:, l, j, :], lv[:, l, 2 * j, :], lv[:, l, 2 * j + 1, :], ALU.mult, ["lv"], ["pr"])
                        K.P.op("vector", lambda E, o=sm_[:, l, j:j + 1], i=pr[:, l, j, :]: E.tensor_reduce(out=o, in_=i, axis=AX.X, op=ALU.add), ["pr"], ["lsum"])
                    K.act(sm_[:, l, :], sm_[:, l, :], AF.Exp, ["lsum"], ["lsum"])
                    lam_init = 0.8 - 0.6 * math.exp(-0.3 * l)
                    K.tt("vector", neglam[:, l:l + 1], sm_[:, l, 1:2], sm_[:, l, 0:1], ALU.subtract, ["lsum"], ["neglam"])
                    K.ts("vector", neglam[:, l:l + 1], neglam[:, l:l + 1], -lam_init, None, ALU.add, None, ["neglam"], ["neglam"])
            P.barrier()

        def layer_norm_tile(xt, xn, gt, bt, stats, mv, rstd, tagx):
            for c in range(4):
                K.P.op("vector", lambda E, o=stats[:, c, :], i=xt[:, c * 512:(c + 1) * 512]: E.bn_stats(out=o, in_=i), [tagx], ["lnstats"])
            K.P.op("vector", lambda E: E.bn_aggr(out=mv[:], in_=stats[:].rearrange("p a b -> p (a b)")), ["lnstats"], ["lnmv"])
            K.ts("vector", rstd[:], mv[:, 1:2], LN_EPS, None, ALU.add, None, ["lnmv"], ["lnrstd"])
            K.act(rstd[:], rstd[:], AF.Sqrt, ["lnrstd"], ["lnrstd"])
            K.recip(rstd[:], rstd[:], ["lnrstd"], ["lnrstd"])
            K.ts("vector", xt[:], xt[:], mv[:, 0:1], rstd[:, 0:1], ALU.subtract, ALU.mult, [tagx, "lnmv", "lnrstd"], [tagx])
            K.tt("gpsimd", xt[:], xt[:], gt, ALU.mult, [tagx, "lng"], [tagx])
            K.tt("gpsimd", xt[:], xt[:], bt, ALU.add, [tagx, "lnb"], [tagx])

        def phase_ln_emb():
            with ExitStack() as st:
                gt = sb(st, "lng", [128, D], F32)
                bt = sb(st, "lnb", [128, D], F32)
                xt2 = [sb(st, "xt", [128, D], F32) for _ in range(2)]
                stats = sb(st, "stats", [128, 4, 6], F32)
                mv = sb(st, "mv", [128, 2], F32)
                rstd = sb(st, "rstd", [128, 1], F32)
                K.dma("sync", gt[:], lnp_d[:, 0, :], [], ["lng"])
                K.dma("sync", bt[:], lnp_d[:, 1, :], [], ["lnb"])
                for i in range(NT):
                    xt = xt2[i % 2]
                    tag = f"xt{i%2}"
                    K.dma("sync", xt[:], x_d[i * 128:(i + 1) * 128, :], [], [tag])
                    layer_norm_tile(xt, None, gt[:], bt[:], stats, mv, rstd, tag)
                    K.dma("sync", xres_d[i * 128:(i + 1) * 128, :], xt[:], [tag], [f"xres{i}"])
            P.barrier()

        def phase_in(l, Z):
            with ExitStack() as st:
                xT = sb(st, "xT", [128, 16, T], BF16)
                Wb = [sb(st, "Wb", [128, 16, 512], BF16) for _ in range(2)]
                xin = [sb(st, "xin", [128, D], F32) for _ in range(2)]
                xbf = [sb(st, "xbf", [128, D], BF16) for _ in range(2)]
                ropeT = sb(st, "ropeT", [128, NT, 16], F32)
                stg = [sb(st, "stg", [128, 512], F32) for _ in range(2)]
                stb = [sb(st, "stb", [128, 512], BF16) for _ in range(2)]
                stT = [sb(st, "stT", [128, 4, 128], BF16) for _ in range(2)]
                rt = [sb(st, "rt", [128, 8, 8], F32) for _ in range(4)]
                K.dma("sync", ropeT[:], rope_d.rearrange("(i p) c -> p i c", p=128), [], ["ropeT"])
                for i in range(NT):
                    xi, xb = xin[i % 2], xbf[i % 2]
                    K.dma("sync", xi[:], xres_d[i * 128:(i + 1) * 128, :], [f"xres{i}"], [f"xin{i%2}"])
                    K.cp("scalar", xb[:], xi[:], [f"xin{i%2}"], [f"xbf{i%2}"])
                    for h in range(4):
                        pt = ps[(i * 4 + h) % 4]
                        ptn = f"ps{(i*4+h)%4}"
                        for j in range(4):
                            dc = h * 4 + j
                            K.mm(pt[:, j * 128:(j + 1) * 128], xb[:, dc * 128:(dc + 1) * 128], identb[:], True, True,
                                 [f"xbf{i%2}", "identb"], [ptn])
                        eng = "vector" if h % 2 == 0 else "scalar"
                        K.cp(eng, xT[:, h * 4:(h + 1) * 4, i * 128:(i + 1) * 128],
                             pt[:, :].rearrange("p (j c) -> p j c", j=4), [ptn], [f"xT{i}"])
                for cb in range(12):
                    W = Wb[cb % 2]
                    wn = f"Wb{cb%2}"
                    src = w_in_d[l].rearrange("(dc p) n -> p dc n", p=128)
                    for hh in range(2):
                        K.dma("gpsimd", W[:, hh * 8:(hh + 1) * 8, :], src[:, hh * 8:(hh + 1) * 8, cb * 512:(cb + 1) * 512], [], [wn])
                    for i in range(NT):
                        pk = 4 + (cb * NT + i) % 2
                        pt, ptn = ps[pk], f"ps{pk}"
                        for dc in range(16):
                            K.mm(pt[:, :], xT[:, dc, i * 128:(i + 1) * 128], W[:, dc, :], dc == 0, dc == 15, [f"xT{i}", wn], [ptn])
                        j0, mh = i // 2, i % 2
                        part = cb // 2
                        c0 = (cb % 2) * 512
                        k2 = i % 2
                        if part == 0:
                            eng = "vector" if i % 2 == 0 else "scalar"
                            K.cp(eng, Z[:, mh, (cb % 2) * 32:(cb % 2) * 32 + 32, j0, :],
                                 pt[:, :].rearrange("p (g c) -> p g c", c=16), [ptn], [f"Z{cb%2}"])
                        elif part in (1, 5):
                            K.act(stg[k2][:], pt[:, :], AF.Silu, [ptn], [f"stg{k2}"])
                            dst = gs_d if part == 1 else ga_d
                            K.dma("sync", dst[i * 128:(i + 1) * 128, c0:c0 + 512], stg[k2][:], [f"stg{k2}"],
                                  [f"{'gs' if part == 1 else 'ga'}{i}"])
                        elif part == 4:
                            K.cp("scalar", stb[k2][:], pt[:, :], [ptn], [f"stb{k2}"])
                            K.dma("sync", v_d[i * 128:(i + 1) * 128, c0:c0 + 512], stb[k2][:], [f"stb{k2}"], [f"v{i}"])
                        else:
                            pv = pt[:, :].rearrange("p (a d) -> p a d", d=64)
                            ob = stb[k2][:, :].rearrange("p (a d) -> p a d", d=64)
                            cosb = ropeT[:, i, 0:8].unsqueeze(1).broadcast_to([128, 8, 8])
                            sinb = ropeT[:, i, 8:16].unsqueeze(1).broadcast_to([128, 8, 8])
                            r1, r2 = pv[:, :, 0:8], pv[:, :, 8:16]
                            V = "vector"
                            K.tt(V, rt[0][:], r1, cosb, ALU.mult, [ptn, "ropeT"], ["rt0"])
                            K.tt(V, rt[1][:], r2, sinb, ALU.mult, [ptn, "ropeT"], ["rt1"])
                            K.tt(V, ob[:, :, 0:8], rt[0][:], rt[1][:], ALU.subtract, ["rt0", "rt1"], [f"stb{k2}"])
                            K.tt(V, rt[2][:], r2, cosb, ALU.mult, [ptn, "ropeT"], ["rt2"])
                            K.tt(V, rt[3][:], r1, sinb, ALU.mult, [ptn, "ropeT"], ["rt3"])
                            K.tt(V, ob[:, :, 8:16], rt[2][:], rt[3][:], ALU.add, ["rt2", "rt3"], [f"stb{k2}"])
                            K.cp("scalar", ob[:, :, 16:64], pv[:, :, 16:64], [ptn], [f"stb{k2}"])
                            p2k = 6 + (cb * NT + i) % 2
                            p2, p2n = ps[p2k], f"ps{p2k}"
                            for hh in range(4):
                                K.mm(p2[:, hh * 128:(hh + 1) * 128], stb[k2][:, hh * 128:(hh + 1) * 128], identb[:], True, True,
                                     [f"stb{k2}", "identb"], [p2n])
                            K.cp("vector", stT[k2][:], p2[:, :].rearrange("p (h c) -> p h c", h=4), [p2n], [f"stT{k2}"])
                            dst = qT_d if part == 2 else kT_d
                            h0 = (cb % 2) * 4
                            K.dma("sync", dst[h0:h0 + 4, :, i * 128:(i + 1) * 128].rearrange("h p t -> p h t"), stT[k2][:],
                                  [f"stT{k2}"], [f"{'qT' if part == 2 else 'kT'}"])
                Ust = [stb[0], stb[1]]
                for gp in range(G // 2):
                    pk = gp % 4
                    pt, ptn = ps[pk], f"ps{pk}"
                    for q in range(4):
                        g, mh = 2 * gp + q // 2, q % 2
                        K.mm(pt[:, q * 128:(q + 1) * 128], Z[:, mh, g].rearrange("p j c -> p (j c)"), identb[:], True, True,
                             [f"Z{g//32}", "identb"], [ptn])
                    eng = "vector" if gp % 2 == 0 else "scalar"
                    K.cp(eng, Ust[gp % 2][:], pt[:, :], [ptn], [f"stb{gp%2}"])
                    K.dma("sync", Ud_d[:, 2 * gp:2 * gp + 2, :], Ust[gp % 2][:].rearrange("p (g m) -> p g m", g=2), [f"stb{gp%2}"], ["Ud"])
            P.barrier()

        def alloc_ssm(st):
            B_ = {}
            B_["U"] = [sb(st, "U", [128, GB, 256], BF16) for _ in range(2)]
            B_["TTb"] = [sb(st, "TTb", [128, GB, 128], BF16) for _ in range(2)]
            B_["WAb"] = [sb(st, "WAb", [128, GB, 2, 128], BF16) for _ in range(2)]
            B_["CTb"] = B_["WAb"]
            B_["Vb"] = [sb(st, "Vb", [128, GB, 2, 256], BF16) for _ in range(2)]
            B_["V32"] = sb(st, "V32", [128, 2 * GB, 2, 16], F32)
            B_["X"] = sb(st, "X", [128, 2 * GB, 2], F32)
            B_["t1"] = sb(st, "rt1", [128, 2 * GB, 2], F32)
            B_["t2"] = sb(st, "rt2", [128, 2 * GB, 2], F32)
            B_["S0sb"] = None
            B_["Ysb"] = sb(st, "Ysb", [128, 2, 256], F32)
            B_["Yst"] = [sb(st, "Yst", [128, 2, 8, 8, 16], F32)] * 2
            return B_

        def gen_ssm(l, B_):
            U, TTb, WAb, CTb, Vb, V32, X, t1, t2 = (B_[n_] for n_ in ["U", "TTb", "WAb", "CTb", "Vb", "V32", "X", "t1", "t2"])
            Ysb, Yst, S0sb = B_["Ysb"], B_["Yst"], B_["S0sb"]
            V = "vector"
            ares = [f"A2a{l}", f"A2b{l}", f"A3a{l}", f"A3b{l}"]
            SB = [6, 7]
            for bp in range(NB // 2):
                for k in range(2):
                    g0 = (2 * bp + k) * GB
                    K.dma("sync", TTb[k][:], sTT_d[l, :, g0:g0 + GB, :], [f"sTT{l}"], [f"TTb{k}"])
                    K.dma("sync", WAb[k][:], sWA_d[l, :, g0:g0 + GB], [f"sWA{l}"], [f"WAb{k}"])
                    K.dma("sync", U[k][:], Ud_d[:, g0:g0 + GB, :], ["Ud"], [f"U{k}"])
                K.memset(CHAIN_ENG, X[:], 0.0, ["X"])
                yield
                S0B = [(psB, ["ps6", "ps7"]), (psA[:, 1024:2048], ["ps2", "ps3"])]

                def s0_mm(mb):
                    sbuf, sres = S0B[mb % 2]
                    for k in range(2):
                        spv = sbuf[:, k * 512:(k + 1) * 512].rearrange("p (g r m) -> p g r m", g=GB, r=2)
                        for gl in range(GB):
                            for ri in range(2):
                                K.mm(spv[0:64, gl, ri, :], WAb[k][:, gl, ri, 0:64], U[k][:, gl, 16 * mb:16 * mb + 16], True, True,
                                     [f"WAb{k}", f"U{k}"], [sres[k]])
                                lo = 255 - 16 * mb
                                rhs_b = U[k][:, gl, lo - 15:lo + 1][:, ::-1]
                                K.mm(spv[64:128, gl, ri, :], WAb[k][:, gl, ri, 64:128], rhs_b, True, True, [f"WAb{k}", f"U{k}"], [sres[k]])
                            if gl % 4 == 3:
                                yield

                yield from s0_mm(0)
                for mb in range(16):
                    if mb + 1 < 16:
                        yield from s0_mm(mb + 1)
                    gA = 2 * bp * GB
                    sbuf, sres = S0B[mb % 2]
                    s0v = sbuf[:, :].rearrange("p (g r m) -> p g r m", g=2 * GB, r=2)
                    for j in range(16):
                        K.tt(V, t1[:], X[:], A2[:, l, gA:gA + 2 * GB, :], ALU.mult, ["X"] + ares, ["rt1"])
                        K.tt(V, t2[:], X[:, :, ::-1], A3[:, l, gA:gA + 2 * GB, :], ALU.mult, ["X"] + ares, ["rt2"])
                        K.tt(V, V32[:, :, :, j], t1[:], t2[:], ALU.add, ["rt1", "rt2"], ["V32"])
                        K.tt(V, X[:], V32[:, :, :, j], s0v[:, :, :, j], ALU.add, ["V32"] + sres, ["X"])
                        yield
                    for k in range(2):
                        v32 = V32[:, k * GB:(k + 1) * GB]
                        K.cp("gpsimd", Vb[k][0:64, :, :, 16 * mb:16 * mb + 16], v32[0:64], ["V32"], [f"Vb{k}"])
                        lo = 255 - 16 * mb
                        K.cp("gpsimd", Vb[k][64:128, :, :, lo - 15:lo + 1], v32[64:128][:, :, :, ::-1], ["V32"], [f"Vb{k}"])
                    yield
                for k in range(2):
                    g0 = (2 * bp + k) * GB
                    K.dma("sync", CTb[k][:], sCT_d[l, :, g0:g0 + GB], [f"sCT{l}"], [f"WAb{k}"])
                for k in range(2):
                    g0 = (2 * bp + k) * GB
                    Ub, Tb_, Cb_, Vbb = U[k], TTb[k], CTb[k], Vb[k]
                    un, tn, cn, vn = f"U{k}", f"TTb{k}", f"WAb{k}", f"Vb{k}"
                    for gp in range(GB // 2):
                        yp, ypn = ps[6], "ps6"
                        ys, ysn = Ysb, "Ysb"
                        for q in range(2):
                            gl = 2 * gp + q
                            o = yp[:, q * 256:(q + 1) * 256]
                            K.mm(o, Tb_[:, gl, :], Ub[:, gl, :], True, False, [tn, un], [ypn])
                            K.mm(o, Cb_[:, gl, 0, :], Vbb[:, gl, 0, :], False, False, [cn, vn], [ypn])
                            K.mm(o, Cb_[:, gl, 1, :], Vbb[:, gl, 1, :], False, True, [cn, vn], [ypn])
                        K.cp("scalar", ys[:].rearrange("p a b -> p (a b)"), yp[:, :], [ypn], [ysn])
                        tp, tpn = ps[7], "ps7"
                        for q in range(2):
                            for mh in range(2):
                                K.tr(tp[:, (q * 2 + mh) * 128:(q * 2 + mh + 1) * 128], ys[:, q, mh * 128:(mh + 1) * 128], identf[:],
                                     [ysn, "identf"], [tpn])
                        g8 = (g0 + 2 * gp) // 8
                        yst, ystn = Yst[0], "Yst"
                        for q in range(2):
                            gin8 = (g0 + 2 * gp + q) % 8
                            K.cp(V, yst[:, :, :, gin8, :],
                                 tp[:, q * 256:(q + 1) * 256].rearrange("p (m i c) -> p m i c", m=2, i=8), [tpn], [ystn])
                        if (g0 + 2 * gp + 1) % 8 == 7:
                            dst = ytok_d[:, g8 * 128:(g8 + 1) * 128].rearrange("(i m p) c -> p m i c", i=8, m=2)
                            for mh in range(2):
                                K.dma("sync", dst[:, mh], yst[:, mh].rearrange("p i g c -> p i (g c)"), [ystn], ["ytok"])
                        yield

        def phase_glu_out(l, last):
            LAG = 6
            with ExitStack() as st:
                mixT = sb(st, "mixT", [128, 16, T], BF16)
                Wo = sb(st, "Wo", [128, 16, D], BF16)
                Wg = sb(st, "Wg", [128, 8, 1024], BF16)
                bg = sb(st, "bg", [128, 1024], F32)
                yt = sb(st, "yt", [128, 1024], F32)
                gst = sb(st, "gst", [128, 1024], F32)
                ybf = [sb(st, "ybf", [128, 1024], BF16) for _ in range(2)]
                yT = [sb(st, "yT", [128, 8, 128], BF16)] * 2
                z = sb(st, "z", [128, 1024], F32)
                gt = sb(st, "lng", [128, D], F32)
                bt = sb(st, "lnb", [128, D], F32)
                xt2 = [sb(st, "xt", [128, D], F32) for _ in range(2)]
                stats = sb(st, "stats", [128, 4, 6], F32)
                mv = sb(st, "mv", [128, 2], F32)
                rstd = sb(st, "rstd", [128, 1], F32)
                msrc = mixT_d.rearrange("(ec p) t -> p ec t", p=128)
                K.dma("gpsimd", Wg[:], w_glu_d[l].rearrange("(kc p) n -> p kc n", p=128), [], ["Wg"])
                wsrc = w_out_d[l].rearrange("(ec p) n -> p ec n", p=128)
                for ec in range(16):
                    K.dma("gpsimd", Wo[:, ec, :], wsrc[:, ec, :], [], [f"Wo{ec}"])
                K.dma("sync", bg[:], bglu_d[:, l, :], [], ["bg"])
                yab = [sb(st, "yab", [128, 1024], BF16)] * 2
                K.dma("sync", gt[:], lnp_d[:, 2 + 2 * l, :], [], ["lng"])
                K.dma("sync", bt[:], lnp_d[:, 3 + 2 * l, :], [], ["lnb"])

                def glu_tile(i):
                    k = i % 2
                    K.dma("sync", yt[:], ytok_d[i * 128:(i + 1) * 128, :], ["ytok"], ["yt"])
                    K.dma("sync", gst[:], gs_d[i * 128:(i + 1) * 128, :], [f"gs{i}"], ["gst"])
                    K.act(ybf[k][:], yt[:], AF.Gelu_apprx_tanh, ["yt"], [f"ybf{k}"])
                    K.act(yt[:], yt[:], AF.Gelu_apprx_tanh, ["yt"], ["yt"])
                    for h in range(2):
                        pt, ptn = ps[h], f"ps{h}"
                        for j in range(4):
                            kc = h * 4 + j
                            K.mm(pt[:, j * 128:(j + 1) * 128], ybf[k][:, kc * 128:(kc + 1) * 128], identb[:], True, True,
                                 [f"ybf{k}", "identb"], [ptn])
                        K.cp("scalar" if h == 0 else "vector", yT[k][:, h * 4:(h + 1) * 4, :], pt[:, :].rearrange("p (j c) -> p j c", j=4),
                             [ptn], [f"yT{h}"])
                    for cb in range(2):
                        pz, pzn = ps[2 + cb], f"ps{2+cb}"
                        for kc in range(8):
                            K.mm(pz[:, :], yT[k][:, kc, :], Wg[:, kc, cb * 512:(cb + 1) * 512], kc == 0, kc == 7,
                                 ["yT0", "yT1", "Wg"], [pzn])
                        K.tt("vector", z[:, cb * 512:(cb + 1) * 512], pz[:, :], bg[:, cb * 512:(cb + 1) * 512], ALU.add, [pzn, "bg"], [f"z{cb}"])
                    K.act(z[:], z[:], AF.Sigmoid, ["z0", "z1"], ["z"])
                    K.tt("vector", z[:], z[:], yt[:], ALU.mult, ["z", "yt"], ["z"])
                    K.tt("vector", ybf[k][:], z[:], gst[:], ALU.mult, ["z", "gst"], [f"ybf{k}"])
                    for h in range(2):
                        pt, ptn = ps[4 + h], f"ps{4+h}"
                        for j in range(4):
                            kc = h * 4 + j
                            K.mm(pt[:, j * 128:(j + 1) * 128], ybf[k][:, kc * 128:(kc + 1) * 128], identb[:], True, True,
                                 [f"ybf{k}", "identb"], [ptn])
                        K.cp("scalar" if h == 0 else "vector", mixT[:, h * 4:(h + 1) * 4, i * 128:(i + 1) * 128],
                             pt[:, :].rearrange("p (j c) -> p j c", j=4), [ptn], [f"mixT{ec}" for ec in range(h * 4, h * 4 + 4)])

                def ya_tile(i):
                    k = i % 2
                    K.dma("sync", yab[k][:], ya_d[i * 128:(i + 1) * 128, :], [f"ya{i}"], ["yab"])
                    for h in range(2):
                        pt, ptn = ps[h], f"ps{h}"
                        for j in range(4):
                            kc = h * 4 + j
                            K.mm(pt[:, j * 128:(j + 1) * 128], yab[k][:, kc * 128:(kc + 1) * 128], identb[:], True, True,
                                 ["yab", "identb"], [ptn])
                        K.cp("vector" if h == 0 else "scalar", mixT[:, 8 + h * 4:8 + (h + 1) * 4, i * 128:(i + 1) * 128],
                             pt[:, :].rearrange("p (j c) -> p j c", j=4), [ptn], [f"mixT{ec}" for ec in range(8 + h * 4, 12 + h * 4)])

                def out_tile(i):
                    xt = xt2[i % 2]
                    tag = f"xt{i%2}"
                    K.dma("sync", xt[:], xres_d[i * 128:(i + 1) * 128, :], [f"xres{i}"], [tag])
                    for cb in range(4):
                        pk = 6 + cb % 2
                        pt, ptn = ps[pk], f"ps{pk}"
                        for ec in range(16):
                            K.mm(pt[:, :], mixT[:, ec, i * 128:(i + 1) * 128], Wo[:, ec, cb * 512:(cb + 1) * 512], ec == 0, ec == 15,
                                 [f"mixT{ec}", f"Wo{ec}"], [ptn])
                        K.stt(xt[:, cb * 512:(cb + 1) * 512], xt[:, cb * 512:(cb + 1) * 512], ALPHA, pt[:, :], ALU.mult, ALU.add,
                              [tag, ptn], [tag])
                    layer_norm_tile(xt, None, gt[:], bt[:], stats, mv, rstd, tag)
                    dst = out_d if last else xres_d
                    K.dma("sync", dst[i * 128:(i + 1) * 128, :], xt[:], [tag], [f"xres{i}"])

                for i in range(NT + LAG):
                    if i < NT:
                        glu_tile(i)
                        ya_tile(i)
                    if i >= LAG:
                        out_tile(i - LAG)
            P.barrier()

        def alloc_attn(st):
            B_ = {}
            B_["qT"] = [sb(st, "qT", [128, T], BF16) for _ in range(2)]
            B_["kT"] = [sb(st, "kT", [128, T], BF16) for _ in range(2)]
            B_["Va"] = [sb(st, "Va", [128, 16, 132], BF16) for _ in range(2)]
            B_["GA"] = [sb(st, "GA", [128, 4, 128], F32) for _ in range(2)]
            B_["ET"] = [[sb(st, "ET", [128, 16, 512], BF16) for _ in range(2)] for _ in range(2)]
            B_["gN"] = sb(st, "gN", [128, 128], F32)
            B_["sm"] = [{n_: sb(st, n_, [128, 1], F32) for n_ in ["r1", "r2", "c2", "ss", "rr"]} for _ in range(2)]
            B_["tA"] = [sb(st, "atA", [128, 128], F32) for _ in range(2)]
            B_["o"] = [sb(st, "ao", [128, 128], F32) for _ in range(2)]
            B_["sq"] = sb(st, "asq", [128, 128], F32)
            B_["obf"] = [sb(st, "aobf", [128, 128], BF16) for _ in range(2)]
            return B_

        def gen_attn(l, B_):
            lam_init = 0.8 - 0.6 * math.exp(-0.3 * l)
            qT, kT, Va, GA, ET, gN, smL = (B_[n_] for n_ in ["qT", "kT", "Va", "GA", "ET", "gN", "sm"])
            tAL, oL, sq, obfL = (B_[n_] for n_ in ["tA", "o", "sq", "obf"])
            K.dma("sync", gN[:], angg_d[:, l, :], [], ["gN"])
            for k in range(2):
                K.memset("vector", Va[k][:, :, 128:129], 1.0, [f"Va1{k}"])
            V = "vector"

            def load_head(h):
                k = h % 2
                K.dma("sync", qT[k][:], qT_d[h], ["qT"], [f"qT{k}"])
                K.dma("sync", kT[k][:], kT_d[h], ["kT"], [f"kT{k}"])
                K.dma("sync", Va[k][:, :, 0:128], v_d[:, h * 128:(h + 1) * 128].rearrange("(i p) c -> p i c", p=128),
                      [f"v{i}" for i in range(NT)], [f"Va{k}"])

            def load_ga(blk):
                h, qb = blk // 4, blk % 4
                K.dma("sync", GA[blk % 2][:], ga_d[qb * 512:(qb + 1) * 512, h * 128:(h + 1) * 128].rearrange("(i p) c -> p i c", p=128),
                      [f"ga{i}" for i in range(NT)], [f"GA{blk%2}"])

            def scores(blk, kc):
                h, qb = blk // 4, blk % 4
                k, eb = h % 2, blk % 2
                for mp in range(2):
                    K.mm(ps[mp][:, :], kT[k][64 * mp:64 * mp + 64, kc * 128:(kc + 1) * 128],
                         qT[k][64 * mp:64 * mp + 64, qb * 512:(qb + 1) * 512], True, True, [f"kT{k}", f"qT{k}"], [f"ps{mp}"])
                    K.act(ET[eb][mp][:, kc, :], ps[mp][:, :], AF.Exp, [f"ps{mp}"], [f"ET{eb}{mp}_{kc}"], scale=0.125)

            def pv(blk, step):
                h, qb = blk // 4, blk % 4
                k, eb = h % 2, blk % 2
                qt, mp, half = step // 4, (step % 4) // 2, step % 2
                ob, obn = ps[4 + qt % 2], f"ps{4+qt%2}"
                for kc in range(half * 8, half * 8 + 8):
                    K.mm(ob[:, mp * 129:mp * 129 + 129], ET[eb][mp][:, kc, qt * 128:(qt + 1) * 128], Va[k][:, kc, 0:129], kc == 0, kc == 15,
                         [f"ET{eb}{mp}_{kc}", f"Va{k}", f"Va1{k}"], [obn])
                if step % 4 != 3:
                    return
                e = qt % 2
                sm_, tA, o_, obf = smL[e], tAL[e], oL[e], obfL[e]
                O1, O2 = ob[:, 0:129], ob[:, 129:258]
                K.recip(sm_["r1"][:], O1[:, 128:129], [obn], [f"r1{e}"])
                K.recip(sm_["r2"][:], O2[:, 128:129], [obn], [f"r2{e}"])
                K.tt(V, sm_["c2"][:], sm_["r2"][:], neglam[:, l:l + 1], ALU.mult, [f"r2{e}", "neglam"], [f"c2{e}"])
                K.ts(V, tA[:], O2[:, 0:128], sm_["c2"][:, 0:1], None, ALU.mult, None, [obn, f"c2{e}"], [f"atA{e}"])
                K.stt(o_[:], O1[:, 0:128], sm_["r1"][:, 0:1], tA[:], ALU.mult, ALU.add, [obn, f"r1{e}", f"atA{e}"], [f"ao{e}"])
                K.P.op("vector", lambda E, o=sq[:], a=o_[:], acc=sm_["ss"][:]: E.scalar_tensor_tensor(
                    out=o, in0=a, scalar=1.0, in1=a, op0=ALU.mult, op1=ALU.mult, accum_out=acc), [f"ao{e}"], ["asq", f"ss{e}"])
                K.ts(V, sm_["rr"][:], sm_["ss"][:], 1.0 / 128.0, RMS_EPS, ALU.mult, ALU.add, [f"ss{e}"], [f"rr{e}"])
                K.act(sm_["rr"][:], sm_["rr"][:], AF.Ln, [f"rr{e}"], [f"rr{e}"])
                K.act(sm_["rr"][:], sm_["rr"][:], AF.Exp, [f"rr{e}"], [f"rr{e}"], scale=-0.5)
                K.stt(o_[:], o_[:], sm_["rr"][:, 0:1], gN[:], ALU.mult, ALU.mult, [f"ao{e}", f"rr{e}", "gN"], [f"ao{e}"])
                K.stt(obf[:], o_[:], 1.0 - lam_init, GA[blk % 2][:, qt, :], ALU.mult, ALU.mult, [f"ao{e}", f"GA{blk%2}"], [f"aobf{e}"])
                r0 = qb * 512 + qt * 128
                K.dma("sync", ya_d[r0:r0 + 128, h * 128:(h + 1) * 128], obf[:], [f"aobf{e}"], [f"ya{r0//128}"])

            load_head(0)
            for blk in range(33):
                if blk < 32 and blk % 4 == 1 and blk // 4 + 1 < 8:
                    load_head(blk // 4 + 1)
                if blk < 32:
                    load_ga(blk)
                for kc in range(16):
                    if blk < 32:
                        scores(blk, kc)
                    if blk >= 1:
                        pv(blk - 1, kc)
                    yield

        def phase_out(l, last):
            with ExitStack() as st:
                mixT = sb(st, "mixT", [128, 16, T], BF16)
                Wo = sb(st, "Wo", [128, 16, D], BF16)
                gt = sb(st, "lng", [128, D], F32)
                bt = sb(st, "lnb", [128, D], F32)
                xt2 = [sb(st, "xt", [128, D], F32) for _ in range(2)]
                stats = sb(st, "stats", [128, 4, 6], F32)
                mv = sb(st, "mv", [128, 2], F32)
                rstd = sb(st, "rstd", [128, 1], F32)
                K.dma("sync", gt[:], lnp_d[:, 2 + 2 * l, :], [], ["lng"])
                K.dma("sync", bt[:], lnp_d[:, 3 + 2 * l, :], [], ["lnb"])
                msrc = mixT_d.rearrange("(ec p) t -> p ec t", p=128)
                for ec in range(16):
                    K.dma("sync", mixT[:, ec, :], msrc[:, ec, :], ["mixT_s", "mixT_a"], [f"mixT{ec}"])
                wsrc = w_out_d[l].rearrange("(ec p) n -> p ec n", p=128)
                for ec in range(16):
                    K.dma("gpsimd", Wo[:, ec, :], wsrc[:, ec, :], [], [f"Wo{ec}"])
                for i in range(NT):
                    xt = xt2[i % 2]
                    tag = f"xt{i%2}"
                    K.dma("sync", xt[:], xres_d[i * 128:(i + 1) * 128, :], [f"xres{i}"], [tag])
                    for cb in range(4):
                        pk = (i * 4 + cb) % 8
                        pt, ptn = ps[pk], f"ps{pk}"
                        for ec in range(16):
                            K.mm(pt[:, :], mixT[:, ec, i * 128:(i + 1) * 128], Wo[:, ec, cb * 512:(cb + 1) * 512], ec == 0, ec == 15,
                                 [f"mixT{ec}", f"Wo{ec}"], [ptn])
                        K.stt(xt[:, cb * 512:(cb + 1) * 512], xt[:, cb * 512:(cb + 1) * 512], ALPHA, pt[:, :], ALU.mult, ALU.add,
                              [tag, ptn], [tag])
                    layer_norm_tile(xt, None, gt[:], bt[:], stats, mv, rstd, tag)
                    dst = out_d if last else xres_d
                    K.dma("sync", dst[i * 128:(i + 1) * 128, :], xt[:], [tag], [f"xres{i}"])
            P.barrier()

        def run(gen):
            for _ in gen:
                pass

        def count(genf):
            K.dry = True
            n = sum(1 for _ in genf())
            K.dry = False
            return n

        def co_run(ga, na, gb, nb):
            ia = ib = 0
            da = db = False
            while not (da and db):
                if not da and (db or ia * nb <= ib * na):
                    try:
                        next(ga)
                        ia += 1
                    except StopIteration:
                        da = True
                else:
                    try:
                        next(gb)
                        ib += 1
                    except StopIteration:
                        db = True

        for l in range(n_layers):
            ssm_pre(l)
        lam_pre()
        phase_ln_emb()
        for l in range(n_layers):
            with ExitStack() as lst:
                Z = sb(lst, "Z", [128, 2, G, 8, 16], BF16)
                phase_in(l, Z)
            if stop_after == ("in", l):
                break
            with ExitStack() as lst:
                BA = alloc_attn(lst)
                BS = alloc_ssm(lst)
                na = count(lambda: gen_attn(l, BA))
                ns = count(lambda: gen_ssm(l, BS))
                if COVERLAP:
                    co_run(gen_attn(l, BA), na, gen_ssm(l, BS), ns)
                else:
                    run(gen_ssm(l, BS))
                    run(gen_attn(l, BA))
                P.barrier()
            if stop_after == ("attn", l):
                break
            phase_glu_out(l, l == n_layers - 1)
        if dbg:
            for nme, (src, shp, dt) in {"xres": (xres_d, [T, D], F32), "gs": (gs_d, [T, 1024], F32), "ga": (ga_d, [T, 1024], F32),
                                        "qT": (qT_d, [8, 128, T], BF16), "kT": (kT_d, [8, 128, T], BF16), "v": (v_d, [T, 1024], BF16),
                                        "ytok": (ytok_d, [T, 1024], F32), "mixT": (mixT_d, [D, T], BF16),
                                        "sTT": (sTT_d, [L, 128, G, 128], BF16), "sWA": (sWA_d, [L, 128, G, 2, 128], BF16),
                                        "sCT": (sCT_d, [L, 128, G, 2, 128], BF16)}.items():
                dd = ddbg(nme, shp, dt)
                P.barrier()
                if len(shp) == 2:
                    for i in range(0, shp[0], 512):
                        K.dma("sync", dd[i:i + 512], src[i:i + 512], [], ["dbgout"])
                else:
                    for i in range(shp[0]):
                        K.dma("sync", dd[i], src[i], [], ["dbgout"])
        P.wait_all("sync")
        P.emit()
    return nc


def _tok_perm():
    tau = np.arange(T)
    j0, m = tau // 256, tau % 256
    return 8 * m + j0


def prep_shared(inp):
    f = np.float32
    perm = _tok_perm()
    sh = {}
    sh["w_in"] = np.ascontiguousarray(inp["w_in"], dtype=f)
    sh["w_glu"] = np.ascontiguousarray(inp["w_glu"], dtype=f)
    sh["w_out"] = np.ascontiguousarray(inp["w_out"], dtype=f)
    rows = [inp["ln_emb_g"], inp["ln_emb_b"]]
    for l in range(L):
        rows += [inp["ln_g"][l], inp["ln_b"][l]]
    lnp = np.stack(rows, 0).astype(f)
    sh["lnp"] = np.ascontiguousarray(np.broadcast_to(lnp[None], (128,) + lnp.shape))
    sh["bglu"] = np.ascontiguousarray(np.broadcast_to(np.asarray(inp["b_glu"], f)[None], (128, L, 1024)))
    sh["angg"] = np.ascontiguousarray(np.broadcast_to(np.asarray(inp["attn_norm_g"], f)[None], (128, L, 128)))
    lamv = np.stack([inp["lambda_q1"], inp["lambda_k1"], inp["lambda_q2"], inp["lambda_k2"]], 1).astype(f)
    sh["lamv"] = np.ascontiguousarray(np.broadcast_to(lamv[None], (128, L, 4, 64)))
    lr = np.asarray(inp["ssm_lam_re"], f)
    li = np.asarray(inp["ssm_lam_im"], f)
    ls = np.asarray(inp["ssm_log_step"], f)
    s = np.zeros((2, 64, L, 3, G), f)
    s[:, :, :, 0, :] = lr.transpose(1, 3, 0, 2)
    s[:, :, :, 1, :] = li.transpose(1, 3, 0, 2)
    s[:, :, :, 2, :] = np.broadcast_to(ls.transpose(1, 0, 2)[:, None], (2, 64, L, G))
    sh["ssm_s"] = np.ascontiguousarray(s.reshape(128, L, 3, G))
    br = np.asarray(inp["ssm_b_re"], f)
    bi = np.asarray(inp["ssm_b_im"], f)
    bq = np.stack([br, bi], 0)
    sh["ssm_b"] = np.ascontiguousarray(bq.transpose(2, 4, 1, 0, 3, 5).reshape(128, L, 2, G, 16))
    cr = np.asarray(inp["ssm_c_re"], f)
    ci = np.asarray(inp["ssm_c_im"], f)
    cq = np.stack([cr, ci], 0)
    sh["ssm_c"] = np.ascontiguousarray(cq.transpose(2, 5, 1, 0, 3, 4).reshape(128, L, 2, G, 16))
    d = np.asarray(inp["ssm_d"], f).reshape(L, G, 16)
    dcol = np.broadcast_to(d.transpose(2, 0, 1)[None], (8, 16, L, G)).reshape(128, L, G)
    sh["ssm_dcol"] = np.ascontiguousarray(dcol)
    pos = perm.astype(np.float32)
    inv_freq = (np.float32(500000.0) ** (-np.arange(0, 16, 2, dtype=np.float32) / np.float32(16))).astype(np.float32)
    ang = (pos[:, None] * inv_freq[None, :]).astype(np.float32)
    sh["rope"] = np.ascontiguousarray(np.concatenate([np.cos(ang), np.sin(ang)], 1).astype(f))
    cst = np.zeros((128, 4, 128), f)
    cst[:, 0, :] = np.eye(128, dtype=f)
    jj = np.arange(128) // 16
    cst[:, 1, :] = (jj[None, :] >= jj[:, None]).astype(f)
    cst[:, 2, :] = (jj[None, :] <= jj[:, None]).astype(f)
    cst[0:64, 3, 0] = 1.0
    cst[64:128, 3, 1] = 1.0
    sh["cst"] = cst
    return sh, perm


_NC_CACHE = {}


def kernel(**inputs):
    inp = {k: np.asarray(v) for k, v in inputs.items()}
    sh, perm = prep_shared(inp)
    x = np.asarray(inp["x"], np.float32)
    in_maps = []
    for c in range(8):
        m = dict(sh)
        m["x"] = np.ascontiguousarray(x[c][perm])
        in_maps.append(m)
    if "nc" not in _NC_CACHE:
        _NC_CACHE["nc"] = build()
    nc = _NC_CACHE["nc"]
    res = run_bass_kernel_spmd(nc, in_maps, core_ids=list(range(8)))
    out = np.empty((8, T, D), np.float32)
    for c in range(8):
        out[c][perm] = res.results[c]["out"]
    return out
```
